# Optimizing a Trainium2 kernel written in Bass

```python
import math
import jax, jax.numpy as jnp
from jax import lax
import numpy as np

D_MODEL = 1024
BATCH = 16
SEQ = 4096
DEPTH = 2

HEAD_DIM = 64
LRU_WIDTH = D_MODEL // 2
LRU_BLOCKS = LRU_WIDTH // HEAD_DIM
SB_HEADS = D_MODEL // 256
FOX_HEADS = D_MODEL // 256
SB_WIDTH = SB_HEADS * HEAD_DIM
FOX_WIDTH = FOX_HEADS * HEAD_DIM
D_MIX = LRU_WIDTH + SB_WIDTH + FOX_WIDTH
CONV_WIDTH = 4
LRU_C = 8.0
D_FF = 2816
Q_BLOCK = 128
N_SUB = 3
EPS = 1e-6

SPLIT_SIZES = (LRU_WIDTH, LRU_WIDTH,
               SB_WIDTH, SB_WIDTH, SB_WIDTH,
               FOX_WIDTH, FOX_WIDTH, FOX_WIDTH,
               FOX_HEADS)
N_IN = sum(SPLIT_SIZES)
SPLIT_POINTS = tuple(int(v) for v in np.cumsum(SPLIT_SIZES)[:-1])

kernel_name = "hybrid_macaron_rglru_stickbreak_fox"


def _rms(x):
    xf = x.astype(jnp.float32)
    y = xf * lax.rsqrt(jnp.mean(xf * xf, axis=-1, keepdims=True) + EPS)
    return y.astype(x.dtype)


def _swiglu(h, w_up, w_down):
    gu = h @ w_up
    g, u = jnp.split(gu, 2, axis=-1)
    return (jax.nn.silu(g) * u) @ w_down


def _causal_depthwise_conv(x, w, b):
    y = lax.conv_general_dilated(
        x, w[:, None, :], window_strides=(1,), padding=[(CONV_WIDTH - 1, 0)],
        dimension_numbers=("NWC", "WIO", "NWC"), feature_group_count=x.shape[-1])
    return y + b


def _lru_combine(left, right):
    a1, b1 = left
    a2, b2 = right
    return a1 * a2, a2 * b1 + b2


def _rg_lru(u, w_r, b_r, w_i, b_i, lam):
    B, S, _ = u.shape
    ub = u.reshape(B, S, LRU_BLOCKS, HEAD_DIM)
    r = jax.nn.sigmoid(jnp.einsum("bshi,hij->bshj", ub, w_r).reshape(B, S, LRU_WIDTH) + b_r)
    i = jax.nn.sigmoid(jnp.einsum("bshi,hij->bshj", ub, w_i).reshape(B, S, LRU_WIDTH) + b_i)
    r = r.astype(jnp.float32)
    log_a = -LRU_C * r * jax.nn.softplus(-lam.astype(jnp.float32))
    a = jnp.exp(log_a)
    mult = jnp.sqrt(-jnp.expm1(2.0 * log_a))
    bt = mult * (i.astype(jnp.float32) * u.astype(jnp.float32))
    _, h = lax.associative_scan(_lru_combine, (a, bt), axis=1)
    return h.astype(u.dtype)


def _to_heads(t, n_heads):
    B, S, _ = t.shape
    return t.reshape(B, S, n_heads, HEAD_DIM).transpose(0, 2, 1, 3)


def _from_blocks(out):
    nb, B, H, Q, Dh = out.shape
    return out.transpose(1, 0, 3, 2, 4).reshape(B, nb * Q, H * Dh)


def _stick_breaking_attention(q, k, v):
    S = q.shape[2]
    scale = HEAD_DIM ** -0.5
    key_pos = jnp.arange(S)

    def block(bi):
        start = bi * Q_BLOCK
        qb = lax.dynamic_slice_in_dim(q, start, Q_BLOCK, axis=2)
        z = jnp.einsum("bhqd,bhkd->bhqk", qb, k).astype(jnp.float32) * scale
        q_pos = start + jnp.arange(Q_BLOCK)
        past = key_pos[None, :] < q_pos[:, None]
        log_beta = jax.nn.log_sigmoid(z)
        log_1mb = jnp.where(past, jax.nn.log_sigmoid(-z), 0.0)
        after = lax.cumsum(log_1mb, axis=3, reverse=True) - log_1mb
        w = jnp.where(past, jnp.exp(log_beta + after), 0.0)
        return jnp.einsum("bhqk,bhkd->bhqd", w.astype(v.dtype), v)

    return _from_blocks(lax.map(block, jnp.arange(S // Q_BLOCK)))


def _forgetting_attention(q, k, v, log_f):
    S = q.shape[2]
    scale = HEAD_DIM ** -0.5
    key_pos = jnp.arange(S)
    F = jnp.cumsum(log_f, axis=-1)

    def block(bi):
        start = bi * Q_BLOCK
        qb = lax.dynamic_slice_in_dim(q, start, Q_BLOCK, axis=2)
        Fq = lax.dynamic_slice_in_dim(F, start, Q_BLOCK, axis=2)
        logits = jnp.einsum("bhqd,bhkd->bhqk", qb, k).astype(jnp.float32) * scale
        logits = logits + Fq[..., :, None] - F[..., None, :]
        q_pos = start + jnp.arange(Q_BLOCK)
        causal = key_pos[None, :] <= q_pos[:, None]
        logits = jnp.where(causal, logits, -jnp.inf)
        p = jax.nn.softmax(logits, axis=-1)
        return jnp.einsum("bhqk,bhkd->bhqd", p.astype(v.dtype), v)

    return _from_blocks(lax.map(block, jnp.arange(S // Q_BLOCK)))


def _mixer(h, w_in, b_fgate, conv_w, conv_b, w_rgate, b_rgate, w_igate, b_igate,
           lru_lambda, g_qk, g_mix_out, w_out):
    proj = h @ w_in
    (lru_x, lru_g, sb_q, sb_k, sb_v, fx_q, fx_k, fx_v, fx_f) = jnp.split(proj, SPLIT_POINTS, axis=-1)

    u = _causal_depthwise_conv(lru_x, conv_w, conv_b)
    y_lru = _rg_lru(u, w_rgate, b_rgate, w_igate, b_igate, lru_lambda) * jax.nn.gelu(lru_g)

    y_sb = _stick_breaking_attention(_to_heads(sb_q, SB_HEADS), _to_heads(sb_k, SB_HEADS),
                                     _to_heads(sb_v, SB_HEADS))

    fq = _rms(_to_heads(fx_q, FOX_HEADS)) * g_qk[0]
    fk = _rms(_to_heads(fx_k, FOX_HEADS)) * g_qk[1]
    log_f = jax.nn.log_sigmoid(fx_f.astype(jnp.float32) + b_fgate.astype(jnp.float32))
    y_fox = _forgetting_attention(fq, fk, _to_heads(fx_v, FOX_HEADS), log_f.transpose(0, 2, 1))

    y = jnp.concatenate([_rms(y_lru), _rms(y_sb), _rms(y_fox)], axis=-1) * g_mix_out
    return y @ w_out


def setup_inputs(seed: int = 0) -> dict:
    key = jax.random.key(seed)
    ks = jax.random.split(key, 20)
    f32 = jnp.float32
    nrm = lambda k, shape, s: jax.random.normal(k, shape, f32) * s
    x = jax.random.normal(ks[0], (BATCH, SEQ, D_MODEL), f32)
    c = jax.random.normal(ks[1], (BATCH, D_MODEL), f32)
    w_ada = nrm(ks[2], (DEPTH, D_MODEL, N_SUB * 3 * D_MODEL), 0.1 * D_MODEL ** -0.5)
    b_ada = nrm(ks[3], (DEPTH, N_SUB * 3 * D_MODEL), 0.01)
    g_norm = 1.0 + nrm(ks[4], (DEPTH, N_SUB, D_MODEL), 0.05)
    w_ffn_up = nrm(ks[5], (DEPTH, 2, D_MODEL, 2 * D_FF), D_MODEL ** -0.5)
    w_ffn_down = nrm(ks[6], (DEPTH, 2, D_FF, D_MODEL), D_FF ** -0.5)
    w_in = nrm(ks[7], (DEPTH, D_MODEL, N_IN), D_MODEL ** -0.5)
    b_fgate = 3.0 + nrm(ks[8], (DEPTH, FOX_HEADS), 0.1)
    conv_w = nrm(ks[9], (DEPTH, CONV_WIDTH, LRU_WIDTH), CONV_WIDTH ** -0.5)
    conv_b = nrm(ks[10], (DEPTH, LRU_WIDTH), 0.01)
    w_rgate = nrm(ks[11], (DEPTH, LRU_BLOCKS, HEAD_DIM, HEAD_DIM), HEAD_DIM ** -0.5)
    b_rgate = nrm(ks[12], (DEPTH, LRU_WIDTH), 0.01)
    w_igate = nrm(ks[13], (DEPTH, LRU_BLOCKS, HEAD_DIM, HEAD_DIM), HEAD_DIM ** -0.5)
    b_igate = nrm(ks[14], (DEPTH, LRU_WIDTH), 0.01)
    a_c = jax.random.uniform(ks[15], (DEPTH, LRU_WIDTH), f32, 0.9, 0.999)
    s = a_c ** (1.0 / LRU_C)
    lru_lambda = jnp.log(s) - jnp.log1p(-s)
    g_qk = 1.0 + nrm(ks[16], (DEPTH, 2, HEAD_DIM), 0.05)
    g_mix_out = 1.0 + nrm(ks[17], (DEPTH, D_MIX), 0.05)
    w_out = nrm(ks[18], (DEPTH, D_MIX, D_MODEL), D_MIX ** -0.5)
    return {"x": x, "c": c, "w_ada": w_ada, "b_ada": b_ada, "g_norm": g_norm,
            "w_ffn_up": w_ffn_up, "w_ffn_down": w_ffn_down, "w_in": w_in,
            "b_fgate": b_fgate, "conv_w": conv_w, "conv_b": conv_b,
            "w_rgate": w_rgate, "b_rgate": b_rgate, "w_igate": w_igate,
            "b_igate": b_igate, "lru_lambda": lru_lambda, "g_qk": g_qk,
            "g_mix_out": g_mix_out, "w_out": w_out}


def reference(x, c, w_ada, b_ada, g_norm, w_ffn_up, w_ffn_down, w_in, b_fgate, conv_w,
              conv_b, w_rgate, b_rgate, w_igate, b_igate, lru_lambda, g_qk, g_mix_out, w_out):
    B = x.shape[0]
    c_act = jax.nn.silu(c)
    for l in range(DEPTH):
        mod = (c_act @ w_ada[l] + b_ada[l]).reshape(B, N_SUB, 3, D_MODEL)

        def norm_mod(h, j):
            shift = mod[:, j, 0][:, None, :]
            scale = mod[:, j, 1][:, None, :]
            return _rms(h) * g_norm[l, j] * (1.0 + scale) + shift

        def gate(j):
            return (1.0 + mod[:, j, 2])[:, None, :]

        x = x + 0.5 * gate(0) * _swiglu(norm_mod(x, 0), w_ffn_up[l, 0], w_ffn_down[l, 0])
        x = x + gate(1) * _mixer(norm_mod(x, 1), w_in[l], b_fgate[l], conv_w[l], conv_b[l],
                                 w_rgate[l], b_rgate[l], w_igate[l], b_igate[l],
                                 lru_lambda[l], g_qk[l], g_mix_out[l], w_out[l])
        x = x + 0.5 * gate(2) * _swiglu(norm_mod(x, 2), w_ffn_up[l, 1], w_ffn_down[l, 1])
    return x
```

```python
from contextlib import ExitStack
import os
CUT = int(os.environ.get('CUT', '99'))
CUT2 = int(os.environ.get('CUT2', '99'))
import numpy as np
import concourse.bass as bass
import concourse.mybir as mybir
from concourse.bass_utils import run_bass_kernel_spmd

F32 = mybir.dt.float32
BF16 = mybir.dt.bfloat16
AF = mybir.ActivationFunctionType
ALU = mybir.AluOpType

ENG = ("pe", "act", "dve", "pool", "sp")

D = 1024
NCH = 8
DFF = 2816
NFF = 22
NIN = 2564
EPS = 1e-6
NCORES = 8


class Buf:
    __slots__ = ("name", "w", "r", "dk", "gd")

    def __init__(self, name):
        self.name = name
        self.w = None
        self.r = {}
        self.dk = None
        self.gd = ()


class Prog:
    def __init__(self, nc):
        self.nc = nc
        self.sems = {e: nc.alloc_semaphore("s_" + e) for e in ENG}
        self.val = {e: 0 for e in ENG}
        self.seen = {e: {} for e in ENG}
        self.q = {e: [] for e in ENG}
        self.ninst = 0

    def _need(self, e, reads, writes, skip_key=None):
        need = {}
        seen = self.seen[e]

        def add(k, v, raw):
            if k == e and e == "pe":
                return
            if seen.get(k, 0) >= v:
                return
            if need.get(k, 0) < v:
                need[k] = v

        for b in reads:
            if b.w is not None:
                add(b.w[0], b.w[1], True)
        for b in writes:
            if b.w is not None:
                if b.w[0] == skip_key:
                    for k, v in b.gd:
                        add(k, v, False)
                else:
                    add(b.w[0], b.w[1], False)
            for k, v in b.r.items():
                add(k, v, False)
        for k, v in need.items():
            self.q[e].append(("w", k, v))
            seen[k] = v

    def op(self, e, fn, reads=(), writes=()):
        self._need(e, reads, writes)
        self.val[e] += 1
        v = self.val[e]
        self.q[e].append(("i", fn))
        for b in reads:
            b.r[e] = v
        for b in writes:
            b.w = (e, v)
            b.r = {}
        self.ninst += 1

    def dma_key(self, buf):
        if buf.dk is None:
            buf.dk = "d_" + buf.name
            if buf.dk not in self.sems:
                self.sems[buf.dk] = self.nc.alloc_semaphore(buf.dk)
                self.val[buf.dk] = 0
        return buf.dk

    def dma(self, qe, out, in_, own, reads=(), writes=(), **kw):
        dk = self.dma_key(own)
        self._need(qe, reads, writes, skip_key=dk)
        self.val[dk] += 16
        v = self.val[dk]
        self.q[qe].append(("d", out, in_, dk, kw))
        for b in reads:
            b.r[dk] = v
        for b in writes:
            if b.w is None or b.w[0] != dk or b.r:
                b.gd = tuple(([b.w] if b.w is not None else []) + list(b.r.items()))
            b.w = (dk, v)
            b.r = {}
        self.ninst += 1

    def flush(self):
        for e in ENG:
            for k, v in self.val.items():
                if k != e and v > 0 and self.seen[e].get(k, 0) < v:
                    self.q[e].append(("w", k, v))
                    self.seen[e][k] = v
        nc = self.nc
        sems = self.sems
        with nc.Block() as blk:
            def mk(e):
                items = self.q[e]
                own = sems[e]

                def body(eng):
                    for it in items:
                        if it[0] == "w":
                            eng.wait_ge(sems[it[1]], it[2])
                        elif it[0] == "i":
                            it[1](eng).then_inc(own, 1)
                        else:
                            eng.dma_start(out=it[1], in_=it[2], **it[4]).then_inc(sems[it[3]], 16)
                return body
            blk.tensor(mk("pe"))
            blk.scalar(mk("act"))
            blk.vector(mk("dve"))
            blk.gpsimd(mk("pool"))
            blk.sync(mk("sp"))
        self.q = {e: [] for e in ENG}


PP_L = 142


def pp_off(l, name):
    offs = {"gn": 0, "bada": 24, "cw": 96, "cb": 112, "br": 116, "bi": 120, "lam": 124, "gmix": 128, "gqk": 136, "bf": 138}
    return l * PP_L + offs[name]


def pack_pp(g_norm, b_ada, conv_w, conv_b, b_rgate, b_igate, lru_lambda, g_mix_out, g_qk, b_fgate):
    pp = np.zeros((128, 2 * PP_L), np.float32)

    def fm(v):
        return np.ascontiguousarray(v.reshape(-1, 128).T)

    for l in range(2):
        o = l * PP_L
        for j in range(3):
            pp[:, o + j * 8:o + j * 8 + 8] = fm(g_norm[l, j])
        pp[:, o + 24:o + 96] = fm(b_ada[l])
        for k in range(4):
            pp[:, o + 96 + k * 4:o + 96 + k * 4 + 4] = fm(conv_w[l, k])
        pp[:, o + 112:o + 116] = fm(conv_b[l])
        pp[:, o + 116:o + 120] = fm(b_rgate[l])
        pp[:, o + 120:o + 124] = fm(b_igate[l])
        pp[:, o + 124:o + 128] = fm(lru_lambda[l])
        pp[:, o + 128:o + 136] = fm(g_mix_out[l])
        for i in range(2):
            pp[:, o + 136 + i] = np.concatenate([g_qk[l, i], g_qk[l, i]])
        pp[:, o + 138:o + 142] = b_fgate[l][None, :]
    return pp


class MK:
    def __init__(self, nseq, S, layers=(0, 1), dbg=None, phases=("ffn", "lru", "sb", "fox", "outp")):
        self.nseq, self.S = nseq, S
        self.T = T = nseq * S
        self.dbg = dbg
        nc = self.nc = bass.Bass("TRN2", target_bir_lowering=False)
        self.P = Prog(nc)
        di = lambda n, s: nc.dram_tensor(n, s, F32, kind="ExternalInput").ap()
        self.x_d = di("x", [T, D])
        self.cT_d = di("cT", [128, NCH * nseq])
        self.pp_d = di("pp", [128, 2 * PP_L])
        self.w_ada_d = di("w_ada", [2, D, 9 * D])
        self.w_up_d = di("w_ffn_up", [2, 2, D, 2 * DFF])
        self.w_dn_d = di("w_ffn_down", [2, 2, DFF, D])
        self.w_in_d = di("w_in", [2, D, NIN])
        self.w_rg_d = di("w_rgate", [2, 8, 64, 64])
        self.w_ig_d = di("w_igate", [2, 8, 64, 64])
        self.w_out_d = di("w_out", [2, D, D])
        self.out_d = nc.dram_tensor("out", [T, D], F32, kind="ExternalOutput").ap()
        self.xT_d = nc.dram_tensor("xT_scr", [NCH, 128, T], F32, kind="Internal").ap().rearrange("c p t -> p c t")
        self.yT_d = nc.dram_tensor("yT_scr", [NCH, 128, T], BF16, kind="Internal").ap().rearrange("c p t -> p c t")
        self.hT_d = nc.dram_tensor("hT_scr", [NCH, 128, T], BF16, kind="Internal").ap().rearrange("c p t -> p c t")
        self.uid = 0
        self.consts()
        for l in layers:
            if l == layers[0]:
                with ExitStack() as esw:
                    w = self.ffn_weights(esw, l, 0) if "ffn" in phases else None
                    self.ada_phase()
                    self.ffn_phase(l, 0, first=True, last=False, w=w)
            else:
                self.ffn_phase(l, 0, first=False, last=False)
            for s in range(nseq):
                if "lru" in phases:
                    self.lru_phase(l, s)
                if "sb" in phases:
                    self.sb_phase(l, s)
                if "fox" in phases:
                    self.fox_phase(l, s)
            with ExitStack() as esw:
                w = None
                if "outp" in phases:
                    w = self.ffn_weights(esw, l, 1, load=False)
                    with ExitStack() as eso:
                        self.outp_wo(eso, l)
                        self.ffn_weights_load(w, l, 1)
                        self.outp_phase(l)
                self.ffn_phase(l, 1, first=False, last=(l == layers[-1]), w=w)

    def nm(self, s):
        self.uid += 1
        return f"{s}_{self.uid}"

    def mm(self, out, lhsT, rhs, start, stop, reads, writes):
        self.P.op("pe", lambda e: e.matmul(out, lhsT, rhs, start=start, stop=stop), reads, writes)

    def tr(self, out, in_, ident, reads, writes):
        self.P.op("pe", lambda e: e.transpose(out, in_, ident), reads, writes)

    def act(self, out, in_, func, reads, writes, **kw):
        self.P.op("act", lambda e: e.activation(out=out, in_=in_, func=func, **kw), reads, writes)

    def tt(self, eng, out, in0, in1, op, reads, writes):
        self.P.op(eng, lambda e: e.tensor_tensor(out=out, in0=in0, in1=in1, op=op), reads, writes)

    def stt(self, out, in0, scalar, in1, op0, op1, reads, writes):
        self.P.op("dve", lambda e: e.scalar_tensor_tensor(out=out, in0=in0, scalar=scalar, in1=in1, op0=op0, op1=op1),
                  reads, writes)

    def ts(self, eng, out, in0, s1, s2, op0, op1, reads, writes):
        self.P.op(eng, lambda e: e.tensor_scalar(out=out, in0=in0, scalar1=s1, scalar2=s2, op0=op0, op1=op1), reads, writes)

    def cp(self, eng, out, in_, reads, writes):
        if eng == "act":
            self.act(out, in_, AF.Identity, reads, writes)
        else:
            self.P.op(eng, lambda e: e.tensor_copy(out=out, in_=in_), reads, writes)

    def memset(self, eng, ap, val, writes):
        self.P.op(eng, lambda e: e.memset(ap, val), (), writes)

    def recip(self, out, in_, reads, writes):
        self.P.op("dve", lambda e: e.reciprocal(out=out, in_=in_), reads, writes)

    def consts(self):
        nc, P = self.nc, self.P
        al = lambda n, s, d: nc.alloc_sbuf_tensor("c_" + n, s, d)
        self.pp = al("pp", [128, 2 * PP_L], F32)
        self.b_pp = Buf("pp")
        P.dma("sp", self.pp[:], self.pp_d, self.b_pp, writes=[self.b_pp])
        self.identf = al("identf", [128, 128], F32)
        self.ones16 = al("ones16", [128, 128], BF16)
        self.onesf = al("onesf", [128, 128], F32)
        self.bd16 = al("bd16", [128, 128], BF16)
        self.negU16 = al("negU16", [128, 128], BF16)
        self.Uf = al("Uf", [128, 128], F32)
        self.selL = al("selL", [128, 128], F32)
        self.mask_lt = al("mask_lt", [128, 4, 512], BF16)
        self.mask_le = al("mask_le", [128, 4, 512], BF16)
        self.b_c = Buf("consts")
        bc = [self.b_c]

        def sel(t, pattern, cmp, base, cm):
            P.op("pool", lambda g: g.affine_select(out=t, in_=t, pattern=pattern, compare_op=cmp, fill=0.0,
                                                   base=base, channel_multiplier=cm), bc, bc)
        self.memset("pool", self.identf[:], 1.0, bc)
        sel(self.identf[:], [[-1, 128]], ALU.is_equal, 0, 1)
        self.memset("pool", self.ones16[:], 1.0, bc)
        self.memset("pool", self.onesf[:], 1.0, bc)
        self.memset("pool", self.bd16[:], 0.0, bc)
        self.memset("pool", self.bd16[0:64, 0:64], 1.0, bc)
        self.memset("pool", self.bd16[64:128, 64:128], 1.0, bc)
        self.memset("pool", self.negU16[:], -1.0, bc)
        sel(self.negU16[:], [[-1, 128]], ALU.is_ge, 0, 1)
        self.memset("pool", self.Uf[:], 1.0, bc)
        sel(self.Uf[:], [[1, 128]], ALU.is_ge, 0, -1)
        self.memset("pool", self.selL[:], 1.0, bc)
        sel(self.selL[:], [[0, 128]], ALU.is_equal, -127, 1)
        self.memset("pool", self.mask_lt[:], 1.0, bc)
        self.memset("pool", self.mask_le[:], 1.0, bc)
        for o in range(4):
            sel(self.mask_lt[:, o, :], [[1, 512]], ALU.is_gt, -o * 128, -1)
            sel(self.mask_le[:, o, :], [[1, 512]], ALU.is_ge, -o * 128, -1)
        ns = self.nseq
        self.modT = al("modT", [128, 2, ns, 72], F32)
        self.Atab = al("Atab", [128, 2, ns, 24], F32)
        self.Gtab = al("Gtab", [128, 2, ns, 24], F32)
        self.cneg = al("cneg", [128, 2, 2, 4], F32)
        self.gq8 = al("gq8", [128, 2, 2], F32)
        self.b_mod = Buf("mod")

    def ppc(self, l, name, i=0, n=1):
        o = pp_off(l, name) + i
        return self.pp[:, o:o + n]

    def ada_phase(self):
        nc, P, ns = self.nc, self.P, self.nseq
        with ExitStack() as es:
            sb = lambda n, s, d: es.enter_context(nc.sbuf_tensor(self.nm(n), s, d))
            cT = sb("cT", [128, NCH * ns], F32)
            b_cT = Buf("cT")
            wb = [sb("wada", [128, NCH, 512], F32) for _ in range(2)]
            b_wb = [Buf("wada0"), Buf("wada1")]
            ps = es.enter_context(nc.psum_tensor(self.nm("adaps"), [128, 512], F32))
            b_ps = Buf("adaps")
            tmp = sb("adatmp", [128, 4], F32)
            P.dma("sp", cT[:], self.cT_d, b_cT, writes=[b_cT])
            self.act(cT[:], cT[:], AF.Silu, [b_cT], [b_cT])
            it = 0
            for l in range(2):
                for g in range(18):
                    w, bw = wb[it % 2], b_wb[it % 2]
                    it += 1
                    for k in range(NCH):
                        P.dma("sp" if k % 2 == 0 else "act", w[:, k, :],
                              self.w_ada_d[l, k * 128:(k + 1) * 128, g * 512:(g + 1) * 512], bw, writes=[bw])
                    for mm_ in range(4):
                        m = g * 4 + mm_
                        for k in range(NCH):
                            self.mm(ps[:, m * ns:(m + 1) * ns], w[:, k, mm_ * 128:(mm_ + 1) * 128],
                                    cT[:, k * ns:(k + 1) * ns], k == 0, k == NCH - 1, [bw, b_cT], [b_ps])
                bm = [self.b_mod]
                for s in range(ns):
                    self.tt("dve", self.modT[:, l, s, :], ps[:, s:72 * ns:ns], self.ppc(l, "bada", 0, 72), ALU.add,
                            [b_ps, self.b_pp], bm)
                    for j in range(3):
                        self.stt(self.Atab[:, l, s, j * 8:(j + 1) * 8], self.modT[:, l, s, j * 24 + 8:j * 24 + 16], 1.0,
                                 self.ppc(l, "gn", j * 8, 8), ALU.add, ALU.mult, bm + [self.b_pp], bm)
                        self.ts("dve", self.Gtab[:, l, s, j * 8:(j + 1) * 8], self.modT[:, l, s, j * 24 + 16:j * 24 + 24],
                                1.0, 1.0 if j == 1 else 0.5, ALU.add, ALU.mult, bm, bm)
                self.act(tmp[:], self.ppc(l, "lam", 0, 4), AF.Exp, [self.b_pp], bm, scale=-1.0)
                self.act(tmp[:], tmp[:], AF.Ln, bm, bm, bias=1.0)
                self.ts("dve", self.cneg[:, l, 0, :], tmp[:], -8.0, None, ALU.mult, ALU.bypass, bm, bm)
                self.ts("dve", self.cneg[:, l, 1, :], tmp[:], -16.0, None, ALU.mult, ALU.bypass, bm, bm)
                self.ts("dve", self.gq8[:, l, 0:1], self.ppc(l, "gqk", 0, 1), 0.125, None, ALU.mult, ALU.bypass,
                        [self.b_pp], bm)
                self.cp("dve", self.gq8[:, l, 1:2], self.ppc(l, "gqk", 1, 1), [self.b_pp], bm)
            if self.dbg == "ada":
                self.dbg_out("dbg_mod", self.modT[:].rearrange("p a b c -> p (a b c)"), [128, 2 * ns * 72], [self.b_mod])
            P.flush()

    def dbg_out(self, name, ap, shape, reads, dt=F32):
        d = self.nc.dram_tensor(name, shape, dt, kind="ExternalOutput").ap()
        b = Buf(self.nm("dbg"))
        self.P.dma("sp", d, ap, b, reads=reads)

    def norm_a(self, xt, b_xt, sq, b_sq):
        self.act(sq[:], xt[:], AF.Square, b_xt, [b_sq])

    def norm_b(self, xt, b_xt, h, b_h, sq, b_sq, ps, b_psb, rs, b_rs, tmp, b_tmp, l, j, s, NT):
        for c in range(NCH):
            self.mm(ps[:, 0:NT], self.ones16[:], sq[:, c, :], c == 0, c == NCH - 1, [b_sq, self.b_c], [b_psb])
        self.act(rs[:], ps[:, 0:NT], AF.Sqrt, [b_psb], [b_rs], scale=1.0 / D, bias=EPS)
        self.recip(rs[:], rs[:], [b_rs], [b_rs])
        for c in range(NCH):
            t, bt = tmp[c % 2], b_tmp[c % 2]
            self.stt(t[:], xt[:, c, :], self.Atab[:, l, s, j * 8 + c:j * 8 + c + 1], rs[:], ALU.mult, ALU.mult,
                     [b_xt[c], b_rs, self.b_mod], [bt])
            self.act(h[:, c, :], t[:], AF.Identity, [bt, self.b_mod], [b_h[c]],
                     bias=self.modT[:, l, s, j * 24 + c:j * 24 + c + 1])

    def norm_mod(self, xt, b_xt, h, b_h, sq, b_sq, ps, b_psb, rs, b_rs, tmp, b_tmp, l, j, s, NT):
        self.norm_a(xt, b_xt, sq, b_sq)
        self.norm_b(xt, b_xt, h, b_h, sq, b_sq, ps, b_psb, rs, b_rs, tmp, b_tmp, l, j, s, NT)

    def load_x(self, xt, b_xt, tok0, NT, first, xtok=None, b_xtok=None, pst=None, b_pst=None):
        P = self.P
        if not first:
            P.dma("sp", xt[:], self.xT_d[:, :, tok0:tok0 + NT], b_xt[0], writes=b_xt)
            return
        for g in range(NT // 128):
            xk, bk = xtok[g % 2], b_xtok[g % 2]
            P.dma("sp", xk[:], self.x_d[tok0 + g * 128:tok0 + (g + 1) * 128, :], bk, writes=[bk])
            for c in range(NCH):
                pt, bpt = pst[c % 2], b_pst[c % 2]
                self.tr(pt[:, 0:128], xk[:, c * 128:(c + 1) * 128], self.identf[:], [bk, self.b_c], [bpt])
                self.cp("act" if c % 2 == 0 else "dve", xt[:, c, g * 128:(g + 1) * 128], pt[:, 0:128], [bpt], [b_xt[c]])

    def store_x(self, xo, b_xo, tok0, NT, last, xtok=None, b_xtok=None, pst=None, b_pst=None):
        P = self.P
        if not last:
            P.dma("sp", self.xT_d[:, :, tok0:tok0 + NT], xo[:], b_xo[0], reads=b_xo)
            return
        for g in range(NT // 128):
            xk, bk = xtok[g % 2], b_xtok[g % 2]
            for hh in range(2):
                pt, bpt = pst[hh], b_pst[hh]
                for cc in range(4):
                    c = hh * 4 + cc
                    self.tr(pt[:, cc * 128:(cc + 1) * 128], xo[:, c, g * 128:(g + 1) * 128], self.identf[:],
                            [b_xo[c], self.b_c], [bpt])
                self.cp("act" if hh == 0 else "dve", xk[:, hh * 512:(hh + 1) * 512], pt[:], [bpt], [bk])
            P.dma("sp", self.out_d[tok0 + g * 128:tok0 + (g + 1) * 128, :], xk[:], bk, reads=[bk])

    def ffn_weights(self, es, l, f, load=True):
        nc = self.nc
        wup = es.enter_context(nc.sbuf_tensor(self.nm("wup"), [128, NCH, 2 * DFF], BF16))
        wdn = es.enter_context(nc.sbuf_tensor(self.nm("wdn"), [128, NFF, D], BF16))
        w = (wup, wdn, Buf("wup"), Buf("wdn"))
        if load:
            self.ffn_weights_load(w, l, f)
        return w

    def ffn_weights_load(self, w, l, f):
        wup, wdn, b_wup, b_wdn = w
        for k in range(NCH):
            self.P.dma("pool", wup[:, k, :], self.w_up_d[l, f, k * 128:(k + 1) * 128, :], b_wup, writes=[b_wup],
                       max_dma_last_dim=4096)
        for k in range(NFF):
            self.P.dma("pool", wdn[:, k, :], self.w_dn_d[l, f, k * 128:(k + 1) * 128, :], b_wdn, writes=[b_wdn],
                       max_dma_last_dim=4096)

    def ffn_phase(self, l, f, first, last, w=None):
        nc, P = self.nc, self.P
        NT = 256
        j = 0 if f == 0 else 2
        with ExitStack() as es:
            sb = lambda n, s, d: es.enter_context(nc.sbuf_tensor(self.nm(n), s, d))
            wup, wdn, b_wup, b_wdn = w if w is not None else self.ffn_weights(es, l, f)
            NS = 2
            xt = [sb("xt", [128, NCH, NT], F32) for _ in range(NS)]
            b_xt = [[Buf(f"xt{i}_{c}") for c in range(NCH)] for i in range(NS)]
            hh = [sb("h", [128, NCH, NT], BF16) for _ in range(2)]
            b_hh = [[Buf(f"h{i}_{c}") for c in range(NCH)] for i in range(2)]
            sqq = [sb("sq", [128, NCH, NT], BF16) for _ in range(2)]
            b_sqq = [Buf("sq0"), Buf("sq1")]
            rss = [sb("rs", [128, NT], F32) for _ in range(2)]
            b_rss = [Buf("rs0"), Buf("rs1")]
            tmp = [sb("tmp", [128, NT], F32) for _ in range(2)]
            b_tmp = [Buf("tmp0"), Buf("tmp1")]
            sg = [sb("sg", [128, NT], F32) for _ in range(2)]
            b_sg = [Buf("sg0"), Buf("sg1")]
            a = sb("a", [128, NFF, NT], BF16)
            b_a = [Buf(f"a{k}") for k in range(NFF)]
            ps = [es.enter_context(nc.psum_tensor(self.nm("ps"), [128, 512], F32)) for _ in range(8)]
            b_ps = [Buf(f"ps{i}") for i in range(8)]
            xtok = b_xtok = None
            if first or last:
                xtok = [sb("xtok", [128, D], F32) for _ in range(2)]
                b_xtok = [Buf("xtok0"), Buf("xtok1")]
            pst, b_pst = [ps[7], ps[0]], [b_ps[7], b_ps[0]]
            ntl = self.T // NT

            def prep_a(ti):
                self.load_x(xt[ti % NS], b_xt[ti % NS], ti * NT, NT, first, xtok, b_xtok, pst, b_pst)
                self.norm_a(xt[ti % NS], b_xt[ti % NS], sqq[ti % 2], b_sqq[ti % 2])

            def prep_b(ti):
                self.norm_b(xt[ti % NS], b_xt[ti % NS], hh[ti % 2], b_hh[ti % 2], sqq[ti % 2], b_sqq[ti % 2], ps[0], b_ps[0],
                            rss[ti % 2], b_rss[ti % 2], tmp, b_tmp, l, j, (ti * NT) // self.S, NT)

            prep_a(0)
            prep_b(0)
            for ti in range(ntl):
                tok0 = ti * NT
                s = tok0 // self.S
                x, bx = xt[ti % NS], b_xt[ti % NS]
                h, b_h = hh[ti % 2], b_hh[ti % 2]
                for jj in range(NFF):
                    pg, pu = ps[1 + 2 * (jj % 2)], ps[2 + 2 * (jj % 2)]
                    bg, bu = b_ps[1 + 2 * (jj % 2)], b_ps[2 + 2 * (jj % 2)]
                    for k in range(NCH):
                        self.mm(pg[:, 0:NT], wup[:, k, jj * 128:(jj + 1) * 128], h[:, k, :], k == 0, k == NCH - 1,
                                [b_wup, b_h[k]], [bg])
                    for k in range(NCH):
                        self.mm(pu[:, 0:NT], wup[:, k, DFF + jj * 128:DFF + (jj + 1) * 128], h[:, k, :], k == 0,
                                k == NCH - 1, [b_wup, b_h[k]], [bu])
                    self.act(sg[jj % 2][:], pg[:, 0:NT], AF.Silu, [bg], [b_sg[jj % 2]])
                    self.tt("dve", a[:, jj, :], sg[jj % 2][:], pu[:, 0:NT], ALU.mult, [b_sg[jj % 2], bu], [b_a[jj]])
                    if jj == 12 and ti + 1 < ntl:
                        prep_a(ti + 1)
                if ti + 1 < ntl:
                    prep_b(ti + 1)
                for c in range(NCH):
                    pd, bd = ps[5 + c % 2], b_ps[5 + c % 2]
                    for k in range(NFF):
                        self.mm(pd[:, 0:NT], wdn[:, k, c * 128:(c + 1) * 128], a[:, k, :], k == 0, k == NFF - 1,
                                [b_wdn, b_a[k]], [bd])
                    self.stt(x[:, c, :], pd[:, 0:NT], self.Gtab[:, l, s, j * 8 + c:j * 8 + c + 1], x[:, c, :],
                             ALU.mult, ALU.add, [bd, bx[c], self.b_mod], [bx[c]])
                self.store_x(x, bx, tok0, NT, last, xtok, b_xtok, pst, b_pst)
            P.flush()

    def tile_bufs(self, sb, NT, nslots=2, h_only=False):
        o = {}
        if h_only:
            o["h"] = [sb("h", [128, NCH, NT], BF16) for _ in range(2)]
            o["b_h"] = [[Buf(f"h{i}_{c}") for c in range(NCH)] for i in range(2)]
            return o
        o["xt"] = [sb("xt", [128, NCH, NT], F32) for _ in range(nslots)]
        o["b_xt"] = [[Buf(f"xt{i}_{c}") for c in range(NCH)] for i in range(nslots)]
        o["h"] = [sb("h", [128, NCH, NT], BF16) for _ in range(2)]
        o["b_h"] = [[Buf(f"h{i}_{c}") for c in range(NCH)] for i in range(2)]
        o["sq"] = [sb("sq", [128, NCH, NT], BF16) for _ in range(2)]
        o["b_sq"] = [Buf("sq0"), Buf("sq1")]
        o["rs"] = [sb("rs", [128, NT], F32) for _ in range(2)]
        o["b_rs"] = [Buf("rs0"), Buf("rs1")]
        o["tmp"] = [sb("tmp", [128, NT], F32) for _ in range(2)]
        o["b_tmp"] = [Buf("tmp0"), Buf("tmp1")]
        return o

    def tile_h(self, o, ti, tok0, NT, ps0, b_ps0, l, s):
        x, bx = o["xt"][ti % 2], o["b_xt"][ti % 2]
        self.load_x(x, bx, tok0, NT, False)
        i = ti % 2
        self.norm_mod(x, bx, o["h"][i], o["b_h"][i], o["sq"][i], o["b_sq"][i], ps0, b_ps0, o["rs"][i], o["b_rs"][i],
                      o["tmp"], o["b_tmp"], l, 1, s, NT)
        return o["h"][i], o["b_h"][i]

    def tile_a(self, o, ti, tok0, NT):
        x, bx = o["xt"][ti % 2], o["b_xt"][ti % 2]
        self.load_x(x, bx, tok0, NT, False)
        self.norm_a(x, bx, o["sq"][ti % 2], o["b_sq"][ti % 2])

    def tile_b(self, o, ti, NT, ps0, b_ps0, l, s):
        i = ti % 2
        self.norm_b(o["xt"][i], o["b_xt"][i], o["h"][i], o["b_h"][i], o["sq"][i], o["b_sq"][i], ps0, b_ps0, o["rs"][i],
                    o["b_rs"][i], o["tmp"], o["b_tmp"], l, 1, s, NT)

    def store_h(self, o, ti, tok0, NT):
        i = ti % 2
        self.P.dma("act", self.hT_d[:, :, tok0:tok0 + NT], o["h"][i][:], o["b_h"][i][0], reads=o["b_h"][i])

    def load_h(self, o, ti, tok0, NT):
        i = ti % 2
        self.P.dma("sp", o["h"][i][:], self.hT_d[:, :, tok0:tok0 + NT], o["b_h"][i][0], writes=o["b_h"][i])

    def load_w_in(self, w, b_w, l, c0, c1):
        for k in range(NCH):
            self.P.dma("pool", w[:, k, :], self.w_in_d[l, k * 128:(k + 1) * 128, c0:c1], b_w, writes=[b_w],
                       max_dma_last_dim=4096)

    def group_rms_store(self, y, b_y, nchk, gm0, l, tok0, NT, sqy, b_sqy, ps, b_ps, rs2, b_rs2, yn, b_yn, ych0):
        self.act(sqy[:], y[:], AF.Square, b_y, [b_sqy])
        for c in range(nchk):
            self.mm(ps[:, 0:NT], self.ones16[:], sqy[:, c, :], c == 0, c == nchk - 1, [b_sqy, self.b_c], [b_ps])
        self.act(rs2[:], ps[:, 0:NT], AF.Sqrt, [b_ps], [b_rs2], scale=1.0 / (nchk * 128), bias=EPS)
        self.recip(rs2[:], rs2[:], [b_rs2], [b_rs2])
        for c in range(nchk):
            self.stt(yn[:, c, :], y[:, c, :], self.ppc(l, "gmix", gm0 + c, 1), rs2[:], ALU.mult, ALU.mult,
                     [b_y[c], b_rs2, self.b_pp], [b_yn])
        self.P.dma("sp", self.yT_d[:, ych0:ych0 + nchk, tok0:tok0 + NT], yn[:], b_yn, reads=[b_yn])

    def lru_phase(self, l, s):
        nc, P = self.nc, self.P
        NT = 512
        with ExitStack() as es:
            sb = lambda n, sh, d: es.enter_context(nc.sbuf_tensor(self.nm(n), sh, d))
            win = sb("win", [128, NCH, 1024], BF16)
            b_win = Buf("win")
            self.load_w_in(win, b_win, l, 0, 1024)
            wg = [sb("wg", [128, 4, 128], F32) for _ in range(2)]
            b_wg = [Buf("wg0"), Buf("wg1")]
            for gi, wd in enumerate((self.w_rg_d, self.w_ig_d)):
                self.memset("pool", wg[gi][:], 0.0, [b_wg[gi]])
                src = wd[l].rearrange("(c two) i j -> two i c j", two=2)
                P.dma("sp", wg[gi][0:64, :, 0:64], src[0], b_wg[gi], writes=[b_wg[gi]])
                P.dma("sp", wg[gi][64:128, :, 64:128], src[1], b_wg[gi], writes=[b_wg[gi]])
            o = self.tile_bufs(sb, NT)
            ps = [es.enter_context(nc.psum_tensor(self.nm("ps"), [128, 512], F32)) for _ in range(8)]
            b_ps = [Buf(f"ps{i}") for i in range(8)]
            lx = [sb("lx", [128, NT + 3], F32) for _ in range(4)]
            b_lx = [Buf(f"lx{c}") for c in range(4)]
            carry = sb("carry", [128, 4], F32)
            b_carry = [Buf(f"carry{c}") for c in range(4)]
            y = sb("y", [128, 4, NT], F32)
            b_y = [Buf(f"y{c}") for c in range(4)]
            f32t = lambda n: (sb(n, [128, NT], F32), Buf(n))
            f32q = lambda n: [f32t(n + str(i)) for i in range(4)]
            gs_, u_, r_, ig_, a_, m_ = (f32q(n) for n in ("gs", "u", "r", "ig", "a", "m"))
            hs_ = r_
            t1_2 = [f32q("t1a"), f32q("t1b")]
            y2 = [y, sb("y2", [128, 4, NT], F32)]
            b_y2 = [b_y, [Buf(f"y2{c}") for c in range(4)]]
            sqy = sb("sqy", [128, 4, NT], BF16)
            b_sqy = Buf("sqy")
            rs2, b_rs2 = f32t("rs2")
            yn = sb("yn", [128, 4, NT], BF16)
            b_yn = Buf("yn")
            cw = lambda k, c: self.ppc(l, "cw", k * 4 + c, 1)
            C4 = range(4)
            ntl = self.S // NT
            S0 = s * self.S

            def head(ti):
                h, b_h = o["h"][ti % 2], o["b_h"][ti % 2]
                t1_ = t1_2[ti % 2]
                for c in C4:
                    pl, bl = ps[1 + c % 2], b_ps[1 + c % 2]
                    for k in range(NCH):
                        self.mm(pl[:], win[:, k, c * 128:(c + 1) * 128], h[:, k, :], k == 0, k == NCH - 1,
                                [b_win, b_h[k]], [bl])
                    if ti == 0:
                        self.memset("pool", lx[c][:, 0:3], 0.0, [b_lx[c]])
                    else:
                        self.cp("pool", lx[c][:, 0:3], lx[c][:, NT:NT + 3], [b_lx[c]], [b_lx[c]])
                    self.cp("act", lx[c][:, 3:NT + 3], pl[:], [bl], [b_lx[c]])
                if ti + 1 < ntl:
                    self.tile_a(o, ti + 1, S0 + (ti + 1) * NT, NT)
                for c in C4:
                    pg, bg = ps[3 + c % 2], b_ps[3 + c % 2]
                    for k in range(NCH):
                        self.mm(pg[:], win[:, k, 512 + c * 128:512 + (c + 1) * 128], h[:, k, :], k == 0, k == NCH - 1,
                                [b_win, b_h[k]], [bg])
                    self.cp("act", gs_[c][0][:], pg[:], [bg], [gs_[c][1]])
                    self.act(t1_[c][0][:], pg[:], AF.Square, [bg], [t1_[c][1]])
                if ti + 1 < ntl:
                    self.tile_b(o, ti + 1, NT, ps[0], b_ps[0], l, s)
                    self.store_h(o, ti + 1, S0 + (ti + 1) * NT, NT)
                for c in C4:
                    u, b_u = u_[c]
                    self.act(u[:], lx[c][:, 0:NT], AF.Identity, [b_lx[c], self.b_pp], [b_u], scale=cw(0, c),
                             bias=self.ppc(l, "cb", c, 1))
                    for k in range(1, 4):
                        self.stt(u[:], lx[c][:, k:k + NT], cw(k, c), u[:], ALU.mult, ALU.add, [b_lx[c], b_u, self.b_pp], [b_u])

            def st_a(ti):
                t1_ = t1_2[ti % 2]
                for c in C4:
                    u, b_u = u_[c]
                    pr, br_ = ps[1 + c], b_ps[1 + c]
                    pi, bi_ = ps[5 + c % 3], b_ps[5 + c % 3]
                    self.mm(pr[:], wg[0][:, c, :], u[:], True, True, [b_wg[0], b_u], [br_])
                    self.mm(pi[:], wg[1][:, c, :], u[:], True, True, [b_wg[1], b_u], [bi_])
                    self.act(r_[c][0][:], pr[:], AF.Sigmoid, [br_, self.b_pp], [r_[c][1]], bias=self.ppc(l, "br", c, 1))
                    self.act(ig_[c][0][:], pi[:], AF.Sigmoid, [bi_, self.b_pp], [ig_[c][1]], bias=self.ppc(l, "bi", c, 1))
                for c in C4:
                    t1, b_t1 = t1_[c]
                    self.ts("dve", t1[:], t1[:], 0.044715, 1.0, ALU.mult, ALU.add, [b_t1], [b_t1])
                    self.tt("pool", t1[:], t1[:], gs_[c][0][:], ALU.mult, [b_t1, gs_[c][1]], [b_t1])
                for c in C4:
                    t1, b_t1 = t1_[c]
                    self.act(t1[:], t1[:], AF.Sigmoid, [b_t1], [b_t1], scale=1.5957691216057308)
                for c in C4:
                    t1, b_t1 = t1_[c]
                    self.tt("pool", t1[:], t1[:], gs_[c][0][:], ALU.mult, [b_t1, gs_[c][1]], [b_t1])
                for c in C4:
                    r, b_r = r_[c]
                    self.act(a_[c][0][:], r[:], AF.Exp, [b_r, self.b_mod], [a_[c][1]], scale=self.cneg[:, l, 0, c:c + 1])
                    self.act(m_[c][0][:], r[:], AF.Exp, [b_r, self.b_mod], [m_[c][1]], scale=self.cneg[:, l, 1, c:c + 1])
                    ig, b_ig = ig_[c]
                    self.tt("pool", ig[:], ig[:], u_[c][0][:], ALU.mult, [b_ig, u_[c][1]], [b_ig])

            def st_d(ti):
                t1_ = t1_2[ti % 2]
                yy, b_yy = y2[ti % 2], b_y2[ti % 2]
                for c in C4:
                    self.act(m_[c][0][:], m_[c][0][:], AF.Sqrt, [m_[c][1]], [m_[c][1]], scale=-1.0, bias=1.0)
                    ig, b_ig = ig_[c]
                    self.tt("dve", ig[:], ig[:], m_[c][0][:], ALU.mult, [b_ig, m_[c][1]], [b_ig])
                for c in C4:
                    hs, b_hs = hs_[c]
                    init = 0.0 if ti == 0 else carry[:, c:c + 1]
                    self.P.op("dve", lambda e, init=init, hs=hs, a=a_[c][0], ig=ig_[c][0]: e.tensor_tensor_scan(
                        out=hs[:], data0=a[:], data1=ig[:], initial=init, op0=ALU.mult, op1=ALU.add),
                              [a_[c][1], ig_[c][1], b_carry[c]], [b_hs])
                    self.cp("pool", carry[:, c:c + 1], hs[:, NT - 1:NT], [b_hs], [b_carry[c]])
                    self.tt("pool", yy[:, c, :], hs[:], t1_[c][0][:], ALU.mult, [b_hs, t1_[c][1]], [b_yy[c]])

            def st_b(ti):
                self.group_rms_store(y2[ti % 2], b_y2[ti % 2], 4, 0, l, S0 + ti * NT, NT, sqy, b_sqy, ps[0], b_ps[0],
                                     rs2, b_rs2, yn, b_yn, 0)

            self.tile_a(o, 0, S0, NT)
            self.tile_b(o, 0, NT, ps[0], b_ps[0], l, s)
            self.store_h(o, 0, S0, NT)
            head(0)
            for ti in range(ntl):
                st_a(ti)
                if ti > 0:
                    st_b(ti - 1)
                if ti + 1 < ntl:
                    head(ti + 1)
                st_d(ti)
            st_b(ntl - 1)
            P.flush()

    def sb_phase(self, l, s):
        nc, P, S = self.nc, self.P, self.S
        NT = 512
        NKT = S // 128
        with ExitStack() as es:
            sb = lambda n, sh, d: es.enter_context(nc.sbuf_tensor(self.nm(n), sh, d))
            win = sb("win", [128, NCH, 768], BF16)
            b_win = Buf("win")
            self.load_w_in(win, b_win, l, 1024, 1792)
            o = self.tile_bufs(sb, NT, h_only=True)
            ps = [es.enter_context(nc.psum_tensor(self.nm("ps"), [128, 512], F32)) for _ in range(8)]
            b_ps = [Buf(f"ps{i}") for i in range(8)]
            qh = sb("qh", [128, 4, S], BF16)
            kT = sb("kT", [128, 2, S], BF16)
            v = sb("v", [128, NKT, 256], BF16)
            nsp = S // 512
            b_q = [[Buf(f"q{c}_{i}") for i in range(nsp)] for c in range(2)]
            b_k = [[Buf(f"k{c}_{i}") for i in range(nsp)] for c in range(2)]
            b_v = [Buf(f"v{i}") for i in range(NKT)]
            for hq in range(4):
                self.memset("pool", qh[:, hq, :], 0.0, [b for bb in b_q for b in bb])
            ntl = S // NT
            self.load_h(o, 0, s * S, NT)
            for ti in range(ntl):
                tok0 = s * S + ti * NT
                h, b_h = o["h"][ti % 2], o["b_h"][ti % 2]
                if ti + 1 < ntl:
                    self.load_h(o, ti + 1, tok0 + NT, NT)
                for c in range(2):
                    pq, bq = ps[1 + c], b_ps[1 + c]
                    for k in range(NCH):
                        self.mm(pq[:], win[:, k, c * 128:(c + 1) * 128], h[:, k, :], k == 0, k == NCH - 1, [b_win, b_h[k]], [bq])
                    self.act(qh[0:64, 2 * c, ti * NT:(ti + 1) * NT], pq[0:64, :], AF.Identity, [bq], [b_q[c][ti]], scale=0.125)
                    self.act(qh[64:128, 2 * c + 1, ti * NT:(ti + 1) * NT], pq[64:128, :], AF.Identity, [bq], [b_q[c][ti]],
                             scale=0.125)
                    pk, bk = ps[3 + c], b_ps[3 + c]
                    for k in range(NCH):
                        self.mm(pk[:], win[:, k, 256 + c * 128:256 + (c + 1) * 128], h[:, k, :], k == 0, k == NCH - 1,
                                [b_win, b_h[k]], [bk])
                    self.cp("dve", kT[:, c, ti * NT:(ti + 1) * NT], pk[:], [bk], [b_k[c][ti]])
                for g in range(4):
                    pv, bv = ps[5 + g % 2], b_ps[5 + g % 2]
                    for k in range(NCH):
                        self.mm(pv[:, 0:256], h[:, k, g * 128:(g + 1) * 128], win[:, k, 512:768], k == 0, k == NCH - 1,
                                [b_win, b_h[k]], [bv])
                    self.cp("act" if g % 2 == 0 else "dve", v[:, ti * 4 + g, :], pv[:, 0:256], [bv], [b_v[ti * 4 + g]])
            f32t = lambda n: (sb(n, [128, 512], F32), Buf(n))
            R = [f32t("R0"), f32t("R1")]
            E = [f32t("E0"), f32t("E1")]
            Tt = [f32t("T0"), f32t("T1")]
            SP = [(sb(f"sp{i}", [128, 512], BF16), Buf(f"sp{i}")) for i in range(3)]
            W = [(sb(f"w{i}", [128, 512], BF16), Buf(f"w{i}")) for i in range(3)]
            ysb = sb("ysb", [128, 2, 512], F32)
            b_ysb = [Buf("ysb0"), Buf("ysb1")]
            sqy = sb("sqy", [128, 2, 512], BF16)
            b_sqy = Buf("sqy")
            rs2, b_rs2 = f32t("rs2")
            yn = sb("yn", [128, 2, 512], BF16)
            b_yn = Buf("yn")
            pairs = []
            grp = 0
            for qs in range(nsp):
                njt = 4 * qs + 4
                for hd in range(4):
                    for idx, jt in enumerate(range(njt - 1, -1, -1)):
                        pairs.append((qs, hd, idx, jt, njt, grp))
                    grp += 1
            NP = len(pairs)

            def geo(i):
                qs, hd, idx, jt, njt, g = pairs[i]
                c, po = hd // 2, (hd % 2) * 64
                return qs, hd, idx, jt, njt, g, c, po, jt >= 4 * qs, jt - 4 * qs

            def s1(i):
                qs, hd, idx, jt, njt, g, c, po, diag, od = geo(i)
                pA, bA = ps[1 + i % 2], b_ps[1 + i % 2]
                self.mm(pA[:], kT[:, c, jt * 128:(jt + 1) * 128], qh[:, hd, qs * 512:(qs + 1) * 512],
                        True, True, [b_q[c][qs], b_k[c][jt // 4]], [bA])

            def s2(i):
                qs, hd, idx, jt, njt, g, c, po, diag, od = geo(i)
                pA, bA = ps[1 + i % 2], b_ps[1 + i % 2]
                Eb, bE = E[i % 2]
                sp, bsp = SP[i % 3]
                self.act(Eb[:], pA[:], AF.Exp, [bA], [bE])
                self.act(sp[:], Eb[:], AF.Ln, [bE], [bsp], bias=1.0)
                if diag:
                    self.tt("pool", sp[:], sp[:], self.mask_lt[:, od, :], ALU.mult, [bsp, self.b_c], [bsp])

            def s3(i):
                qs, hd, idx, jt, njt, g, c, po, diag, od = geo(i)
                pB, bB = ps[3 + i % 2], b_ps[3 + i % 2]
                sp, bsp = SP[i % 3]
                self.mm(pB[:], kT[:, c, jt * 128:(jt + 1) * 128], qh[:, hd, qs * 512:(qs + 1) * 512],
                        True, False, [b_q[c][qs], b_k[c][jt // 4]], [bB])
                self.mm(pB[:], self.negU16[:], sp[:], False, True, [bsp, self.b_c], [bB])
                if idx < njt - 1:
                    pC, bC = ps[5], b_ps[5]
                    self.mm(pC[:], self.ones16[:], sp[:], True, True, [bsp, self.b_c], [bC])

            def s4(i):
                qs, hd, idx, jt, njt, g, c, po, diag, od = geo(i)
                pB, bB = ps[3 + i % 2], b_ps[3 + i % 2]
                Rb, bR = R[g % 2]
                Tb, bT = Tt[i % 2]
                w, bw = W[i % 3]
                if idx == 0:
                    self.act(w[:], pB[:], AF.Exp, [bB], [bw])
                else:
                    self.tt("dve", Tb[:], pB[:], Rb[:], ALU.subtract, [bB, bR], [bT])
                    self.act(w[:], Tb[:], AF.Exp, [bT], [bw])
                if diag:
                    self.tt("pool", w[:], w[:], self.mask_lt[:, od, :], ALU.mult, [bw, self.b_c], [bw])
                if idx < njt - 1:
                    pC, bC = ps[5], b_ps[5]
                    if idx == 0:
                        self.cp("dve", Rb[:], pC[:], [bC], [bR])
                    else:
                        self.tt("dve", Rb[:], pC[:], Rb[:], ALU.add, [bC, bR], [bR])

            def s5(i):
                qs, hd, idx, jt, njt, g, c, po, diag, od = geo(i)
                w, bw = W[i % 3]
                pO, bO = ps[6 + g % 2], b_ps[6 + g % 2]
                self.mm(pO[:], v[:, jt, c * 128:(c + 1) * 128], w[:], idx == 0, idx == njt - 1, [b_v[jt], bw], [bO])
                if idx == njt - 1:
                    self.cp("act", ysb[po:po + 64, c, :], pO[po:po + 64, :], [bO], [b_ysb[c]])
                    if hd == 3:
                        self.group_rms_store(ysb, b_ysb, 2, 4, l, s * S + qs * 512, 512, sqy, b_sqy, ps[0], b_ps[0],
                                             rs2, b_rs2, yn, b_yn, 4)

            for n in range(NP + 3):
                if n < NP:
                    s1(n)
                    s2(n)
                if 0 <= n - 2 < NP:
                    s4(n - 2)
                if 0 <= n - 1 < NP:
                    s3(n - 1)
                if 0 <= n - 3 < NP:
                    s5(n - 3)
            P.flush()

    def fox_phase(self, l, s):
        nc, P, S = self.nc, self.P, self.S
        NT = 512
        NKT = S // 128
        with ExitStack() as es:
            sb = lambda n, sh, d: es.enter_context(nc.sbuf_tensor(self.nm(n), sh, d))
            win = sb("win", [128, NCH, 772], BF16)
            b_win = Buf("win")
            self.load_w_in(win, b_win, l, 1792, 2564)
            o = self.tile_bufs(sb, NT, h_only=True)
            ps = [es.enter_context(nc.psum_tensor(self.nm("ps"), [128, 512], F32)) for _ in range(8)]
            b_ps = [Buf(f"ps{i}") for i in range(8)]
            qh = sb("qh", [128, 4, S], BF16)
            kT = sb("kT", [128, 2, S], BF16)
            va = sb("va", [128, NKT, 4, 128], BF16)
            Fn = sb("Fn", [128, NKT, 4], F32)
            Fc = sb("Fc", [128, NKT, 4], F32)
            nsp = S // 512
            b_q = [[Buf(f"q{c}_{i}") for i in range(nsp)] for c in range(2)]
            b_k = [[Buf(f"k{c}_{i}") for i in range(nsp)] for c in range(2)]
            b_v = [Buf(f"v{i}") for i in range(NKT)]
            b_F = Buf("F")
            f32t = lambda n: (sb(n, [128, 512], F32), Buf(n))
            sq16 = [(sb(f"sq16{i}", [128, 512], BF16), Buf(f"sq16{i}")) for i in range(4)]
            rsq = [f32t(f"rsq{i}") for i in range(4)]
            tfs = [(sb(f"tf{i}", [128, 4], F32), Buf(f"tf{i}")) for i in range(4)]
            for i in range(NKT):
                self.memset("pool", va[:, i, :, :], 1.0, [b_v[i]])
            for hq in range(4):
                self.memset("pool", qh[:, hq, :], 0.0, [b for bb in b_q for b in bb])
            ntl = S // NT
            self.load_h(o, 0, s * S, NT)
            for ti in range(ntl):
                tok0 = s * S + ti * NT
                h, b_h = o["h"][ti % 2], o["b_h"][ti % 2]
                if ti + 1 < ntl:
                    self.load_h(o, ti + 1, tok0 + NT, NT)
                combos = [(0, 0), (0, 1), (1, 0), (1, 1)]
                for i2, (qk, c) in enumerate(combos):
                    pq, bq = ps[1 + i2], b_ps[1 + i2]
                    for k in range(NCH):
                        self.mm(pq[:], win[:, k, qk * 256 + c * 128:qk * 256 + (c + 1) * 128], h[:, k, :], k == 0,
                                k == NCH - 1, [b_win, b_h[k]], [bq])
                for i2 in range(4):
                    self.act(sq16[i2][0][:], ps[1 + i2][:], AF.Square, [b_ps[1 + i2]], [sq16[i2][1]])
                for i2 in range(4):
                    pss, bss = ps[5 + i2 % 2], b_ps[5 + i2 % 2]
                    self.mm(pss[:], self.bd16[:], sq16[i2][0][:], True, True, [sq16[i2][1], self.b_c], [bss])
                    self.act(rsq[i2][0][:], pss[:], AF.Sqrt, [bss], [rsq[i2][1]], scale=1.0 / 64, bias=EPS)
                for i2 in range(4):
                    self.recip(rsq[i2][0][:], rsq[i2][0][:], [rsq[i2][1]], [rsq[i2][1]])
                for i2, (qk, c) in enumerate(combos):
                    pq, bq = ps[1 + i2], b_ps[1 + i2]
                    rq_, brq = rsq[i2]
                    if qk == 1:
                        self.stt(kT[:, c, ti * NT:(ti + 1) * NT], pq[:], self.gq8[:, l, 1:2], rq_[:], ALU.mult, ALU.mult,
                                 [bq, brq, self.b_mod], [b_k[c][ti]])
                    else:
                        for hf in range(2):
                            pp_ = slice(hf * 64, hf * 64 + 64)
                            self.stt(qh[pp_, 2 * c + hf, ti * NT:(ti + 1) * NT], pq[pp_, :], self.gq8[pp_, l, 0:1], rq_[pp_, :],
                                     ALU.mult, ALU.mult, [bq, brq, self.b_mod], [b_q[c][ti]])
                for g in range(4):
                    tg = ti * 4 + g
                    tf, b_tf = tfs[g]
                    pv, bv = ps[5 + g % 2], b_ps[5 + g % 2]
                    for k in range(NCH):
                        self.mm(pv[:, 0:260], h[:, k, g * 128:(g + 1) * 128], win[:, k, 512:772], k == 0, k == NCH - 1,
                                [b_win, b_h[k]], [bv])
                    pv4 = pv[:, 0:256].rearrange("p (h d) -> p h d", h=4)
                    self.cp("act", va[:, tg, 0:4:2, 0:64], pv4[:, 0:4:2, :], [bv], [b_v[tg]])
                    self.cp("act", va[:, tg, 1:4:2, 64:128], pv4[:, 1:4:2, :], [bv], [b_v[tg]])
                    if CUT < 2:
                        continue
                    self.cp("act", tf[:], pv[:, 256:260], [bv], [b_tf])
                    self.tt("dve", tf[:], tf[:], self.ppc(l, "bf", 0, 4), ALU.add, [b_tf, self.b_pp], [b_tf])
                    self.act(tf[:], tf[:], AF.Exp, [b_tf], [b_tf], scale=-1.0)
                    self.act(tf[:], tf[:], AF.Ln, [b_tf], [b_tf], bias=1.0)
                    pF, bF_ = ps[7], b_ps[7]
                    self.mm(pF[:, 8 * g:8 * g + 4], self.Uf[:], tf[:], True, True, [b_tf, self.b_c], [bF_])
                    self.cp("act", Fn[:, tg, :], pF[:, 8 * g:8 * g + 4], [bF_], [b_F])
            pT, bT_ = ps[7], b_ps[7]
            self.mm(pT[:, 0:NKT * 4], self.selL[:], Fn[:].rearrange("p g h -> p (g h)"), True, True, [b_F, self.b_c], [bT_])
            Ft = sb("Ft", [128, NKT, 4], F32)
            b_Ft = Buf("Ft")
            self.cp("act", Ft[:].rearrange("p g h -> p (g h)"), pT[:, 0:NKT * 4], [bT_], [b_Ft])
            for hd in range(4):
                self.P.op("dve", lambda e, hd=hd: e.tensor_tensor_scan(out=Fc[:, :, hd], data0=self.onesf[:, 0:NKT],
                                                                      data1=Ft[:, :, hd], initial=0.0, op0=ALU.mult, op1=ALU.add),
                          [b_Ft, self.b_c], [b_F])
            if NKT > 1:
                self.tt("dve", Fn[:, 1:NKT, :], Fn[:, 1:NKT, :], Fc[:, 0:NKT - 1, :], ALU.add, [b_F], [b_F])
            Pw = [(sb(f"p{i}", [128, 512], BF16), Buf(f"p{i}")) for i in range(4)]
            bias_t = [(sb(f"bias{i}", [128, NKT], F32), Buf(f"bias{i}")) for i in range(2)]
            rd, b_rd = rsq[0]
            bsb, b_bsb = rsq[1]
            yfx = sb("yfx", [128, 2, 512], F32)
            b_yfx = [Buf("yfx0"), Buf("yfx1")]
            sqy = sb("sqy", [128, 2, 512], BF16)
            b_sqy = Buf("sqy")
            rs2, b_rs2 = f32t("rs2")
            yn = sb("yn", [128, 2, 512], BF16)
            b_yn = Buf("yn")
            pairs = []
            grp = 0
            for qs in range(nsp):
                njt = 4 * qs + 4
                for hd in range(4):
                    for jt in range(njt):
                        pairs.append((qs, hd, jt, njt, grp))
                    grp += 1
            NP = len(pairs)

            def geo(i):
                qs, hd, jt, njt, g = pairs[i]
                return qs, hd, jt, njt, g, hd // 2, (hd % 2) * 64, jt >= 4 * qs, jt - 4 * qs

            def f1(i):
                qs, hd, jt, njt, g, c, po, diag, od = geo(i)
                if jt == 0:
                    bt_, bbt = bias_t[g % 2]
                    self.ts("dve", bt_[:, 0:njt], Fn[:, 0:njt, hd], Fc[:, njt - 1, hd:hd + 1], None, ALU.subtract, ALU.bypass,
                            [b_F], [bbt])
                pA, bA = ps[1 + i % 4], b_ps[1 + i % 4]
                self.mm(pA[:], kT[:, c, jt * 128:(jt + 1) * 128], qh[:, hd, qs * 512:(qs + 1) * 512],
                        True, True, [b_q[c][qs], b_k[c][jt // 4]], [bA])

            def f2(i):
                qs, hd, jt, njt, g, c, po, diag, od = geo(i)
                bt_, bbt = bias_t[g % 2]
                pA, bA = ps[1 + i % 4], b_ps[1 + i % 4]
                p_, bp = Pw[i % 4]
                self.act(p_[:], pA[:], AF.Exp, [bA, bbt], [bp], bias=bt_[:, jt:jt + 1])
                if diag:
                    self.tt("dve", p_[:], p_[:], self.mask_le[:, od, :], ALU.mult, [bp, self.b_c], [bp])

            def f3(i):
                qs, hd, jt, njt, g, c, po, diag, od = geo(i)
                p_, bp = Pw[i % 4]
                pO, bO = ps[5 + g % 2], b_ps[5 + g % 2]
                self.mm(pO[:], va[:, jt, hd, :], p_[:], jt == 0, jt == njt - 1, [b_v[jt], bp], [bO])
                if jt == njt - 1:
                    def fin(pO=pO, bO=bO, po=po, c=c, hd=hd, qs=qs):
                        dn = slice(64 - po, 128 - po)
                        self.recip(rd[dn, :], pO[dn, :], [bO], [b_rd])
                        self.cp("dve", bsb[po:po + 64, :], rd[dn, :], [b_rd], [b_bsb])
                        self.tt("dve", yfx[po:po + 64, c, :], pO[po:po + 64, :], bsb[po:po + 64, :], ALU.mult, [bO, b_bsb],
                                [b_yfx[c]])
                        if hd == 3:
                            self.group_rms_store(yfx, b_yfx, 2, 6, l, s * S + qs * 512, 512, sqy, b_sqy, ps[0], b_ps[0],
                                                 rs2, b_rs2, yn, b_yn, 6)
                    pend.append((i + 3, fin))

            pend = []
            for n in range(NP + 3):
                if n < NP:
                    f1(n)
                if 0 <= n - 1 < NP:
                    f2(n - 1)
                if 0 <= n - 3 < NP:
                    f3(n - 3)
                while pend and pend[0][0] <= n - 3:
                    pend.pop(0)[1]()
            for _, fn in pend:
                fn()
            P.flush()

    def outp_wo(self, es, l):
        wo = es.enter_context(self.nc.sbuf_tensor(self.nm("wo"), [128, NCH, D], BF16))
        b_wo = Buf("wo")
        for k in range(NCH):
            self.P.dma("pool", wo[:, k, :], self.w_out_d[l, k * 128:(k + 1) * 128, :], b_wo, writes=[b_wo], max_dma_last_dim=4096)
        self.wo_t = (wo, b_wo)

    def outp_phase(self, l):
        nc, P = self.nc, self.P
        NT = 256
        with ExitStack() as es:
            sb = lambda n, sh, d: es.enter_context(nc.sbuf_tensor(self.nm(n), sh, d))
            wo, b_wo = self.wo_t
            xt = [sb("xt", [128, NCH, NT], F32) for _ in range(2)]
            b_xt = [[Buf(f"xt{i}_{c}") for c in range(NCH)] for i in range(2)]
            yt = [sb("yt", [128, NCH, NT], BF16) for _ in range(2)]
            b_yt = [Buf("yt0"), Buf("yt1")]
            ps = [es.enter_context(nc.psum_tensor(self.nm("ps"), [128, 512], F32)) for _ in range(4)]
            b_ps = [Buf(f"ps{i}") for i in range(4)]
            for ti in range(self.T // NT):
                tok0 = ti * NT
                s = tok0 // self.S
                x, bx = xt[ti % 2], b_xt[ti % 2]
                y, by = yt[ti % 2], b_yt[ti % 2]
                self.load_x(x, bx, tok0, NT, False)
                P.dma("act", y[:], self.yT_d[:, :, tok0:tok0 + NT], by, writes=[by])
                for c in range(NCH):
                    pd, bd = ps[c % 4], b_ps[c % 4]
                    for k in range(NCH):
                        self.mm(pd[:, 0:NT], wo[:, k, c * 128:(c + 1) * 128], y[:, k, :], k == 0, k == NCH - 1, [b_wo, by], [bd])
                    self.stt(x[:, c, :], pd[:, 0:NT], self.Gtab[:, l, s, 8 + c:9 + c], x[:, c, :], ALU.mult, ALU.add,
                             [bd, bx[c], self.b_mod], [bx[c]])
                self.store_x(x, bx, tok0, NT, False)
            P.flush()


def make_in_maps(nseq_per_core, ncores, x, c, w_ada, b_ada, g_norm, w_ffn_up, w_ffn_down, w_in, b_fgate, conv_w, conv_b,
                 w_rgate, b_rgate, w_igate, b_igate, lru_lambda, g_qk, g_mix_out, w_out):
    f = lambda a: np.ascontiguousarray(np.asarray(a, dtype=np.float32))
    x, c = f(x), f(c)
    S = x.shape[1]
    pp = pack_pp(f(g_norm), f(b_ada), f(conv_w), f(conv_b), f(b_rgate), f(b_igate), f(lru_lambda), f(g_mix_out),
                 f(g_qk), f(b_fgate))
    shared = {"pp": pp, "w_ada": f(w_ada), "w_ffn_up": f(w_ffn_up), "w_ffn_down": f(w_ffn_down), "w_in": f(w_in),
              "w_rgate": f(w_rgate), "w_igate": f(w_igate), "w_out": f(w_out)}
    maps = []
    for i in range(ncores):
        xs = x[i * nseq_per_core:(i + 1) * nseq_per_core].reshape(nseq_per_core * S, D)
        cs = c[i * nseq_per_core:(i + 1) * nseq_per_core]
        cT = np.ascontiguousarray(cs.reshape(nseq_per_core, NCH, 128).transpose(2, 1, 0).reshape(128, NCH * nseq_per_core))
        m = dict(shared)
        m["x"] = np.ascontiguousarray(xs)
        m["cT"] = cT
        maps.append(m)
    return maps


_CACHE = {}


def kernel(**inputs):
    x = inputs["x"]
    B, S, _ = x.shape
    ns = B // NCORES
    key = (ns, S)
    if key not in _CACHE:
        _CACHE[key] = MK(ns, S)
    mk = _CACHE[key]
    maps = make_in_maps(ns, NCORES, **inputs)
    res = run_bass_kernel_spmd(mk.nc, maps, core_ids=list(range(NCORES)))
    out = np.stack([r["out"].reshape(ns, S, D) for r in res.results], axis=0).reshape(B, S, D)
    return out.astype(np.float32)
```

```python
from contextlib import ExitStack
import os
CUT = int(os.environ.get('CUT', '99'))
CUT2 = int(os.environ.get('CUT2', '99'))
import numpy as np
import concourse.bass as bass
import concourse.mybir as mybir
from concourse.bass_utils import run_bass_kernel_spmd

F32 = mybir.dt.float32
BF16 = mybir.dt.bfloat16
AF = mybir.ActivationFunctionType
ALU = mybir.AluOpType

ENG = ("pe", "act", "dve", "pool", "sp")

D = 1024
NCH = 8
DFF = 2816
NFF = 22
NIN = 2564
EPS = 1e-6
NCORES = 8


class Buf:
    __slots__ = ("name", "w", "r", "dk", "gd")

    def __init__(self, name):
        self.name = name
        self.w = None
        self.r = {}
        self.dk = None
        self.gd = ()


class Prog:
    def __init__(self, nc):
        self.nc = nc
        self.sems = {e: nc.alloc_semaphore("s_" + e) for e in ENG}
        self.val = {e: 0 for e in ENG}
        self.seen = {e: {} for e in ENG}
        self.q = {e: [] for e in ENG}
        self.ninst = 0

    def _need(self, e, reads, writes, skip_key=None):
        need = {}
        seen = self.seen[e]

        def add(k, v, raw):
            if k == e and e == "pe":
                return
            if seen.get(k, 0) >= v:
                return
            if need.get(k, 0) < v:
                need[k] = v

        for b in reads:
            if b.w is not None:
                add(b.w[0], b.w[1], True)
        for b in writes:
            if b.w is not None:
                if b.w[0] == skip_key:
                    for k, v in b.gd:
                        add(k, v, False)
                else:
                    add(b.w[0], b.w[1], False)
            for k, v in b.r.items():
                add(k, v, False)
        for k, v in need.items():
            self.q[e].append(("w", k, v))
            seen[k] = v

    def op(self, e, fn, reads=(), writes=()):
        self._need(e, reads, writes)
        self.val[e] += 1
        v = self.val[e]
        self.q[e].append(("i", fn))
        for b in reads:
            b.r[e] = v
        for b in writes:
            b.w = (e, v)
            b.r = {}
        self.ninst += 1

    def dma_key(self, buf):
        if buf.dk is None:
            buf.dk = "d_" + buf.name
            if buf.dk not in self.sems:
                self.sems[buf.dk] = self.nc.alloc_semaphore(buf.dk)
                self.val[buf.dk] = 0
        return buf.dk

    def dma(self, qe, out, in_, own, reads=(), writes=(), **kw):
        dk = self.dma_key(own)
        self._need(qe, reads, writes, skip_key=dk)
        self.val[dk] += 16
        v = self.val[dk]
        self.q[qe].append(("d", out, in_, dk, kw))
        for b in reads:
            b.r[dk] = v
        for b in writes:
            if b.w is None or b.w[0] != dk or b.r:
                b.gd = tuple(([b.w] if b.w is not None else []) + list(b.r.items()))
            b.w = (dk, v)
            b.r = {}
        self.ninst += 1

    def flush(self):
        for e in ENG:
            for k, v in self.val.items():
                if k != e and v > 0 and self.seen[e].get(k, 0) < v:
                    self.q[e].append(("w", k, v))
                    self.seen[e][k] = v
        nc = self.nc
        sems = self.sems
        with nc.Block() as blk:
            def mk(e):
                items = self.q[e]
                own = sems[e]

                def body(eng):
                    for it in items:
                        if it[0] == "w":
                            eng.wait_ge(sems[it[1]], it[2])
                        elif it[0] == "i":
                            it[1](eng).then_inc(own, 1)
                        else:
                            eng.dma_start(out=it[1], in_=it[2], **it[4]).then_inc(sems[it[3]], 16)
                return body
            blk.tensor(mk("pe"))
            blk.scalar(mk("act"))
            blk.vector(mk("dve"))
            blk.gpsimd(mk("pool"))
            blk.sync(mk("sp"))
        self.q = {e: [] for e in ENG}


PP_L = 142


def pp_off(l, name):
    offs = {"gn": 0, "bada": 24, "cw": 96, "cb": 112, "br": 116, "bi": 120, "lam": 124, "gmix": 128, "gqk": 136, "bf": 138}
    return l * PP_L + offs[name]


def pack_pp(g_norm, b_ada, conv_w, conv_b, b_rgate, b_igate, lru_lambda, g_mix_out, g_qk, b_fgate):
    pp = np.zeros((128, 2 * PP_L), np.float32)

    def fm(v):
        return np.ascontiguousarray(v.reshape(-1, 128).T)

    for l in range(2):
        o = l * PP_L
        for j in range(3):
            pp[:, o + j * 8:o + j * 8 + 8] = fm(g_norm[l, j])
        pp[:, o + 24:o + 96] = fm(b_ada[l])
        for k in range(4):
            pp[:, o + 96 + k * 4:o + 96 + k * 4 + 4] = fm(conv_w[l, k])
        pp[:, o + 112:o + 116] = fm(conv_b[l])
        pp[:, o + 116:o + 120] = fm(b_rgate[l])
        pp[:, o + 120:o + 124] = fm(b_igate[l])
        pp[:, o + 124:o + 128] = fm(lru_lambda[l])
        pp[:, o + 128:o + 136] = fm(g_mix_out[l])
        for i in range(2):
            pp[:, o + 136 + i] = np.concatenate([g_qk[l, i], g_qk[l, i]])
        pp[:, o + 138:o + 142] = b_fgate[l][None, :]
    return pp


class MK:
    def __init__(self, nseq, S, layers=(0, 1), dbg=None, phases=("ffn", "lru", "sb", "fox", "outp")):
        self.nseq, self.S = nseq, S
        self.T = T = nseq * S
        self.dbg = dbg
        nc = self.nc = bass.Bass("TRN2", target_bir_lowering=False)
        self.P = Prog(nc)
        di = lambda n, s: nc.dram_tensor(n, s, F32, kind="ExternalInput").ap()
        self.x_d = di("x", [T, D])
        self.cT_d = di("cT", [128, NCH * nseq])
        self.pp_d = di("pp", [128, 2 * PP_L])
        self.w_ada_d = di("w_ada", [2, D, 9 * D])
        self.w_up_d = di("w_ffn_up", [2, 2, D, 2 * DFF])
        self.w_dn_d = di("w_ffn_down", [2, 2, DFF, D])
        self.w_in_d = di("w_in", [2, D, NIN])
        self.w_rg_d = di("w_rgate", [2, 8, 64, 64])
        self.w_ig_d = di("w_igate", [2, 8, 64, 64])
        self.w_out_d = di("w_out", [2, D, D])
        self.out_d = nc.dram_tensor("out", [T, D], F32, kind="ExternalOutput").ap()
        self.xT_d = nc.dram_tensor("xT_scr", [NCH, 128, T], F32, kind="Internal").ap().rearrange("c p t -> p c t")
        self.yT_d = nc.dram_tensor("yT_scr", [NCH, 128, T], BF16, kind="Internal").ap().rearrange("c p t -> p c t")
        self.hT_d = nc.dram_tensor("hT_scr", [NCH, 128, T], BF16, kind="Internal").ap().rearrange("c p t -> p c t")
        self.uid = 0
        self.consts()
        for l in layers:
            if l == layers[0]:
                with ExitStack() as esw:
                    w = self.ffn_weights(esw, l, 0) if "ffn" in phases else None
                    self.ada_phase()
                    self.ffn_phase(l, 0, first=True, last=False, w=w)
            else:
                self.ffn_phase(l, 0, first=False, last=False)
            for s in range(nseq):
                if "lru" in phases:
                    self.lru_phase(l, s)
                if "sb" in phases:
                    self.sb_phase(l, s)
                if "fox" in phases:
                    self.fox_phase(l, s)
            with ExitStack() as esw:
                w = None
                if "outp" in phases:
                    w = self.ffn_weights(esw, l, 1, load=False)
                    with ExitStack() as eso:
                        self.outp_wo(eso, l)
                        self.ffn_weights_load(w, l, 1)
                        self.outp_phase(l)
                self.ffn_phase(l, 1, first=False, last=(l == layers[-1]), w=w)

    def nm(self, s):
        self.uid += 1
        return f"{s}_{self.uid}"

    def mm(self, out, lhsT, rhs, start, stop, reads, writes):
        self.P.op("pe", lambda e: e.matmul(out, lhsT, rhs, start=start, stop=stop), reads, writes)

    def tr(self, out, in_, ident, reads, writes):
        self.P.op("pe", lambda e: e.transpose(out, in_, ident), reads, writes)

    def act(self, out, in_, func, reads, writes, **kw):
        self.P.op("act", lambda e: e.activation(out=out, in_=in_, func=func, **kw), reads, writes)

    def tt(self, eng, out, in0, in1, op, reads, writes):
        self.P.op(eng, lambda e: e.tensor_tensor(out=out, in0=in0, in1=in1, op=op), reads, writes)

    def stt(self, out, in0, scalar, in1, op0, op1, reads, writes):
        self.P.op("dve", lambda e: e.scalar_tensor_tensor(out=out, in0=in0, scalar=scalar, in1=in1, op0=op0, op1=op1),
                  reads, writes)

    def ts(self, eng, out, in0, s1, s2, op0, op1, reads, writes):
        self.P.op(eng, lambda e: e.tensor_scalar(out=out, in0=in0, scalar1=s1, scalar2=s2, op0=op0, op1=op1), reads, writes)

    def cp(self, eng, out, in_, reads, writes):
        if eng == "act":
            self.act(out, in_, AF.Identity, reads, writes)
        else:
            self.P.op(eng, lambda e: e.tensor_copy(out=out, in_=in_), reads, writes)

    def memset(self, eng, ap, val, writes):
        self.P.op(eng, lambda e: e.memset(ap, val), (), writes)

    def recip(self, out, in_, reads, writes):
        self.P.op("dve", lambda e: e.reciprocal(out=out, in_=in_), reads, writes)

    def consts(self):
        nc, P = self.nc, self.P
        al = lambda n, s, d: nc.alloc_sbuf_tensor("c_" + n, s, d)
        self.pp = al("pp", [128, 2 * PP_L], F32)
        self.b_pp = Buf("pp")
        P.dma("sp", self.pp[:], self.pp_d, self.b_pp, writes=[self.b_pp])
        self.identf = al("identf", [128, 128], F32)
        self.ones16 = al("ones16", [128, 128], BF16)
        self.onesf = al("onesf", [128, 128], F32)
        self.negones16 = al("negones16", [128, 128], BF16)
        self.bd16 = al("bd16", [128, 128], BF16)
        self.negU16 = al("negU16", [128, 128], BF16)
        self.Uf = al("Uf", [128, 128], F32)
        self.selL = al("selL", [128, 128], F32)
        self.mask_lt = al("mask_lt", [128, 4, 512], BF16)
        self.mask_le = al("mask_le", [128, 4, 512], BF16)
        self.b_c = Buf("consts")
        bc = [self.b_c]

        def sel(t, pattern, cmp, base, cm):
            P.op("pool", lambda g: g.affine_select(out=t, in_=t, pattern=pattern, compare_op=cmp, fill=0.0,
                                                   base=base, channel_multiplier=cm), bc, bc)
        self.memset("pool", self.identf[:], 1.0, bc)
        sel(self.identf[:], [[-1, 128]], ALU.is_equal, 0, 1)
        self.memset("pool", self.ones16[:], 1.0, bc)
        self.memset("pool", self.onesf[:], 1.0, bc)
        self.memset("pool", self.negones16[:], -1.0, bc)
        self.memset("pool", self.bd16[:], 0.0, bc)
        self.memset("pool", self.bd16[0:64, 0:64], 1.0, bc)
        self.memset("pool", self.bd16[64:128, 64:128], 1.0, bc)
        self.memset("pool", self.negU16[:], -1.0, bc)
        sel(self.negU16[:], [[-1, 128]], ALU.is_ge, 0, 1)
        self.memset("pool", self.Uf[:], 1.0, bc)
        sel(self.Uf[:], [[1, 128]], ALU.is_ge, 0, -1)
        self.memset("pool", self.selL[:], 1.0, bc)
        sel(self.selL[:], [[0, 128]], ALU.is_equal, -127, 1)
        self.memset("pool", self.mask_lt[:], 1.0, bc)
        self.memset("pool", self.mask_le[:], 1.0, bc)
        for o in range(4):
            sel(self.mask_lt[:, o, :], [[1, 512]], ALU.is_gt, -o * 128, -1)
            sel(self.mask_le[:, o, :], [[1, 512]], ALU.is_ge, -o * 128, -1)
        ns = self.nseq
        self.modT = al("modT", [128, 2, ns, 72], F32)
        self.Atab = al("Atab", [128, 2, ns, 24], F32)
        self.Gtab = al("Gtab", [128, 2, ns, 24], F32)
        self.cneg = al("cneg", [128, 2, 2, 4], F32)
        self.gq8 = al("gq8", [128, 2, 2], F32)
        self.b_mod = Buf("mod")

    def ppc(self, l, name, i=0, n=1):
        o = pp_off(l, name) + i
        return self.pp[:, o:o + n]

    def ada_phase(self):
        nc, P, ns = self.nc, self.P, self.nseq
        with ExitStack() as es:
            sb = lambda n, s, d: es.enter_context(nc.sbuf_tensor(self.nm(n), s, d))
            cT = sb("cT", [128, NCH * ns], F32)
            b_cT = Buf("cT")
            wb = [sb("wada", [128, NCH, 512], F32) for _ in range(2)]
            b_wb = [Buf("wada0"), Buf("wada1")]
            ps = es.enter_context(nc.psum_tensor(self.nm("adaps"), [128, 512], F32))
            b_ps = Buf("adaps")
            tmp = sb("adatmp", [128, 4], F32)
            P.dma("sp", cT[:], self.cT_d, b_cT, writes=[b_cT])
            self.act(cT[:], cT[:], AF.Silu, [b_cT], [b_cT])
            it = 0
            for l in range(2):
                for g in range(18):
                    w, bw = wb[it % 2], b_wb[it % 2]
                    it += 1
                    for k in range(NCH):
                        P.dma("sp" if k % 2 == 0 else "act", w[:, k, :],
                              self.w_ada_d[l, k * 128:(k + 1) * 128, g * 512:(g + 1) * 512], bw, writes=[bw])
                    for mm_ in range(4):
                        m = g * 4 + mm_
                        for k in range(NCH):
                            self.mm(ps[:, m * ns:(m + 1) * ns], w[:, k, mm_ * 128:(mm_ + 1) * 128],
                                    cT[:, k * ns:(k + 1) * ns], k == 0, k == NCH - 1, [bw, b_cT], [b_ps])
                bm = [self.b_mod]
                for s in range(ns):
                    self.tt("dve", self.modT[:, l, s, :], ps[:, s:72 * ns:ns], self.ppc(l, "bada", 0, 72), ALU.add,
                            [b_ps, self.b_pp], bm)
                    for j in range(3):
                        self.stt(self.Atab[:, l, s, j * 8:(j + 1) * 8], self.modT[:, l, s, j * 24 + 8:j * 24 + 16], 1.0,
                                 self.ppc(l, "gn", j * 8, 8), ALU.add, ALU.mult, bm + [self.b_pp], bm)
                        self.ts("dve", self.Gtab[:, l, s, j * 8:(j + 1) * 8], self.modT[:, l, s, j * 24 + 16:j * 24 + 24],
                                1.0, 1.0 if j == 1 else 0.5, ALU.add, ALU.mult, bm, bm)
                self.act(tmp[:], self.ppc(l, "lam", 0, 4), AF.Exp, [self.b_pp], bm, scale=-1.0)
                self.act(tmp[:], tmp[:], AF.Ln, bm, bm, bias=1.0)
                self.ts("dve", self.cneg[:, l, 0, :], tmp[:], -8.0, None, ALU.mult, ALU.bypass, bm, bm)
                self.ts("dve", self.cneg[:, l, 1, :], tmp[:], -16.0, None, ALU.mult, ALU.bypass, bm, bm)
                self.ts("dve", self.gq8[:, l, 0:1], self.ppc(l, "gqk", 0, 1), 0.125, None, ALU.mult, ALU.bypass,
                        [self.b_pp], bm)
                self.cp("dve", self.gq8[:, l, 1:2], self.ppc(l, "gqk", 1, 1), [self.b_pp], bm)
            if self.dbg == "ada":
                self.dbg_out("dbg_mod", self.modT[:].rearrange("p a b c -> p (a b c)"), [128, 2 * ns * 72], [self.b_mod])
            P.flush()

    def dbg_out(self, name, ap, shape, reads, dt=F32):
        d = self.nc.dram_tensor(name, shape, dt, kind="ExternalOutput").ap()
        b = Buf(self.nm("dbg"))
        self.P.dma("sp", d, ap, b, reads=reads)

    def norm_a(self, xt, b_xt, sq, b_sq):
        self.act(sq[:], xt[:], AF.Square, b_xt, [b_sq])

    def norm_b(self, xt, b_xt, h, b_h, sq, b_sq, ps, b_psb, rs, b_rs, tmp, b_tmp, l, j, s, NT):
        for c in range(NCH):
            self.mm(ps[:, 0:NT], self.ones16[:], sq[:, c, :], c == 0, c == NCH - 1, [b_sq, self.b_c], [b_psb])
        self.act(rs[:], ps[:, 0:NT], AF.Sqrt, [b_psb], [b_rs], scale=1.0 / D, bias=EPS)
        self.recip(rs[:], rs[:], [b_rs], [b_rs])
        for c in range(NCH):
            t, bt = tmp[c % 2], b_tmp[c % 2]
            self.stt(t[:], xt[:, c, :], self.Atab[:, l, s, j * 8 + c:j * 8 + c + 1], rs[:], ALU.mult, ALU.mult,
                     [b_xt[c], b_rs, self.b_mod], [bt])
            self.act(h[:, c, :], t[:], AF.Identity, [bt, self.b_mod], [b_h[c]],
                     bias=self.modT[:, l, s, j * 24 + c:j * 24 + c + 1])

    def norm_mod(self, xt, b_xt, h, b_h, sq, b_sq, ps, b_psb, rs, b_rs, tmp, b_tmp, l, j, s, NT):
        self.norm_a(xt, b_xt, sq, b_sq)
        self.norm_b(xt, b_xt, h, b_h, sq, b_sq, ps, b_psb, rs, b_rs, tmp, b_tmp, l, j, s, NT)

    def load_x(self, xt, b_xt, tok0, NT, first, xtok=None, b_xtok=None, pst=None, b_pst=None):
        P = self.P
        if not first:
            P.dma("sp", xt[:], self.xT_d[:, :, tok0:tok0 + NT], b_xt[0], writes=b_xt)
            return
        for g in range(NT // 128):
            xk, bk = xtok[g % 2], b_xtok[g % 2]
            P.dma("sp", xk[:], self.x_d[tok0 + g * 128:tok0 + (g + 1) * 128, :], bk, writes=[bk])
            for c in range(NCH):
                pt, bpt = pst[c % 2], b_pst[c % 2]
                self.tr(pt[:, 0:128], xk[:, c * 128:(c + 1) * 128], self.identf[:], [bk, self.b_c], [bpt])
                self.cp("act" if c % 2 == 0 else "dve", xt[:, c, g * 128:(g + 1) * 128], pt[:, 0:128], [bpt], [b_xt[c]])

    def store_x(self, xo, b_xo, tok0, NT, last, xtok=None, b_xtok=None, pst=None, b_pst=None):
        P = self.P
        if not last:
            P.dma("sp", self.xT_d[:, :, tok0:tok0 + NT], xo[:], b_xo[0], reads=b_xo)
            return
        for g in range(NT // 128):
            xk, bk = xtok[g % 2], b_xtok[g % 2]
            for hh in range(2):
                pt, bpt = pst[hh], b_pst[hh]
                for cc in range(4):
                    c = hh * 4 + cc
                    self.tr(pt[:, cc * 128:(cc + 1) * 128], xo[:, c, g * 128:(g + 1) * 128], self.identf[:],
                            [b_xo[c], self.b_c], [bpt])
                self.cp("act" if hh == 0 else "dve", xk[:, hh * 512:(hh + 1) * 512], pt[:], [bpt], [bk])
            P.dma("sp", self.out_d[tok0 + g * 128:tok0 + (g + 1) * 128, :], xk[:], bk, reads=[bk])

    def ffn_weights(self, es, l, f, load=True):
        nc = self.nc
        wup = es.enter_context(nc.sbuf_tensor(self.nm("wup"), [128, NCH, 2 * DFF], BF16))
        wdn = es.enter_context(nc.sbuf_tensor(self.nm("wdn"), [128, NFF, D], BF16))
        w = (wup, wdn, Buf("wup"), Buf("wdn"))
        if load:
            self.ffn_weights_load(w, l, f)
        return w

    def ffn_weights_load(self, w, l, f):
        wup, wdn, b_wup, b_wdn = w
        for k in range(NCH):
            self.P.dma("pool", wup[:, k, :], self.w_up_d[l, f, k * 128:(k + 1) * 128, :], b_wup, writes=[b_wup],
                       max_dma_last_dim=4096)
        for k in range(NFF):
            self.P.dma("pool", wdn[:, k, :], self.w_dn_d[l, f, k * 128:(k + 1) * 128, :], b_wdn, writes=[b_wdn],
                       max_dma_last_dim=4096)

    def ffn_phase(self, l, f, first, last, w=None):
        nc, P = self.nc, self.P
        NT = 256
        j = 0 if f == 0 else 2
        with ExitStack() as es:
            sb = lambda n, s, d: es.enter_context(nc.sbuf_tensor(self.nm(n), s, d))
            wup, wdn, b_wup, b_wdn = w if w is not None else self.ffn_weights(es, l, f)
            NS = 2
            xt = [sb("xt", [128, NCH, NT], F32) for _ in range(NS)]
            b_xt = [[Buf(f"xt{i}_{c}") for c in range(NCH)] for i in range(NS)]
            hh = [sb("h", [128, NCH, NT], BF16) for _ in range(2)]
            b_hh = [[Buf(f"h{i}_{c}") for c in range(NCH)] for i in range(2)]
            sqq = [sb("sq", [128, NCH, NT], BF16) for _ in range(2)]
            b_sqq = [Buf("sq0"), Buf("sq1")]
            rss = [sb("rs", [128, NT], F32) for _ in range(2)]
            b_rss = [Buf("rs0"), Buf("rs1")]
            tmp = [sb("tmp", [128, NT], F32) for _ in range(2)]
            b_tmp = [Buf("tmp0"), Buf("tmp1")]
            sg = [sb("sg", [128, NT], F32) for _ in range(2)]
            b_sg = [Buf("sg0"), Buf("sg1")]
            a = sb("a", [128, NFF, NT], BF16)
            b_a = [Buf(f"a{k}") for k in range(NFF)]
            ps = [es.enter_context(nc.psum_tensor(self.nm("ps"), [128, 512], F32)) for _ in range(8)]
            b_ps = [Buf(f"ps{i}") for i in range(8)]
            xtok = b_xtok = None
            if first or last:
                xtok = [sb("xtok", [128, D], F32) for _ in range(2)]
                b_xtok = [Buf("xtok0"), Buf("xtok1")]
            pst, b_pst = [ps[7], ps[0]], [b_ps[7], b_ps[0]]
            ntl = self.T // NT

            def prep_a(ti):
                self.load_x(xt[ti % NS], b_xt[ti % NS], ti * NT, NT, first, xtok, b_xtok, pst, b_pst)
                self.norm_a(xt[ti % NS], b_xt[ti % NS], sqq[ti % 2], b_sqq[ti % 2])

            def prep_b(ti):
                self.norm_b(xt[ti % NS], b_xt[ti % NS], hh[ti % 2], b_hh[ti % 2], sqq[ti % 2], b_sqq[ti % 2], ps[0], b_ps[0],
                            rss[ti % 2], b_rss[ti % 2], tmp, b_tmp, l, j, (ti * NT) // self.S, NT)

            prep_a(0)
            prep_b(0)
            for ti in range(ntl):
                tok0 = ti * NT
                s = tok0 // self.S
                x, bx = xt[ti % NS], b_xt[ti % NS]
                h, b_h = hh[ti % 2], b_hh[ti % 2]
                for jj in range(NFF):
                    pg, pu = ps[1 + 2 * (jj % 2)], ps[2 + 2 * (jj % 2)]
                    bg, bu = b_ps[1 + 2 * (jj % 2)], b_ps[2 + 2 * (jj % 2)]
                    for k in range(NCH):
                        self.mm(pg[:, 0:NT], wup[:, k, jj * 128:(jj + 1) * 128], h[:, k, :], k == 0, k == NCH - 1,
                                [b_wup, b_h[k]], [bg])
                    for k in range(NCH):
                        self.mm(pu[:, 0:NT], wup[:, k, DFF + jj * 128:DFF + (jj + 1) * 128], h[:, k, :], k == 0,
                                k == NCH - 1, [b_wup, b_h[k]], [bu])
                    self.act(sg[jj % 2][:], pg[:, 0:NT], AF.Silu, [bg], [b_sg[jj % 2]])
                    self.tt("dve", a[:, jj, :], sg[jj % 2][:], pu[:, 0:NT], ALU.mult, [b_sg[jj % 2], bu], [b_a[jj]])
                    if jj == 12 and ti + 1 < ntl:
                        prep_a(ti + 1)
                if ti + 1 < ntl:
                    prep_b(ti + 1)
                for c in range(NCH):
                    pd, bd = ps[5 + c % 2], b_ps[5 + c % 2]
                    for k in range(NFF):
                        self.mm(pd[:, 0:NT], wdn[:, k, c * 128:(c + 1) * 128], a[:, k, :], k == 0, k == NFF - 1,
                                [b_wdn, b_a[k]], [bd])
                    self.stt(x[:, c, :], pd[:, 0:NT], self.Gtab[:, l, s, j * 8 + c:j * 8 + c + 1], x[:, c, :],
                             ALU.mult, ALU.add, [bd, bx[c], self.b_mod], [bx[c]])
                self.store_x(x, bx, tok0, NT, last, xtok, b_xtok, pst, b_pst)
            P.flush()

    def tile_bufs(self, sb, NT, nslots=2, h_only=False):
        o = {}
        if h_only:
            o["h"] = [sb("h", [128, NCH, NT], BF16) for _ in range(2)]
            o["b_h"] = [[Buf(f"h{i}_{c}") for c in range(NCH)] for i in range(2)]
            return o
        o["xt"] = [sb("xt", [128, NCH, NT], F32) for _ in range(nslots)]
        o["b_xt"] = [[Buf(f"xt{i}_{c}") for c in range(NCH)] for i in range(nslots)]
        o["h"] = [sb("h", [128, NCH, NT], BF16) for _ in range(2)]
        o["b_h"] = [[Buf(f"h{i}_{c}") for c in range(NCH)] for i in range(2)]
        o["sq"] = [sb("sq", [128, NCH, NT], BF16) for _ in range(2)]
        o["b_sq"] = [Buf("sq0"), Buf("sq1")]
        o["rs"] = [sb("rs", [128, NT], F32) for _ in range(2)]
        o["b_rs"] = [Buf("rs0"), Buf("rs1")]
        o["tmp"] = [sb("tmp", [128, NT], F32) for _ in range(2)]
        o["b_tmp"] = [Buf("tmp0"), Buf("tmp1")]
        return o

    def tile_h(self, o, ti, tok0, NT, ps0, b_ps0, l, s):
        x, bx = o["xt"][ti % 2], o["b_xt"][ti % 2]
        self.load_x(x, bx, tok0, NT, False)
        i = ti % 2
        self.norm_mod(x, bx, o["h"][i], o["b_h"][i], o["sq"][i], o["b_sq"][i], ps0, b_ps0, o["rs"][i], o["b_rs"][i],
                      o["tmp"], o["b_tmp"], l, 1, s, NT)
        return o["h"][i], o["b_h"][i]

    def tile_a(self, o, ti, tok0, NT):
        x, bx = o["xt"][ti % 2], o["b_xt"][ti % 2]
        self.load_x(x, bx, tok0, NT, False)
        self.norm_a(x, bx, o["sq"][ti % 2], o["b_sq"][ti % 2])

    def tile_b(self, o, ti, NT, ps0, b_ps0, l, s):
        i = ti % 2
        self.norm_b(o["xt"][i], o["b_xt"][i], o["h"][i], o["b_h"][i], o["sq"][i], o["b_sq"][i], ps0, b_ps0, o["rs"][i],
                    o["b_rs"][i], o["tmp"], o["b_tmp"], l, 1, s, NT)

    def store_h(self, o, ti, tok0, NT):
        i = ti % 2
        self.P.dma("act", self.hT_d[:, :, tok0:tok0 + NT], o["h"][i][:], o["b_h"][i][0], reads=o["b_h"][i])

    def load_h(self, o, ti, tok0, NT):
        i = ti % 2
        self.P.dma("sp", o["h"][i][:], self.hT_d[:, :, tok0:tok0 + NT], o["b_h"][i][0], writes=o["b_h"][i])

    def load_w_in(self, w, b_w, l, c0, c1):
        for k in range(NCH):
            self.P.dma("pool", w[:, k, :], self.w_in_d[l, k * 128:(k + 1) * 128, c0:c1], b_w, writes=[b_w],
                       max_dma_last_dim=4096)

    def group_rms_store(self, y, b_y, nchk, gm0, l, tok0, NT, sqy, b_sqy, ps, b_ps, rs2, b_rs2, yn, b_yn, ych0):
        self.act(sqy[:], y[:], AF.Square, b_y, [b_sqy])
        for c in range(nchk):
            self.mm(ps[:, 0:NT], self.ones16[:], sqy[:, c, :], c == 0, c == nchk - 1, [b_sqy, self.b_c], [b_ps])
        self.act(rs2[:], ps[:, 0:NT], AF.Sqrt, [b_ps], [b_rs2], scale=1.0 / (nchk * 128), bias=EPS)
        self.recip(rs2[:], rs2[:], [b_rs2], [b_rs2])
        for c in range(nchk):
            self.stt(yn[:, c, :], y[:, c, :], self.ppc(l, "gmix", gm0 + c, 1), rs2[:], ALU.mult, ALU.mult,
                     [b_y[c], b_rs2, self.b_pp], [b_yn])
        self.P.dma("sp", self.yT_d[:, ych0:ych0 + nchk, tok0:tok0 + NT], yn[:], b_yn, reads=[b_yn])

    def lru_phase(self, l, s):
        nc, P = self.nc, self.P
        NT = 512
        with ExitStack() as es:
            sb = lambda n, sh, d: es.enter_context(nc.sbuf_tensor(self.nm(n), sh, d))
            win = sb("win", [128, NCH, 1024], BF16)
            b_win = Buf("win")
            self.load_w_in(win, b_win, l, 0, 1024)
            wg = [sb("wg", [128, 4, 128], F32) for _ in range(2)]
            b_wg = [Buf("wg0"), Buf("wg1")]
            for gi, wd in enumerate((self.w_rg_d, self.w_ig_d)):
                self.memset("pool", wg[gi][:], 0.0, [b_wg[gi]])
                src = wd[l].rearrange("(c two) i j -> two i c j", two=2)
                P.dma("sp", wg[gi][0:64, :, 0:64], src[0], b_wg[gi], writes=[b_wg[gi]])
                P.dma("sp", wg[gi][64:128, :, 64:128], src[1], b_wg[gi], writes=[b_wg[gi]])
            o = self.tile_bufs(sb, NT)
            ps = [es.enter_context(nc.psum_tensor(self.nm("ps"), [128, 512], F32)) for _ in range(8)]
            b_ps = [Buf(f"ps{i}") for i in range(8)]
            lx = [sb("lx", [128, NT + 3], F32) for _ in range(4)]
            b_lx = [Buf(f"lx{c}") for c in range(4)]
            carry = sb("carry", [128, 4], F32)
            b_carry = [Buf(f"carry{c}") for c in range(4)]
            y = sb("y", [128, 4, NT], F32)
            b_y = [Buf(f"y{c}") for c in range(4)]
            f32t = lambda n: (sb(n, [128, NT], F32), Buf(n))
            f32q = lambda n: [f32t(n + str(i)) for i in range(4)]
            gs_, u_, r_, ig_, a_, m_ = (f32q(n) for n in ("gs", "u", "r", "ig", "a", "m"))
            hs_ = r_
            t1_2 = [f32q("t1a"), f32q("t1b")]
            y2 = [y, sb("y2", [128, 4, NT], F32)]
            b_y2 = [b_y, [Buf(f"y2{c}") for c in range(4)]]
            sqy = sb("sqy", [128, 4, NT], BF16)
            b_sqy = Buf("sqy")
            rs2, b_rs2 = f32t("rs2")
            yn = sb("yn", [128, 4, NT], BF16)
            b_yn = Buf("yn")
            cw = lambda k, c: self.ppc(l, "cw", k * 4 + c, 1)
            C4 = range(4)
            ntl = self.S // NT
            S0 = s * self.S

            def head(ti):
                h, b_h = o["h"][ti % 2], o["b_h"][ti % 2]
                t1_ = t1_2[ti % 2]
                for c in C4:
                    pl, bl = ps[1 + c % 2], b_ps[1 + c % 2]
                    for k in range(NCH):
                        self.mm(pl[:], win[:, k, c * 128:(c + 1) * 128], h[:, k, :], k == 0, k == NCH - 1,
                                [b_win, b_h[k]], [bl])
                    if ti == 0:
                        self.memset("pool", lx[c][:, 0:3], 0.0, [b_lx[c]])
                    else:
                        self.cp("pool", lx[c][:, 0:3], lx[c][:, NT:NT + 3], [b_lx[c]], [b_lx[c]])
                    self.cp("act", lx[c][:, 3:NT + 3], pl[:], [bl], [b_lx[c]])
                if ti + 1 < ntl:
                    self.tile_a(o, ti + 1, S0 + (ti + 1) * NT, NT)
                for c in C4:
                    pg, bg = ps[3 + c % 2], b_ps[3 + c % 2]
                    for k in range(NCH):
                        self.mm(pg[:], win[:, k, 512 + c * 128:512 + (c + 1) * 128], h[:, k, :], k == 0, k == NCH - 1,
                                [b_win, b_h[k]], [bg])
                    self.cp("act", gs_[c][0][:], pg[:], [bg], [gs_[c][1]])
                    self.act(t1_[c][0][:], pg[:], AF.Square, [bg], [t1_[c][1]])
                if ti + 1 < ntl:
                    self.tile_b(o, ti + 1, NT, ps[0], b_ps[0], l, s)
                    self.store_h(o, ti + 1, S0 + (ti + 1) * NT, NT)
                for c in C4:
                    u, b_u = u_[c]
                    self.act(u[:], lx[c][:, 0:NT], AF.Identity, [b_lx[c], self.b_pp], [b_u], scale=cw(0, c),
                             bias=self.ppc(l, "cb", c, 1))
                    for k in range(1, 4):
                        self.stt(u[:], lx[c][:, k:k + NT], cw(k, c), u[:], ALU.mult, ALU.add, [b_lx[c], b_u, self.b_pp], [b_u])

            def st_a(ti):
                t1_ = t1_2[ti % 2]
                for c in C4:
                    u, b_u = u_[c]
                    pr, br_ = ps[1 + c], b_ps[1 + c]
                    pi, bi_ = ps[5 + c % 3], b_ps[5 + c % 3]
                    self.mm(pr[:], wg[0][:, c, :], u[:], True, True, [b_wg[0], b_u], [br_])
                    self.mm(pi[:], wg[1][:, c, :], u[:], True, True, [b_wg[1], b_u], [bi_])
                    self.act(r_[c][0][:], pr[:], AF.Sigmoid, [br_, self.b_pp], [r_[c][1]], bias=self.ppc(l, "br", c, 1))
                    self.act(ig_[c][0][:], pi[:], AF.Sigmoid, [bi_, self.b_pp], [ig_[c][1]], bias=self.ppc(l, "bi", c, 1))
                for c in C4:
                    t1, b_t1 = t1_[c]
                    self.ts("dve", t1[:], t1[:], 0.044715, 1.0, ALU.mult, ALU.add, [b_t1], [b_t1])
                    self.tt("pool", t1[:], t1[:], gs_[c][0][:], ALU.mult, [b_t1, gs_[c][1]], [b_t1])
                for c in C4:
                    t1, b_t1 = t1_[c]
                    self.act(t1[:], t1[:], AF.Sigmoid, [b_t1], [b_t1], scale=1.5957691216057308)
                for c in C4:
                    t1, b_t1 = t1_[c]
                    self.tt("pool", t1[:], t1[:], gs_[c][0][:], ALU.mult, [b_t1, gs_[c][1]], [b_t1])
                for c in C4:
                    r, b_r = r_[c]
                    self.act(a_[c][0][:], r[:], AF.Exp, [b_r, self.b_mod], [a_[c][1]], scale=self.cneg[:, l, 0, c:c + 1])
                    self.act(m_[c][0][:], r[:], AF.Exp, [b_r, self.b_mod], [m_[c][1]], scale=self.cneg[:, l, 1, c:c + 1])
                    ig, b_ig = ig_[c]
                    self.tt("pool", ig[:], ig[:], u_[c][0][:], ALU.mult, [b_ig, u_[c][1]], [b_ig])

            def st_d(ti):
                t1_ = t1_2[ti % 2]
                yy, b_yy = y2[ti % 2], b_y2[ti % 2]
                for c in C4:
                    self.act(m_[c][0][:], m_[c][0][:], AF.Sqrt, [m_[c][1]], [m_[c][1]], scale=-1.0, bias=1.0)
                    ig, b_ig = ig_[c]
                    self.tt("dve", ig[:], ig[:], m_[c][0][:], ALU.mult, [b_ig, m_[c][1]], [b_ig])
                for c in C4:
                    hs, b_hs = hs_[c]
                    init = 0.0 if ti == 0 else carry[:, c:c + 1]
                    self.P.op("dve", lambda e, init=init, hs=hs, a=a_[c][0], ig=ig_[c][0]: e.tensor_tensor_scan(
                        out=hs[:], data0=a[:], data1=ig[:], initial=init, op0=ALU.mult, op1=ALU.add),
                              [a_[c][1], ig_[c][1], b_carry[c]], [b_hs])
                    self.cp("pool", carry[:, c:c + 1], hs[:, NT - 1:NT], [b_hs], [b_carry[c]])
                    self.tt("pool", yy[:, c, :], hs[:], t1_[c][0][:], ALU.mult, [b_hs, t1_[c][1]], [b_yy[c]])

            def st_b(ti):
                self.group_rms_store(y2[ti % 2], b_y2[ti % 2], 4, 0, l, S0 + ti * NT, NT, sqy, b_sqy, ps[0], b_ps[0],
                                     rs2, b_rs2, yn, b_yn, 0)

            self.tile_a(o, 0, S0, NT)
            self.tile_b(o, 0, NT, ps[0], b_ps[0], l, s)
            self.store_h(o, 0, S0, NT)
            head(0)
            for ti in range(ntl):
                st_a(ti)
                if ti > 0:
                    st_b(ti - 1)
                if ti + 1 < ntl:
                    head(ti + 1)
                st_d(ti)
            st_b(ntl - 1)
            P.flush()

    def sb_phase(self, l, s):
        nc, P, S = self.nc, self.P, self.S
        NT = 512
        NKT = S // 128
        with ExitStack() as es:
            sb = lambda n, sh, d: es.enter_context(nc.sbuf_tensor(self.nm(n), sh, d))
            win = sb("win", [128, NCH, 768], BF16)
            b_win = Buf("win")
            self.load_w_in(win, b_win, l, 1024, 1792)
            o = self.tile_bufs(sb, NT, h_only=True)
            pst = lambda w: es.enter_context(nc.psum_tensor(self.nm("ps"), [128, w], F32))
            p0 = pst(512)
            pA2 = pst(1024)
            pB2 = pst(1024)
            ps = [p0, pA2[:, 0:512], pA2[:, 512:1024], pB2[:, 0:512], pB2[:, 512:1024], pst(512), pst(512), pst(512)]
            b_ps = [Buf(f"ps{i}") for i in range(8)]
            qh = sb("qh", [128, 4, S], BF16)
            kT = sb("kT", [128, 2, S], BF16)
            v = sb("v", [128, NKT, 256], BF16)
            nsp = S // 512
            b_q = [[Buf(f"q{c}_{i}") for i in range(nsp)] for c in range(2)]
            b_k = [[Buf(f"k{c}_{i}") for i in range(nsp)] for c in range(2)]
            b_v = [Buf(f"v{i}") for i in range(NKT)]
            for hq in range(4):
                self.memset("pool", qh[:, hq, :], 0.0, [b for bb in b_q for b in bb])
            ntl = S // NT
            self.load_h(o, 0, s * S, NT)
            for ti in range(ntl):
                tok0 = s * S + ti * NT
                h, b_h = o["h"][ti % 2], o["b_h"][ti % 2]
                if ti + 1 < ntl:
                    self.load_h(o, ti + 1, tok0 + NT, NT)
                for c in range(2):
                    pq, bq = ps[1 + c], b_ps[1 + c]
                    for k in range(NCH):
                        self.mm(pq[:], win[:, k, c * 128:(c + 1) * 128], h[:, k, :], k == 0, k == NCH - 1, [b_win, b_h[k]], [bq])
                    self.act(qh[0:64, 2 * c, ti * NT:(ti + 1) * NT], pq[0:64, :], AF.Identity, [bq], [b_q[c][ti]], scale=0.125)
                    self.act(qh[64:128, 2 * c + 1, ti * NT:(ti + 1) * NT], pq[64:128, :], AF.Identity, [bq], [b_q[c][ti]],
                             scale=0.125)
                    pk, bk = ps[3 + c], b_ps[3 + c]
                    for k in range(NCH):
                        self.mm(pk[:], win[:, k, 256 + c * 128:256 + (c + 1) * 128], h[:, k, :], k == 0, k == NCH - 1,
                                [b_win, b_h[k]], [bk])
                    self.cp("dve", kT[:, c, ti * NT:(ti + 1) * NT], pk[:], [bk], [b_k[c][ti]])
                for g in range(4):
                    pv, bv = ps[5 + g % 2], b_ps[5 + g % 2]
                    for k in range(NCH):
                        self.mm(pv[:, 0:256], h[:, k, g * 128:(g + 1) * 128], win[:, k, 512:768], k == 0, k == NCH - 1,
                                [b_win, b_h[k]], [bv])
                    self.cp("act" if g % 2 == 0 else "dve", v[:, ti * 4 + g, :], pv[:, 0:256], [bv], [b_v[ti * 4 + g]])
            f32w = lambda n, w: (sb(n, [128, w], F32), Buf(n))
            R = [f32w("R0", 512), f32w("R1", 512)]
            E = [f32w("E0", 1024), f32w("E1", 1024)]
            Tt = [f32w("T0", 1024), f32w("T1", 1024)]
            SP = [(sb(f"sp{i}", [128, 1024], BF16), Buf(f"sp{i}")) for i in range(3)]
            W = [(sb(f"w{i}", [128, 1024], BF16), Buf(f"w{i}")) for i in range(3)]
            ysb = sb("ysb", [128, 2, 512], F32)
            b_ysb = [Buf("ysb0"), Buf("ysb1")]
            sqy = sb("sqy", [128, 2, 512], BF16)
            b_sqy = Buf("sqy")
            rs2, b_rs2 = f32w("rs2", 512)
            yn = sb("yn", [128, 2, 512], BF16)
            b_yn = Buf("yn")
            bA2 = [b_ps[1], b_ps[2]]
            bB2 = [b_ps[3], b_ps[4]]
            pC, bC = ps[5], b_ps[5]
            pairs = []
            grp = 0
            for qs in range(nsp):
                njt = 4 * qs + 4
                for hd in range(4):
                    for i2 in range(njt // 2):
                        pairs.append((qs, hd, i2, njt - 1 - 2 * i2, njt, grp))
                    grp += 1
            NP = len(pairs)
            H = (slice(0, 512), slice(512, 1024))

            def geo(i):
                qs, hd, i2, jtA, njt, g = pairs[i]
                jts = (jtA, jtA - 1)
                return qs, hd, i2, jts, njt, g, hd // 2, (hd % 2) * 64, [jt >= 4 * qs for jt in jts], [jt - 4 * qs for jt in jts]

            def s1(i):
                qs, hd, i2, jts, njt, g, c, po, diag, od = geo(i)
                for hf in range(2):
                    self.mm(pA2[:, H[hf]], kT[:, c, jts[hf] * 128:(jts[hf] + 1) * 128], qh[:, hd, qs * 512:(qs + 1) * 512],
                            True, True, [b_q[c][qs], b_k[c][jts[hf] // 4]], [bA2[hf]])

            def s2(i):
                qs, hd, i2, jts, njt, g, c, po, diag, od = geo(i)
                Eb, bE = E[i % 2]
                sp, bsp = SP[i % 3]
                self.act(Eb[:], pA2[:], AF.Exp, bA2, [bE])
                self.act(sp[:], Eb[:], AF.Ln, [bE], [bsp], bias=1.0)
                for hf in range(2):
                    if diag[hf]:
                        self.tt("pool", sp[:, H[hf]], sp[:, H[hf]], self.mask_lt[:, od[hf], :], ALU.mult, [bsp, self.b_c], [bsp])

            def s3(i):
                qs, hd, i2, jts, njt, g, c, po, diag, od = geo(i)
                sp, bsp = SP[i % 3]
                for hf in range(2):
                    self.mm(pB2[:, H[hf]], kT[:, c, jts[hf] * 128:(jts[hf] + 1) * 128], qh[:, hd, qs * 512:(qs + 1) * 512],
                            True, False, [b_q[c][qs], b_k[c][jts[hf] // 4]], [bB2[hf]])
                    self.mm(pB2[:, H[hf]], self.negU16[:], sp[:, H[hf]], False, hf == 0, [bsp, self.b_c], [bB2[hf]])
                self.mm(pB2[:, H[1]], self.negones16[:], sp[:, H[0]], False, True, [bsp, self.b_c], [bB2[1]])
                if 2 * i2 + 2 < njt:
                    self.mm(pC[:], self.ones16[:], sp[:, H[0]], True, False, [bsp, self.b_c], [bC])
                    self.mm(pC[:], self.ones16[:], sp[:, H[1]], False, True, [bsp, self.b_c], [bC])

            def s4(i):
                qs, hd, i2, jts, njt, g, c, po, diag, od = geo(i)
                Rb, bR = R[g % 2]
                Tb, bT = Tt[i % 2]
                w, bw = W[i % 3]
                if i2 == 0:
                    self.act(w[:], pB2[:], AF.Exp, bB2, [bw])
                else:
                    for hf in range(2):
                        self.tt("dve", Tb[:, H[hf]], pB2[:, H[hf]], Rb[:], ALU.subtract, [bB2[hf], bR], [bT])
                    self.act(w[:], Tb[:], AF.Exp, [bT], [bw])
                for hf in range(2):
                    if diag[hf]:
                        self.tt("pool", w[:, H[hf]], w[:, H[hf]], self.mask_lt[:, od[hf], :], ALU.mult, [bw, self.b_c], [bw])
                if 2 * i2 + 2 < njt:
                    if i2 == 0:
                        self.cp("dve", Rb[:], pC[:], [bC], [bR])
                    else:
                        self.tt("dve", Rb[:], pC[:], Rb[:], ALU.add, [bC, bR], [bR])

            def s5(i):
                qs, hd, i2, jts, njt, g, c, po, diag, od = geo(i)
                w, bw = W[i % 3]
                pO, bO = ps[6 + g % 2], b_ps[6 + g % 2]
                last = 2 * i2 + 2 == njt
                for hf in range(2):
                    self.mm(pO[:], v[:, jts[hf], c * 128:(c + 1) * 128], w[:, H[hf]], i2 == 0 and hf == 0, last and hf == 1,
                            [b_v[jts[hf]], bw], [bO])
                if last:
                    self.cp("act", ysb[po:po + 64, c, :], pO[po:po + 64, :], [bO], [b_ysb[c]])
                    if hd == 3:
                        self.group_rms_store(ysb, b_ysb, 2, 4, l, s * S + qs * 512, 512, sqy, b_sqy, ps[0], b_ps[0],
                                             rs2, b_rs2, yn, b_yn, 4)

            for n in range(NP + 3):
                if n < NP:
                    s1(n)
                    s2(n)
                if 0 <= n - 2 < NP:
                    s4(n - 2)
                if 0 <= n - 1 < NP:
                    s3(n - 1)
                if 0 <= n - 3 < NP:
                    s5(n - 3)
            P.flush()

    def fox_phase(self, l, s):
        nc, P, S = self.nc, self.P, self.S
        NT = 512
        NKT = S // 128
        with ExitStack() as es:
            sb = lambda n, sh, d: es.enter_context(nc.sbuf_tensor(self.nm(n), sh, d))
            win = sb("win", [128, NCH, 772], BF16)
            b_win = Buf("win")
            self.load_w_in(win, b_win, l, 1792, 2564)
            o = self.tile_bufs(sb, NT, h_only=True)
            ps = [es.enter_context(nc.psum_tensor(self.nm("ps"), [128, 512], F32)) for _ in range(8)]
            b_ps = [Buf(f"ps{i}") for i in range(8)]
            qh = sb("qh", [128, 4, S], BF16)
            kT = sb("kT", [128, 2, S], BF16)
            va = sb("va", [128, NKT, 4, 128], BF16)
            Fn = sb("Fn", [128, NKT, 4], F32)
            Fc = sb("Fc", [128, NKT, 4], F32)
            nsp = S // 512
            b_q = [[Buf(f"q{c}_{i}") for i in range(nsp)] for c in range(2)]
            b_k = [[Buf(f"k{c}_{i}") for i in range(nsp)] for c in range(2)]
            b_v = [Buf(f"v{i}") for i in range(NKT)]
            b_F = Buf("F")
            f32t = lambda n: (sb(n, [128, 512], F32), Buf(n))
            sq16 = [(sb(f"sq16{i}", [128, 512], BF16), Buf(f"sq16{i}")) for i in range(4)]
            rsq = [f32t(f"rsq{i}") for i in range(4)]
            tfs = [(sb(f"tf{i}", [128, 4], F32), Buf(f"tf{i}")) for i in range(4)]
            for i in range(NKT):
                self.memset("pool", va[:, i, :, :], 1.0, [b_v[i]])
            for hq in range(4):
                self.memset("pool", qh[:, hq, :], 0.0, [b for bb in b_q for b in bb])
            ntl = S // NT
            self.load_h(o, 0, s * S, NT)
            for ti in range(ntl):
                tok0 = s * S + ti * NT
                h, b_h = o["h"][ti % 2], o["b_h"][ti % 2]
                if ti + 1 < ntl:
                    self.load_h(o, ti + 1, tok0 + NT, NT)
                combos = [(0, 0), (0, 1), (1, 0), (1, 1)]
                for i2, (qk, c) in enumerate(combos):
                    pq, bq = ps[1 + i2], b_ps[1 + i2]
                    for k in range(NCH):
                        self.mm(pq[:], win[:, k, qk * 256 + c * 128:qk * 256 + (c + 1) * 128], h[:, k, :], k == 0,
                                k == NCH - 1, [b_win, b_h[k]], [bq])
                for i2 in range(4):
                    self.act(sq16[i2][0][:], ps[1 + i2][:], AF.Square, [b_ps[1 + i2]], [sq16[i2][1]])
                for i2 in range(4):
                    pss, bss = ps[5 + i2 % 2], b_ps[5 + i2 % 2]
                    self.mm(pss[:], self.bd16[:], sq16[i2][0][:], True, True, [sq16[i2][1], self.b_c], [bss])
                    self.act(rsq[i2][0][:], pss[:], AF.Sqrt, [bss], [rsq[i2][1]], scale=1.0 / 64, bias=EPS)
                for i2 in range(4):
                    self.recip(rsq[i2][0][:], rsq[i2][0][:], [rsq[i2][1]], [rsq[i2][1]])
                for i2, (qk, c) in enumerate(combos):
                    pq, bq = ps[1 + i2], b_ps[1 + i2]
                    rq_, brq = rsq[i2]
                    if qk == 1:
                        self.stt(kT[:, c, ti * NT:(ti + 1) * NT], pq[:], self.gq8[:, l, 1:2], rq_[:], ALU.mult, ALU.mult,
                                 [bq, brq, self.b_mod], [b_k[c][ti]])
                    else:
                        for hf in range(2):
                            pp_ = slice(hf * 64, hf * 64 + 64)
                            self.stt(qh[pp_, 2 * c + hf, ti * NT:(ti + 1) * NT], pq[pp_, :], self.gq8[pp_, l, 0:1], rq_[pp_, :],
                                     ALU.mult, ALU.mult, [bq, brq, self.b_mod], [b_q[c][ti]])
                for g in range(4):
                    tg = ti * 4 + g
                    tf, b_tf = tfs[g]
                    pv, bv = ps[5 + g % 2], b_ps[5 + g % 2]
                    for k in range(NCH):
                        self.mm(pv[:, 0:260], h[:, k, g * 128:(g + 1) * 128], win[:, k, 512:772], k == 0, k == NCH - 1,
                                [b_win, b_h[k]], [bv])
                    pv4 = pv[:, 0:256].rearrange("p (h d) -> p h d", h=4)
                    self.cp("act", va[:, tg, 0:4:2, 0:64], pv4[:, 0:4:2, :], [bv], [b_v[tg]])
                    self.cp("act", va[:, tg, 1:4:2, 64:128], pv4[:, 1:4:2, :], [bv], [b_v[tg]])
                    if CUT < 2:
                        continue
                    self.cp("act", tf[:], pv[:, 256:260], [bv], [b_tf])
                    self.tt("dve", tf[:], tf[:], self.ppc(l, "bf", 0, 4), ALU.add, [b_tf, self.b_pp], [b_tf])
                    self.act(tf[:], tf[:], AF.Exp, [b_tf], [b_tf], scale=-1.0)
                    self.act(tf[:], tf[:], AF.Ln, [b_tf], [b_tf], bias=1.0)
                    pF, bF_ = ps[7], b_ps[7]
                    self.mm(pF[:, 8 * g:8 * g + 4], self.Uf[:], tf[:], True, True, [b_tf, self.b_c], [bF_])
                    self.cp("act", Fn[:, tg, :], pF[:, 8 * g:8 * g + 4], [bF_], [b_F])
            pT, bT_ = ps[7], b_ps[7]
            self.mm(pT[:, 0:NKT * 4], self.selL[:], Fn[:].rearrange("p g h -> p (g h)"), True, True, [b_F, self.b_c], [bT_])
            Ft = sb("Ft", [128, NKT, 4], F32)
            b_Ft = Buf("Ft")
            self.cp("act", Ft[:].rearrange("p g h -> p (g h)"), pT[:, 0:NKT * 4], [bT_], [b_Ft])
            for hd in range(4):
                self.P.op("dve", lambda e, hd=hd: e.tensor_tensor_scan(out=Fc[:, :, hd], data0=self.onesf[:, 0:NKT],
                                                                      data1=Ft[:, :, hd], initial=0.0, op0=ALU.mult, op1=ALU.add),
                          [b_Ft, self.b_c], [b_F])
            if NKT > 1:
                self.tt("dve", Fn[:, 1:NKT, :], Fn[:, 1:NKT, :], Fc[:, 0:NKT - 1, :], ALU.add, [b_F], [b_F])
            Pw = [(sb(f"p{i}", [128, 512], BF16), Buf(f"p{i}")) for i in range(4)]
            bias_t = [(sb(f"bias{i}", [128, NKT], F32), Buf(f"bias{i}")) for i in range(2)]
            rd, b_rd = rsq[0]
            bsb, b_bsb = rsq[1]
            yfx = sb("yfx", [128, 2, 512], F32)
            b_yfx = [Buf("yfx0"), Buf("yfx1")]
            sqy = sb("sqy", [128, 2, 512], BF16)
            b_sqy = Buf("sqy")
            rs2, b_rs2 = f32t("rs2")
            yn = sb("yn", [128, 2, 512], BF16)
            b_yn = Buf("yn")
            pairs = []
            grp = 0
            for qs in range(nsp):
                njt = 4 * qs + 4
                for hd in range(4):
                    for jt in range(njt):
                        pairs.append((qs, hd, jt, njt, grp))
                    grp += 1
            NP = len(pairs)

            def geo(i):
                qs, hd, jt, njt, g = pairs[i]
                return qs, hd, jt, njt, g, hd // 2, (hd % 2) * 64, jt >= 4 * qs, jt - 4 * qs

            def f1(i):
                qs, hd, jt, njt, g, c, po, diag, od = geo(i)
                if jt == 0:
                    bt_, bbt = bias_t[g % 2]
                    self.ts("dve", bt_[:, 0:njt], Fn[:, 0:njt, hd], Fc[:, njt - 1, hd:hd + 1], None, ALU.subtract, ALU.bypass,
                            [b_F], [bbt])
                pA, bA = ps[1 + i % 4], b_ps[1 + i % 4]
                self.mm(pA[:], kT[:, c, jt * 128:(jt + 1) * 128], qh[:, hd, qs * 512:(qs + 1) * 512],
                        True, True, [b_q[c][qs], b_k[c][jt // 4]], [bA])

            def f2(i):
                qs, hd, jt, njt, g, c, po, diag, od = geo(i)
                bt_, bbt = bias_t[g % 2]
                pA, bA = ps[1 + i % 4], b_ps[1 + i % 4]
                p_, bp = Pw[i % 4]
                self.act(p_[:], pA[:], AF.Exp, [bA, bbt], [bp], bias=bt_[:, jt:jt + 1])
                if diag:
                    self.tt("dve", p_[:], p_[:], self.mask_le[:, od, :], ALU.mult, [bp, self.b_c], [bp])

            def f3(i):
                qs, hd, jt, njt, g, c, po, diag, od = geo(i)
                p_, bp = Pw[i % 4]
                pO, bO = ps[5 + g % 2], b_ps[5 + g % 2]
                self.mm(pO[:], va[:, jt, hd, :], p_[:], jt == 0, jt == njt - 1, [b_v[jt], bp], [bO])
                if jt == njt - 1:
                    def fin(pO=pO, bO=bO, po=po, c=c, hd=hd, qs=qs):
                        dn = slice(64 - po, 128 - po)
                        self.recip(rd[dn, :], pO[dn, :], [bO], [b_rd])
                        self.cp("dve", bsb[po:po + 64, :], rd[dn, :], [b_rd], [b_bsb])
                        self.tt("dve", yfx[po:po + 64, c, :], pO[po:po + 64, :], bsb[po:po + 64, :], ALU.mult, [bO, b_bsb],
                                [b_yfx[c]])
                        if hd == 3:
                            self.group_rms_store(yfx, b_yfx, 2, 6, l, s * S + qs * 512, 512, sqy, b_sqy, ps[0], b_ps[0],
                                                 rs2, b_rs2, yn, b_yn, 6)
                    pend.append((i + 3, fin))

            pend = []
            for n in range(NP + 3):
                if n < NP:
                    f1(n)
                if 0 <= n - 1 < NP:
                    f2(n - 1)
                if 0 <= n - 3 < NP:
                    f3(n - 3)
                while pend and pend[0][0] <= n - 3:
                    pend.pop(0)[1]()
            for _, fn in pend:
                fn()
            P.flush()

    def outp_wo(self, es, l):
        wo = es.enter_context(self.nc.sbuf_tensor(self.nm("wo"), [128, NCH, D], BF16))
        b_wo = Buf("wo")
        for k in range(NCH):
            self.P.dma("pool", wo[:, k, :], self.w_out_d[l, k * 128:(k + 1) * 128, :], b_wo, writes=[b_wo], max_dma_last_dim=4096)
        self.wo_t = (wo, b_wo)

    def outp_phase(self, l):
        nc, P = self.nc, self.P
        NT = 256
        with ExitStack() as es:
            sb = lambda n, sh, d: es.enter_context(nc.sbuf_tensor(self.nm(n), sh, d))
            wo, b_wo = self.wo_t
            xt = [sb("xt", [128, NCH, NT], F32) for _ in range(2)]
            b_xt = [[Buf(f"xt{i}_{c}") for c in range(NCH)] for i in range(2)]
            yt = [sb("yt", [128, NCH, NT], BF16) for _ in range(2)]
            b_yt = [Buf("yt0"), Buf("yt1")]
            ps = [es.enter_context(nc.psum_tensor(self.nm("ps"), [128, 512], F32)) for _ in range(4)]
            b_ps = [Buf(f"ps{i}") for i in range(4)]
            for ti in range(self.T // NT):
                tok0 = ti * NT
                s = tok0 // self.S
                x, bx = xt[ti % 2], b_xt[ti % 2]
                y, by = yt[ti % 2], b_yt[ti % 2]
                self.load_x(x, bx, tok0, NT, False)
                P.dma("act", y[:], self.yT_d[:, :, tok0:tok0 + NT], by, writes=[by])
                for c in range(NCH):
                    pd, bd = ps[c % 4], b_ps[c % 4]
                    for k in range(NCH):
                        self.mm(pd[:, 0:NT], wo[:, k, c * 128:(c + 1) * 128], y[:, k, :], k == 0, k == NCH - 1, [b_wo, by], [bd])
                    self.stt(x[:, c, :], pd[:, 0:NT], self.Gtab[:, l, s, 8 + c:9 + c], x[:, c, :], ALU.mult, ALU.add,
                             [bd, bx[c], self.b_mod], [bx[c]])
                self.store_x(x, bx, tok0, NT, False)
            P.flush()


def make_in_maps(nseq_per_core, ncores, x, c, w_ada, b_ada, g_norm, w_ffn_up, w_ffn_down, w_in, b_fgate, conv_w, conv_b,
                 w_rgate, b_rgate, w_igate, b_igate, lru_lambda, g_qk, g_mix_out, w_out):
    f = lambda a: np.ascontiguousarray(np.asarray(a, dtype=np.float32))
    x, c = f(x), f(c)
    S = x.shape[1]
    pp = pack_pp(f(g_norm), f(b_ada), f(conv_w), f(conv_b), f(b_rgate), f(b_igate), f(lru_lambda), f(g_mix_out),
                 f(g_qk), f(b_fgate))
    shared = {"pp": pp, "w_ada": f(w_ada), "w_ffn_up": f(w_ffn_up), "w_ffn_down": f(w_ffn_down), "w_in": f(w_in),
              "w_rgate": f(w_rgate), "w_igate": f(w_igate), "w_out": f(w_out)}
    maps = []
    for i in range(ncores):
        xs = x[i * nseq_per_core:(i + 1) * nseq_per_core].reshape(nseq_per_core * S, D)
        cs = c[i * nseq_per_core:(i + 1) * nseq_per_core]
        cT = np.ascontiguousarray(cs.reshape(nseq_per_core, NCH, 128).transpose(2, 1, 0).reshape(128, NCH * nseq_per_core))
        m = dict(shared)
        m["x"] = np.ascontiguousarray(xs)
        m["cT"] = cT
        maps.append(m)
    return maps


_CACHE = {}


def kernel(**inputs):
    x = inputs["x"]
    B, S, _ = x.shape
    ns = B // NCORES
    key = (ns, S)
    if key not in _CACHE:
        _CACHE[key] = MK(ns, S)
    mk = _CACHE[key]
    maps = make_in_maps(ns, NCORES, **inputs)
    res = run_bass_kernel_spmd(mk.nc, maps, core_ids=list(range(NCORES)))
    out = np.stack([r["out"].reshape(ns, S, D) for r in res.results], axis=0).reshape(B, S, D)
    return out.astype(np.float32)
```

```python
from contextlib import ExitStack
import os
CUT = int(os.environ.get('CUT', '99'))
CUT2 = int(os.environ.get('CUT2', '99'))
import numpy as np
import concourse.bass as bass
import concourse.mybir as mybir
from concourse.bass_utils import run_bass_kernel_spmd

F32 = mybir.dt.float32
BF16 = mybir.dt.bfloat16
AF = mybir.ActivationFunctionType
ALU = mybir.AluOpType

ENG = ("pe", "act", "dve", "pool", "sp")

D = 1024
NCH = 8
DFF = 2816
NFF = 22
NIN = 2564
EPS = 1e-6
NCORES = 8


class Buf:
    __slots__ = ("name", "w", "r", "dk", "gd")

    def __init__(self, name):
        self.name = name
        self.w = None
        self.r = {}
        self.dk = None
        self.gd = ()


class Prog:
    def __init__(self, nc):
        self.nc = nc
        self.sems = {e: nc.alloc_semaphore("s_" + e) for e in ENG}
        self.val = {e: 0 for e in ENG}
        self.seen = {e: {} for e in ENG}
        self.q = {e: [] for e in ENG}
        self.ninst = 0

    def _need(self, e, reads, writes, skip_key=None):
        need = {}
        seen = self.seen[e]

        def add(k, v, raw):
            if k == e and e == "pe":
                return
            if seen.get(k, 0) >= v:
                return
            if need.get(k, 0) < v:
                need[k] = v

        for b in reads:
            if b.w is not None:
                add(b.w[0], b.w[1], True)
        for b in writes:
            if b.w is not None:
                if b.w[0] == skip_key:
                    for k, v in b.gd:
                        add(k, v, False)
                else:
                    add(b.w[0], b.w[1], False)
            for k, v in b.r.items():
                add(k, v, False)
        for k, v in need.items():
            self.q[e].append(("w", k, v))
            seen[k] = v

    def op(self, e, fn, reads=(), writes=()):
        self._need(e, reads, writes)
        self.val[e] += 1
        v = self.val[e]
        self.q[e].append(("i", fn))
        for b in reads:
            b.r[e] = v
        for b in writes:
            b.w = (e, v)
            b.r = {}
        self.ninst += 1

    def dma_key(self, buf):
        if buf.dk is None:
            buf.dk = "d_" + buf.name
            if buf.dk not in self.sems:
                self.sems[buf.dk] = self.nc.alloc_semaphore(buf.dk)
                self.val[buf.dk] = 0
        return buf.dk

    def dma(self, qe, out, in_, own, reads=(), writes=(), **kw):
        dk = self.dma_key(own)
        self._need(qe, reads, writes, skip_key=dk)
        self.val[dk] += 16
        v = self.val[dk]
        self.q[qe].append(("d", out, in_, dk, kw))
        for b in reads:
            b.r[dk] = v
        for b in writes:
            if b.w is None or b.w[0] != dk or b.r:
                b.gd = tuple(([b.w] if b.w is not None else []) + list(b.r.items()))
            b.w = (dk, v)
            b.r = {}
        self.ninst += 1

    def flush(self):
        for e in ENG:
            for k, v in self.val.items():
                if k != e and v > 0 and self.seen[e].get(k, 0) < v:
                    self.q[e].append(("w", k, v))
                    self.seen[e][k] = v
        nc = self.nc
        sems = self.sems
        with nc.Block() as blk:
            def mk(e):
                items = self.q[e]
                own = sems[e]

                def body(eng):
                    for it in items:
                        if it[0] == "w":
                            eng.wait_ge(sems[it[1]], it[2])
                        elif it[0] == "i":
                            it[1](eng).then_inc(own, 1)
                        else:
                            eng.dma_start(out=it[1], in_=it[2], **it[4]).then_inc(sems[it[3]], 16)
                return body
            blk.tensor(mk("pe"))
            blk.scalar(mk("act"))
            blk.vector(mk("dve"))
            blk.gpsimd(mk("pool"))
            blk.sync(mk("sp"))
        self.q = {e: [] for e in ENG}


PP_L = 142


def pp_off(l, name):
    offs = {"gn": 0, "bada": 24, "cw": 96, "cb": 112, "br": 116, "bi": 120, "lam": 124, "gmix": 128, "gqk": 136, "bf": 138}
    return l * PP_L + offs[name]


def pack_pp(g_norm, b_ada, conv_w, conv_b, b_rgate, b_igate, lru_lambda, g_mix_out, g_qk, b_fgate):
    pp = np.zeros((128, 2 * PP_L), np.float32)

    def fm(v):
        return np.ascontiguousarray(v.reshape(-1, 128).T)

    for l in range(2):
        o = l * PP_L
        for j in range(3):
            pp[:, o + j * 8:o + j * 8 + 8] = fm(g_norm[l, j])
        pp[:, o + 24:o + 96] = fm(b_ada[l])
        for k in range(4):
            pp[:, o + 96 + k * 4:o + 96 + k * 4 + 4] = fm(conv_w[l, k])
        pp[:, o + 112:o + 116] = fm(conv_b[l])
        pp[:, o + 116:o + 120] = fm(b_rgate[l])
        pp[:, o + 120:o + 124] = fm(b_igate[l])
        pp[:, o + 124:o + 128] = fm(lru_lambda[l])
        pp[:, o + 128:o + 136] = fm(g_mix_out[l])
        for i in range(2):
            pp[:, o + 136 + i] = np.concatenate([g_qk[l, i], g_qk[l, i]])
        pp[:, o + 138:o + 142] = b_fgate[l][None, :]
    return pp


class MK:
    def __init__(self, nseq, S, layers=(0, 1), dbg=None, phases=("ffn", "lru", "sb", "fox", "outp")):
        self.nseq, self.S = nseq, S
        self.T = T = nseq * S
        self.dbg = dbg
        nc = self.nc = bass.Bass("TRN2", target_bir_lowering=False)
        self.P = Prog(nc)
        di = lambda n, s: nc.dram_tensor(n, s, F32, kind="ExternalInput").ap()
        self.x_d = di("x", [T, D])
        self.cT_d = di("cT", [128, NCH * nseq])
        self.pp_d = di("pp", [128, 2 * PP_L])
        self.w_ada_d = di("w_ada", [2, D, 9 * D])
        self.w_up_d = di("w_ffn_up", [2, 2, D, 2 * DFF])
        self.w_dn_d = di("w_ffn_down", [2, 2, DFF, D])
        self.w_in_d = di("w_in", [2, D, NIN])
        self.w_rg_d = di("w_rgate", [2, 8, 64, 64])
        self.w_ig_d = di("w_igate", [2, 8, 64, 64])
        self.w_out_d = di("w_out", [2, D, D])
        self.out_d = nc.dram_tensor("out", [T, D], F32, kind="ExternalOutput").ap()
        self.xT_d = nc.dram_tensor("xT_scr", [NCH, 128, T], F32, kind="Internal").ap().rearrange("c p t -> p c t")
        self.yT_d = nc.dram_tensor("yT_scr", [NCH, 128, T], BF16, kind="Internal").ap().rearrange("c p t -> p c t")
        self.hT_d = nc.dram_tensor("hT_scr", [NCH, 128, T], BF16, kind="Internal").ap().rearrange("c p t -> p c t")
        self.uid = 0
        self.consts()
        for l in layers:
            if l == layers[0]:
                with ExitStack() as esw:
                    w = self.ffn_weights(esw, l, 0) if "ffn" in phases else None
                    self.ada_phase()
                    self.ffn_phase(l, 0, first=True, last=False, w=w)
            else:
                self.ffn_phase(l, 0, first=False, last=False)
            for s in range(nseq):
                if "lru" in phases:
                    self.lru_phase(l, s)
                if "sb" in phases:
                    self.sb_phase(l, s)
                if "fox" in phases:
                    self.fox_phase(l, s)
            with ExitStack() as esw:
                w = None
                if "outp" in phases:
                    w = self.ffn_weights(esw, l, 1, load=False)
                    with ExitStack() as eso:
                        self.outp_wo(eso, l)
                        self.ffn_weights_load(w, l, 1)
                        self.outp_phase(l)
                self.ffn_phase(l, 1, first=False, last=(l == layers[-1]), w=w)

    def nm(self, s):
        self.uid += 1
        return f"{s}_{self.uid}"

    def mm(self, out, lhsT, rhs, start, stop, reads, writes):
        self.P.op("pe", lambda e: e.matmul(out, lhsT, rhs, start=start, stop=stop), reads, writes)

    def tr(self, out, in_, ident, reads, writes):
        self.P.op("pe", lambda e: e.transpose(out, in_, ident), reads, writes)

    def act(self, out, in_, func, reads, writes, **kw):
        self.P.op("act", lambda e: e.activation(out=out, in_=in_, func=func, **kw), reads, writes)

    def tt(self, eng, out, in0, in1, op, reads, writes):
        self.P.op(eng, lambda e: e.tensor_tensor(out=out, in0=in0, in1=in1, op=op), reads, writes)

    def stt(self, out, in0, scalar, in1, op0, op1, reads, writes):
        self.P.op("dve", lambda e: e.scalar_tensor_tensor(out=out, in0=in0, scalar=scalar, in1=in1, op0=op0, op1=op1),
                  reads, writes)

    def ts(self, eng, out, in0, s1, s2, op0, op1, reads, writes):
        self.P.op(eng, lambda e: e.tensor_scalar(out=out, in0=in0, scalar1=s1, scalar2=s2, op0=op0, op1=op1), reads, writes)

    def cp(self, eng, out, in_, reads, writes):
        if eng == "act":
            self.act(out, in_, AF.Identity, reads, writes)
        else:
            self.P.op(eng, lambda e: e.tensor_copy(out=out, in_=in_), reads, writes)

    def memset(self, eng, ap, val, writes):
        self.P.op(eng, lambda e: e.memset(ap, val), (), writes)

    def recip(self, out, in_, reads, writes):
        self.P.op("dve", lambda e: e.reciprocal(out=out, in_=in_), reads, writes)

    def consts(self):
        nc, P = self.nc, self.P
        al = lambda n, s, d: nc.alloc_sbuf_tensor("c_" + n, s, d)
        self.pp = al("pp", [128, 2 * PP_L], F32)
        self.b_pp = Buf("pp")
        P.dma("sp", self.pp[:], self.pp_d, self.b_pp, writes=[self.b_pp])
        self.identf = al("identf", [128, 128], F32)
        self.ones16 = al("ones16", [128, 128], BF16)
        self.onesf = al("onesf", [128, 128], F32)
        self.bd16 = al("bd16", [128, 128], BF16)
        self.negU16 = al("negU16", [128, 128], BF16)
        self.Uf = al("Uf", [128, 128], F32)
        self.selL = al("selL", [128, 128], F32)
        self.mask_lt = al("mask_lt", [128, 4, 512], BF16)
        self.mask_le = al("mask_le", [128, 4, 512], BF16)
        self.b_c = Buf("consts")
        bc = [self.b_c]

        def sel(t, pattern, cmp, base, cm):
            P.op("pool", lambda g: g.affine_select(out=t, in_=t, pattern=pattern, compare_op=cmp, fill=0.0,
                                                   base=base, channel_multiplier=cm), bc, bc)
        self.memset("pool", self.identf[:], 1.0, bc)
        sel(self.identf[:], [[-1, 128]], ALU.is_equal, 0, 1)
        self.memset("pool", self.ones16[:], 1.0, bc)
        self.memset("pool", self.onesf[:], 1.0, bc)
        self.memset("pool", self.bd16[:], 0.0, bc)
        self.memset("pool", self.bd16[0:64, 0:64], 1.0, bc)
        self.memset("pool", self.bd16[64:128, 64:128], 1.0, bc)
        self.memset("pool", self.negU16[:], -1.0, bc)
        sel(self.negU16[:], [[-1, 128]], ALU.is_ge, 0, 1)
        self.memset("pool", self.Uf[:], 1.0, bc)
        sel(self.Uf[:], [[1, 128]], ALU.is_ge, 0, -1)
        self.memset("pool", self.selL[:], 1.0, bc)
        sel(self.selL[:], [[0, 128]], ALU.is_equal, -127, 1)
        self.memset("pool", self.mask_lt[:], 1.0, bc)
        self.memset("pool", self.mask_le[:], 1.0, bc)
        for o in range(4):
            sel(self.mask_lt[:, o, :], [[1, 512]], ALU.is_gt, -o * 128, -1)
            sel(self.mask_le[:, o, :], [[1, 512]], ALU.is_ge, -o * 128, -1)
        ns = self.nseq
        self.modT = al("modT", [128, 2, ns, 72], F32)
        self.Atab = al("Atab", [128, 2, ns, 24], F32)
        self.Gtab = al("Gtab", [128, 2, ns, 24], F32)
        self.cneg = al("cneg", [128, 2, 2, 4], F32)
        self.gq8 = al("gq8", [128, 2, 2], F32)
        self.b_mod = Buf("mod")

    def ppc(self, l, name, i=0, n=1):
        o = pp_off(l, name) + i
        return self.pp[:, o:o + n]

    def ada_phase(self):
        nc, P, ns = self.nc, self.P, self.nseq
        with ExitStack() as es:
            sb = lambda n, s, d: es.enter_context(nc.sbuf_tensor(self.nm(n), s, d))
            cT = sb("cT", [128, NCH * ns], F32)
            b_cT = Buf("cT")
            wb = [sb("wada", [128, NCH, 512], F32) for _ in range(2)]
            b_wb = [Buf("wada0"), Buf("wada1")]
            ps = es.enter_context(nc.psum_tensor(self.nm("adaps"), [128, 512], F32))
            b_ps = Buf("adaps")
            tmp = sb("adatmp", [128, 4], F32)
            P.dma("sp", cT[:], self.cT_d, b_cT, writes=[b_cT])
            self.act(cT[:], cT[:], AF.Silu, [b_cT], [b_cT])
            it = 0
            for l in range(2):
                for g in range(18):
                    w, bw = wb[it % 2], b_wb[it % 2]
                    it += 1
                    for k in range(NCH):
                        P.dma("sp" if k % 2 == 0 else "act", w[:, k, :],
                              self.w_ada_d[l, k * 128:(k + 1) * 128, g * 512:(g + 1) * 512], bw, writes=[bw])
                    for mm_ in range(4):
                        m = g * 4 + mm_
                        for k in range(NCH):
                            self.mm(ps[:, m * ns:(m + 1) * ns], w[:, k, mm_ * 128:(mm_ + 1) * 128],
                                    cT[:, k * ns:(k + 1) * ns], k == 0, k == NCH - 1, [bw, b_cT], [b_ps])
                bm = [self.b_mod]
                for s in range(ns):
                    self.tt("dve", self.modT[:, l, s, :], ps[:, s:72 * ns:ns], self.ppc(l, "bada", 0, 72), ALU.add,
                            [b_ps, self.b_pp], bm)
                    for j in range(3):
                        self.stt(self.Atab[:, l, s, j * 8:(j + 1) * 8], self.modT[:, l, s, j * 24 + 8:j * 24 + 16], 1.0,
                                 self.ppc(l, "gn", j * 8, 8), ALU.add, ALU.mult, bm + [self.b_pp], bm)
                        self.ts("dve", self.Gtab[:, l, s, j * 8:(j + 1) * 8], self.modT[:, l, s, j * 24 + 16:j * 24 + 24],
                                1.0, 1.0 if j == 1 else 0.5, ALU.add, ALU.mult, bm, bm)
                self.act(tmp[:], self.ppc(l, "lam", 0, 4), AF.Exp, [self.b_pp], bm, scale=-1.0)
                self.act(tmp[:], tmp[:], AF.Ln, bm, bm, bias=1.0)
                self.ts("dve", self.cneg[:, l, 0, :], tmp[:], -8.0, None, ALU.mult, ALU.bypass, bm, bm)
                self.ts("dve", self.cneg[:, l, 1, :], tmp[:], -16.0, None, ALU.mult, ALU.bypass, bm, bm)
                self.ts("dve", self.gq8[:, l, 0:1], self.ppc(l, "gqk", 0, 1), 0.125, None, ALU.mult, ALU.bypass,
                        [self.b_pp], bm)
                self.cp("dve", self.gq8[:, l, 1:2], self.ppc(l, "gqk", 1, 1), [self.b_pp], bm)
            if self.dbg == "ada":
                self.dbg_out("dbg_mod", self.modT[:].rearrange("p a b c -> p (a b c)"), [128, 2 * ns * 72], [self.b_mod])
            P.flush()

    def dbg_out(self, name, ap, shape, reads, dt=F32):
        d = self.nc.dram_tensor(name, shape, dt, kind="ExternalOutput").ap()
        b = Buf(self.nm("dbg"))
        self.P.dma("sp", d, ap, b, reads=reads)

    def norm_a(self, xt, b_xt, sq, b_sq):
        self.act(sq[:], xt[:], AF.Square, b_xt, [b_sq])

    def norm_b(self, xt, b_xt, h, b_h, sq, b_sq, ps, b_psb, rs, b_rs, tmp, b_tmp, l, j, s, NT):
        for c in range(NCH):
            self.mm(ps[:, 0:NT], self.ones16[:], sq[:, c, :], c == 0, c == NCH - 1, [b_sq, self.b_c], [b_psb])
        self.act(rs[:], ps[:, 0:NT], AF.Ln, [b_psb], [b_rs], scale=1.0 / D, bias=EPS)
        self.act(rs[:], rs[:], AF.Exp, [b_rs], [b_rs], scale=-0.5)
        for c in range(NCH):
            t, bt = tmp[c % 2], b_tmp[c % 2]
            self.stt(t[:], xt[:, c, :], self.Atab[:, l, s, j * 8 + c:j * 8 + c + 1], rs[:], ALU.mult, ALU.mult,
                     [b_xt[c], b_rs, self.b_mod], [bt])
            self.act(h[:, c, :], t[:], AF.Identity, [bt, self.b_mod], [b_h[c]],
                     bias=self.modT[:, l, s, j * 24 + c:j * 24 + c + 1])

    def norm_mod(self, xt, b_xt, h, b_h, sq, b_sq, ps, b_psb, rs, b_rs, tmp, b_tmp, l, j, s, NT):
        self.norm_a(xt, b_xt, sq, b_sq)
        self.norm_b(xt, b_xt, h, b_h, sq, b_sq, ps, b_psb, rs, b_rs, tmp, b_tmp, l, j, s, NT)

    def load_x(self, xt, b_xt, tok0, NT, first, xtok=None, b_xtok=None, pst=None, b_pst=None):
        P = self.P
        if not first:
            P.dma("sp", xt[:], self.xT_d[:, :, tok0:tok0 + NT], b_xt[0], writes=b_xt)
            return
        for g in range(NT // 128):
            xk, bk = xtok[g % 2], b_xtok[g % 2]
            P.dma("sp", xk[:], self.x_d[tok0 + g * 128:tok0 + (g + 1) * 128, :], bk, writes=[bk])
            for c in range(NCH):
                pt, bpt = pst[c % 2], b_pst[c % 2]
                self.tr(pt[:, 0:128], xk[:, c * 128:(c + 1) * 128], self.identf[:], [bk, self.b_c], [bpt])
                self.cp("act" if c % 2 == 0 else "dve", xt[:, c, g * 128:(g + 1) * 128], pt[:, 0:128], [bpt], [b_xt[c]])

    def store_x(self, xo, b_xo, tok0, NT, last, xtok=None, b_xtok=None, pst=None, b_pst=None):
        P = self.P
        if not last:
            P.dma("sp", self.xT_d[:, :, tok0:tok0 + NT], xo[:], b_xo[0], reads=b_xo)
            return
        for g in range(NT // 128):
            xk, bk = xtok[g % 2], b_xtok[g % 2]
            for hh in range(2):
                pt, bpt = pst[hh], b_pst[hh]
                for cc in range(4):
                    c = hh * 4 + cc
                    self.tr(pt[:, cc * 128:(cc + 1) * 128], xo[:, c, g * 128:(g + 1) * 128], self.identf[:],
                            [b_xo[c], self.b_c], [bpt])
                self.cp("act" if hh == 0 else "dve", xk[:, hh * 512:(hh + 1) * 512], pt[:], [bpt], [bk])
            P.dma("sp", self.out_d[tok0 + g * 128:tok0 + (g + 1) * 128, :], xk[:], bk, reads=[bk])

    def ffn_weights(self, es, l, f, load=True):
        nc = self.nc
        wup = es.enter_context(nc.sbuf_tensor(self.nm("wup"), [128, NCH, 2 * DFF], BF16))
        wdn = es.enter_context(nc.sbuf_tensor(self.nm("wdn"), [128, NFF, D], BF16))
        w = (wup, wdn, Buf("wup"), Buf("wdn"))
        if load:
            self.ffn_weights_load(w, l, f)
        return w

    def ffn_weights_load(self, w, l, f):
        wup, wdn, b_wup, b_wdn = w
        for k in range(NCH):
            self.P.dma("pool", wup[:, k, :], self.w_up_d[l, f, k * 128:(k + 1) * 128, :], b_wup, writes=[b_wup],
                       max_dma_last_dim=4096)
        for k in range(NFF):
            self.P.dma("pool", wdn[:, k, :], self.w_dn_d[l, f, k * 128:(k + 1) * 128, :], b_wdn, writes=[b_wdn],
                       max_dma_last_dim=4096)

    def ffn_phase(self, l, f, first, last, w=None):
        nc, P = self.nc, self.P
        NT = 256
        j = 0 if f == 0 else 2
        with ExitStack() as es:
            sb = lambda n, s, d: es.enter_context(nc.sbuf_tensor(self.nm(n), s, d))
            wup, wdn, b_wup, b_wdn = w if w is not None else self.ffn_weights(es, l, f)
            NS = 2
            xt = [sb("xt", [128, NCH, NT], F32) for _ in range(NS)]
            b_xt = [[Buf(f"xt{i}_{c}") for c in range(NCH)] for i in range(NS)]
            hh = [sb("h", [128, NCH, NT], BF16) for _ in range(2)]
            b_hh = [[Buf(f"h{i}_{c}") for c in range(NCH)] for i in range(2)]
            sqq = [sb("sq", [128, NCH, NT], BF16) for _ in range(2)]
            b_sqq = [Buf("sq0"), Buf("sq1")]
            rss = [sb("rs", [128, NT], F32) for _ in range(2)]
            b_rss = [Buf("rs0"), Buf("rs1")]
            tmp = [sb("tmp", [128, NT], F32) for _ in range(2)]
            b_tmp = [Buf("tmp0"), Buf("tmp1")]
            sg = [sb("sg", [128, NT], F32) for _ in range(2)]
            b_sg = [Buf("sg0"), Buf("sg1")]
            a = sb("a", [128, NFF, NT], BF16)
            b_a = [Buf(f"a{k}") for k in range(NFF)]
            ps = [es.enter_context(nc.psum_tensor(self.nm("ps"), [128, 512], F32)) for _ in range(8)]
            b_ps = [Buf(f"ps{i}") for i in range(8)]
            xtok = b_xtok = None
            if first or last:
                xtok = [sb("xtok", [128, D], F32) for _ in range(2)]
                b_xtok = [Buf("xtok0"), Buf("xtok1")]
            pst, b_pst = [ps[7], ps[0]], [b_ps[7], b_ps[0]]
            ntl = self.T // NT

            def prep_a(ti):
                self.load_x(xt[ti % NS], b_xt[ti % NS], ti * NT, NT, first, xtok, b_xtok, pst, b_pst)
                self.norm_a(xt[ti % NS], b_xt[ti % NS], sqq[ti % 2], b_sqq[ti % 2])

            def prep_b(ti):
                self.norm_b(xt[ti % NS], b_xt[ti % NS], hh[ti % 2], b_hh[ti % 2], sqq[ti % 2], b_sqq[ti % 2], ps[0], b_ps[0],
                            rss[ti % 2], b_rss[ti % 2], tmp, b_tmp, l, j, (ti * NT) // self.S, NT)

            prep_a(0)
            prep_b(0)
            for ti in range(ntl):
                tok0 = ti * NT
                s = tok0 // self.S
                x, bx = xt[ti % NS], b_xt[ti % NS]
                h, b_h = hh[ti % 2], b_hh[ti % 2]
                for jj in range(NFF):
                    pg, pu = ps[1 + 2 * (jj % 2)], ps[2 + 2 * (jj % 2)]
                    bg, bu = b_ps[1 + 2 * (jj % 2)], b_ps[2 + 2 * (jj % 2)]
                    for k in range(NCH):
                        self.mm(pg[:, 0:NT], wup[:, k, jj * 128:(jj + 1) * 128], h[:, k, :], k == 0, k == NCH - 1,
                                [b_wup, b_h[k]], [bg])
                    for k in range(NCH):
                        self.mm(pu[:, 0:NT], wup[:, k, DFF + jj * 128:DFF + (jj + 1) * 128], h[:, k, :], k == 0,
                                k == NCH - 1, [b_wup, b_h[k]], [bu])
                    self.act(sg[jj % 2][:], pg[:, 0:NT], AF.Silu, [bg], [b_sg[jj % 2]])
                    self.tt("dve", a[:, jj, :], sg[jj % 2][:], pu[:, 0:NT], ALU.mult, [b_sg[jj % 2], bu], [b_a[jj]])
                    if jj == 12 and ti + 1 < ntl:
                        prep_a(ti + 1)
                if ti + 1 < ntl:
                    prep_b(ti + 1)
                for c in range(NCH):
                    pd, bd = ps[5 + c % 2], b_ps[5 + c % 2]
                    for k in range(NFF):
                        self.mm(pd[:, 0:NT], wdn[:, k, c * 128:(c + 1) * 128], a[:, k, :], k == 0, k == NFF - 1,
                                [b_wdn, b_a[k]], [bd])
                    self.stt(x[:, c, :], pd[:, 0:NT], self.Gtab[:, l, s, j * 8 + c:j * 8 + c + 1], x[:, c, :],
                             ALU.mult, ALU.add, [bd, bx[c], self.b_mod], [bx[c]])
                self.store_x(x, bx, tok0, NT, last, xtok, b_xtok, pst, b_pst)
            P.flush()

    def tile_bufs(self, sb, NT, nslots=2, h_only=False):
        o = {}
        if h_only:
            o["h"] = [sb("h", [128, NCH, NT], BF16) for _ in range(2)]
            o["b_h"] = [[Buf(f"h{i}_{c}") for c in range(NCH)] for i in range(2)]
            return o
        o["xt"] = [sb("xt", [128, NCH, NT], F32) for _ in range(nslots)]
        o["b_xt"] = [[Buf(f"xt{i}_{c}") for c in range(NCH)] for i in range(nslots)]
        o["h"] = [sb("h", [128, NCH, NT], BF16) for _ in range(2)]
        o["b_h"] = [[Buf(f"h{i}_{c}") for c in range(NCH)] for i in range(2)]
        o["sq"] = [sb("sq", [128, NCH, NT], BF16) for _ in range(2)]
        o["b_sq"] = [Buf("sq0"), Buf("sq1")]
        o["rs"] = [sb("rs", [128, NT], F32) for _ in range(2)]
        o["b_rs"] = [Buf("rs0"), Buf("rs1")]
        o["tmp"] = [sb("tmp", [128, NT], F32) for _ in range(2)]
        o["b_tmp"] = [Buf("tmp0"), Buf("tmp1")]
        return o

    def tile_h(self, o, ti, tok0, NT, ps0, b_ps0, l, s):
        x, bx = o["xt"][ti % 2], o["b_xt"][ti % 2]
        self.load_x(x, bx, tok0, NT, False)
        i = ti % 2
        self.norm_mod(x, bx, o["h"][i], o["b_h"][i], o["sq"][i], o["b_sq"][i], ps0, b_ps0, o["rs"][i], o["b_rs"][i],
                      o["tmp"], o["b_tmp"], l, 1, s, NT)
        return o["h"][i], o["b_h"][i]

    def tile_a(self, o, ti, tok0, NT):
        x, bx = o["xt"][ti % 2], o["b_xt"][ti % 2]
        self.load_x(x, bx, tok0, NT, False)
        self.norm_a(x, bx, o["sq"][ti % 2], o["b_sq"][ti % 2])

    def tile_b(self, o, ti, NT, ps0, b_ps0, l, s):
        i = ti % 2
        self.norm_b(o["xt"][i], o["b_xt"][i], o["h"][i], o["b_h"][i], o["sq"][i], o["b_sq"][i], ps0, b_ps0, o["rs"][i],
                    o["b_rs"][i], o["tmp"], o["b_tmp"], l, 1, s, NT)

    def store_h(self, o, ti, tok0, NT):
        i = ti % 2
        self.P.dma("act", self.hT_d[:, :, tok0:tok0 + NT], o["h"][i][:], o["b_h"][i][0], reads=o["b_h"][i])

    def load_h(self, o, ti, tok0, NT):
        i = ti % 2
        self.P.dma("sp", o["h"][i][:], self.hT_d[:, :, tok0:tok0 + NT], o["b_h"][i][0], writes=o["b_h"][i])

    def load_w_in(self, w, b_w, l, c0, c1):
        for k in range(NCH):
            self.P.dma("pool", w[:, k, :], self.w_in_d[l, k * 128:(k + 1) * 128, c0:c1], b_w, writes=[b_w],
                       max_dma_last_dim=4096)

    def group_rms_store(self, y, b_y, nchk, gm0, l, tok0, NT, sqy, b_sqy, ps, b_ps, rs2, b_rs2, yn, b_yn, ych0):
        self.act(sqy[:], y[:], AF.Square, b_y, [b_sqy])
        for c in range(nchk):
            self.mm(ps[:, 0:NT], self.ones16[:], sqy[:, c, :], c == 0, c == nchk - 1, [b_sqy, self.b_c], [b_ps])
        self.act(rs2[:], ps[:, 0:NT], AF.Ln, [b_ps], [b_rs2], scale=1.0 / (nchk * 128), bias=EPS)
        self.act(rs2[:], rs2[:], AF.Exp, [b_rs2], [b_rs2], scale=-0.5)
        for c in range(nchk):
            self.stt(yn[:, c, :], y[:, c, :], self.ppc(l, "gmix", gm0 + c, 1), rs2[:], ALU.mult, ALU.mult,
                     [b_y[c], b_rs2, self.b_pp], [b_yn])
        self.P.dma("sp", self.yT_d[:, ych0:ych0 + nchk, tok0:tok0 + NT], yn[:], b_yn, reads=[b_yn])

    def lru_phase(self, l, s):
        nc, P = self.nc, self.P
        NT = 512
        with ExitStack() as es:
            sb = lambda n, sh, d: es.enter_context(nc.sbuf_tensor(self.nm(n), sh, d))
            win = sb("win", [128, NCH, 1024], BF16)
            b_win = Buf("win")
            self.load_w_in(win, b_win, l, 0, 1024)
            wg = [sb("wg", [128, 4, 128], F32) for _ in range(2)]
            b_wg = [Buf("wg0"), Buf("wg1")]
            for gi, wd in enumerate((self.w_rg_d, self.w_ig_d)):
                self.memset("pool", wg[gi][:], 0.0, [b_wg[gi]])
                src = wd[l].rearrange("(c two) i j -> two i c j", two=2)
                P.dma("sp", wg[gi][0:64, :, 0:64], src[0], b_wg[gi], writes=[b_wg[gi]])
                P.dma("sp", wg[gi][64:128, :, 64:128], src[1], b_wg[gi], writes=[b_wg[gi]])
            o = self.tile_bufs(sb, NT)
            ps = [es.enter_context(nc.psum_tensor(self.nm("ps"), [128, 512], F32)) for _ in range(8)]
            b_ps = [Buf(f"ps{i}") for i in range(8)]
            lx = [sb("lx", [128, NT + 3], F32) for _ in range(4)]
            b_lx = [Buf(f"lx{c}") for c in range(4)]
            carry = sb("carry", [128, 4], F32)
            b_carry = [Buf(f"carry{c}") for c in range(4)]
            y = sb("y", [128, 4, NT], F32)
            b_y = [Buf(f"y{c}") for c in range(4)]
            f32t = lambda n: (sb(n, [128, NT], F32), Buf(n))
            f32q = lambda n: [f32t(n + str(i)) for i in range(4)]
            gs_, u_, r_, ig_, a_, m_ = (f32q(n) for n in ("gs", "u", "r", "ig", "a", "m"))
            hs_ = r_
            t1_2 = [f32q("t1a"), f32q("t1b")]
            y2 = [y, sb("y2", [128, 4, NT], F32)]
            b_y2 = [b_y, [Buf(f"y2{c}") for c in range(4)]]
            sqy = sb("sqy", [128, 4, NT], BF16)
            b_sqy = Buf("sqy")
            rs2, b_rs2 = f32t("rs2")
            yn = sb("yn", [128, 4, NT], BF16)
            b_yn = Buf("yn")
            cw = lambda k, c: self.ppc(l, "cw", k * 4 + c, 1)
            C4 = range(4)
            ntl = self.S // NT
            S0 = s * self.S

            def head(ti):
                h, b_h = o["h"][ti % 2], o["b_h"][ti % 2]
                t1_ = t1_2[ti % 2]
                for c in C4:
                    pl, bl = ps[1 + c % 2], b_ps[1 + c % 2]
                    for k in range(NCH):
                        self.mm(pl[:], win[:, k, c * 128:(c + 1) * 128], h[:, k, :], k == 0, k == NCH - 1,
                                [b_win, b_h[k]], [bl])
                    if ti == 0:
                        self.memset("pool", lx[c][:, 0:3], 0.0, [b_lx[c]])
                    else:
                        self.cp("pool", lx[c][:, 0:3], lx[c][:, NT:NT + 3], [b_lx[c]], [b_lx[c]])
                    self.cp("act", lx[c][:, 3:NT + 3], pl[:], [bl], [b_lx[c]])
                if ti + 1 < ntl:
                    self.tile_a(o, ti + 1, S0 + (ti + 1) * NT, NT)
                for c in C4:
                    pg, bg = ps[3 + c % 2], b_ps[3 + c % 2]
                    for k in range(NCH):
                        self.mm(pg[:], win[:, k, 512 + c * 128:512 + (c + 1) * 128], h[:, k, :], k == 0, k == NCH - 1,
                                [b_win, b_h[k]], [bg])
                    self.cp("act", gs_[c][0][:], pg[:], [bg], [gs_[c][1]])
                    self.act(t1_[c][0][:], pg[:], AF.Square, [bg], [t1_[c][1]])
                if ti + 1 < ntl:
                    self.tile_b(o, ti + 1, NT, ps[0], b_ps[0], l, s)
                    self.store_h(o, ti + 1, S0 + (ti + 1) * NT, NT)
                for c in C4:
                    u, b_u = u_[c]
                    self.act(u[:], lx[c][:, 0:NT], AF.Identity, [b_lx[c], self.b_pp], [b_u], scale=cw(0, c),
                             bias=self.ppc(l, "cb", c, 1))
                    for k in range(1, 4):
                        self.stt(u[:], lx[c][:, k:k + NT], cw(k, c), u[:], ALU.mult, ALU.add, [b_lx[c], b_u, self.b_pp], [b_u])

            def st_a(ti):
                t1_ = t1_2[ti % 2]
                for c in C4:
                    u, b_u = u_[c]
                    pr, br_ = ps[1 + c], b_ps[1 + c]
                    pi, bi_ = ps[5 + c % 3], b_ps[5 + c % 3]
                    self.mm(pr[:], wg[0][:, c, :], u[:], True, True, [b_wg[0], b_u], [br_])
                    self.mm(pi[:], wg[1][:, c, :], u[:], True, True, [b_wg[1], b_u], [bi_])
                    self.act(r_[c][0][:], pr[:], AF.Sigmoid, [br_, self.b_pp], [r_[c][1]], bias=self.ppc(l, "br", c, 1))
                    self.act(ig_[c][0][:], pi[:], AF.Sigmoid, [bi_, self.b_pp], [ig_[c][1]], bias=self.ppc(l, "bi", c, 1))
                for c in C4:
                    t1, b_t1 = t1_[c]
                    self.ts("dve", t1[:], t1[:], 0.044715, 1.0, ALU.mult, ALU.add, [b_t1], [b_t1])
                    self.tt("pool", t1[:], t1[:], gs_[c][0][:], ALU.mult, [b_t1, gs_[c][1]], [b_t1])
                for c in C4:
                    t1, b_t1 = t1_[c]
                    self.act(t1[:], t1[:], AF.Sigmoid, [b_t1], [b_t1], scale=1.5957691216057308)
                for c in C4:
                    t1, b_t1 = t1_[c]
                    self.tt("pool", t1[:], t1[:], gs_[c][0][:], ALU.mult, [b_t1, gs_[c][1]], [b_t1])
                for c in C4:
                    r, b_r = r_[c]
                    self.act(a_[c][0][:], r[:], AF.Exp, [b_r, self.b_mod], [a_[c][1]], scale=self.cneg[:, l, 0, c:c + 1])
                    self.act(m_[c][0][:], r[:], AF.Exp, [b_r, self.b_mod], [m_[c][1]], scale=self.cneg[:, l, 1, c:c + 1])
                    ig, b_ig = ig_[c]
                    self.tt("pool", ig[:], ig[:], u_[c][0][:], ALU.mult, [b_ig, u_[c][1]], [b_ig])

            def st_d(ti):
                t1_ = t1_2[ti % 2]
                yy, b_yy = y2[ti % 2], b_y2[ti % 2]
                for c in C4:
                    self.act(m_[c][0][:], m_[c][0][:], AF.Sqrt, [m_[c][1]], [m_[c][1]], scale=-1.0, bias=1.0)
                    ig, b_ig = ig_[c]
                    self.tt("dve", ig[:], ig[:], m_[c][0][:], ALU.mult, [b_ig, m_[c][1]], [b_ig])
                for c in C4:
                    hs, b_hs = hs_[c]
                    init = 0.0 if ti == 0 else carry[:, c:c + 1]
                    self.P.op("dve", lambda e, init=init, hs=hs, a=a_[c][0], ig=ig_[c][0]: e.tensor_tensor_scan(
                        out=hs[:], data0=a[:], data1=ig[:], initial=init, op0=ALU.mult, op1=ALU.add),
                              [a_[c][1], ig_[c][1], b_carry[c]], [b_hs])
                    self.cp("pool", carry[:, c:c + 1], hs[:, NT - 1:NT], [b_hs], [b_carry[c]])
                    self.tt("pool", yy[:, c, :], hs[:], t1_[c][0][:], ALU.mult, [b_hs, t1_[c][1]], [b_yy[c]])

            def st_b(ti):
                self.group_rms_store(y2[ti % 2], b_y2[ti % 2], 4, 0, l, S0 + ti * NT, NT, sqy, b_sqy, ps[0], b_ps[0],
                                     rs2, b_rs2, yn, b_yn, 0)

            self.tile_a(o, 0, S0, NT)
            self.tile_b(o, 0, NT, ps[0], b_ps[0], l, s)
            self.store_h(o, 0, S0, NT)
            head(0)
            for ti in range(ntl):
                st_a(ti)
                if ti > 0:
                    st_b(ti - 1)
                if ti + 1 < ntl:
                    head(ti + 1)
                st_d(ti)
            st_b(ntl - 1)
            P.flush()

    def sb_phase(self, l, s):
        nc, P, S = self.nc, self.P, self.S
        NT = 512
        NKT = S // 128
        with ExitStack() as es:
            sb = lambda n, sh, d: es.enter_context(nc.sbuf_tensor(self.nm(n), sh, d))
            win = sb("win", [128, NCH, 768], BF16)
            b_win = Buf("win")
            self.load_w_in(win, b_win, l, 1024, 1792)
            o = self.tile_bufs(sb, NT, h_only=True)
            ps = [es.enter_context(nc.psum_tensor(self.nm("ps"), [128, 512], F32)) for _ in range(8)]
            b_ps = [Buf(f"ps{i}") for i in range(8)]
            qh = sb("qh", [128, 4, S], BF16)
            kT = sb("kT", [128, 2, S], BF16)
            v = sb("v", [128, NKT, 256], BF16)
            nsp = S // 512
            b_q = [[Buf(f"q{c}_{i}") for i in range(nsp)] for c in range(2)]
            b_k = [[Buf(f"k{c}_{i}") for i in range(nsp)] for c in range(2)]
            b_v = [Buf(f"v{i}") for i in range(NKT)]
            for hq in range(4):
                self.memset("pool", qh[:, hq, :], 0.0, [b for bb in b_q for b in bb])
            ntl = S // NT
            self.load_h(o, 0, s * S, NT)
            for ti in range(ntl):
                tok0 = s * S + ti * NT
                h, b_h = o["h"][ti % 2], o["b_h"][ti % 2]
                if ti + 1 < ntl:
                    self.load_h(o, ti + 1, tok0 + NT, NT)
                for c in range(2):
                    pq, bq = ps[1 + c], b_ps[1 + c]
                    for k in range(NCH):
                        self.mm(pq[:], win[:, k, c * 128:(c + 1) * 128], h[:, k, :], k == 0, k == NCH - 1, [b_win, b_h[k]], [bq])
                    self.act(qh[0:64, 2 * c, ti * NT:(ti + 1) * NT], pq[0:64, :], AF.Identity, [bq], [b_q[c][ti]], scale=0.125)
                    self.act(qh[64:128, 2 * c + 1, ti * NT:(ti + 1) * NT], pq[64:128, :], AF.Identity, [bq], [b_q[c][ti]],
                             scale=0.125)
                    pk, bk = ps[3 + c], b_ps[3 + c]
                    for k in range(NCH):
                        self.mm(pk[:], win[:, k, 256 + c * 128:256 + (c + 1) * 128], h[:, k, :], k == 0, k == NCH - 1,
                                [b_win, b_h[k]], [bk])
                    self.cp("dve", kT[:, c, ti * NT:(ti + 1) * NT], pk[:], [bk], [b_k[c][ti]])
                for g in range(4):
                    pv, bv = ps[5 + g % 2], b_ps[5 + g % 2]
                    for k in range(NCH):
                        self.mm(pv[:, 0:256], h[:, k, g * 128:(g + 1) * 128], win[:, k, 512:768], k == 0, k == NCH - 1,
                                [b_win, b_h[k]], [bv])
                    self.cp("act" if g % 2 == 0 else "dve", v[:, ti * 4 + g, :], pv[:, 0:256], [bv], [b_v[ti * 4 + g]])
            f32t = lambda n: (sb(n, [128, 512], F32), Buf(n))
            R = [f32t("R0"), f32t("R1")]
            E = [f32t("E0"), f32t("E1")]
            Tt = [f32t("T0"), f32t("T1")]
            SP = [(sb(f"sp{i}", [128, 512], BF16), Buf(f"sp{i}")) for i in range(3)]
            W = [(sb(f"w{i}", [128, 512], BF16), Buf(f"w{i}")) for i in range(3)]
            ysb = sb("ysb", [128, 2, 512], F32)
            b_ysb = [Buf("ysb0"), Buf("ysb1")]
            sqy = sb("sqy", [128, 2, 512], BF16)
            b_sqy = Buf("sqy")
            rs2, b_rs2 = f32t("rs2")
            yn = sb("yn", [128, 2, 512], BF16)
            b_yn = Buf("yn")
            pairs = []
            grp = 0
            for qs in range(nsp):
                njt = 4 * qs + 4
                for hd in range(4):
                    for idx, jt in enumerate(range(njt - 1, -1, -1)):
                        pairs.append((qs, hd, idx, jt, njt, grp))
                    grp += 1
            NP = len(pairs)

            def geo(i):
                qs, hd, idx, jt, njt, g = pairs[i]
                c, po = hd // 2, (hd % 2) * 64
                return qs, hd, idx, jt, njt, g, c, po, jt >= 4 * qs, jt - 4 * qs

            def s1(i):
                qs, hd, idx, jt, njt, g, c, po, diag, od = geo(i)
                pA, bA = ps[1 + i % 2], b_ps[1 + i % 2]
                self.mm(pA[:], kT[:, c, jt * 128:(jt + 1) * 128], qh[:, hd, qs * 512:(qs + 1) * 512],
                        True, True, [b_q[c][qs], b_k[c][jt // 4]], [bA])

            def s2(i):
                qs, hd, idx, jt, njt, g, c, po, diag, od = geo(i)
                pA, bA = ps[1 + i % 2], b_ps[1 + i % 2]
                Eb, bE = E[i % 2]
                sp, bsp = SP[i % 3]
                self.act(Eb[:], pA[:], AF.Exp, [bA], [bE])
                self.act(sp[:], Eb[:], AF.Ln, [bE], [bsp], bias=1.0)
                if diag:
                    self.tt("pool", sp[:], sp[:], self.mask_lt[:, od, :], ALU.mult, [bsp, self.b_c], [bsp])

            def s3(i):
                qs, hd, idx, jt, njt, g, c, po, diag, od = geo(i)
                pB, bB = ps[3 + i % 2], b_ps[3 + i % 2]
                sp, bsp = SP[i % 3]
                self.mm(pB[:], kT[:, c, jt * 128:(jt + 1) * 128], qh[:, hd, qs * 512:(qs + 1) * 512],
                        True, False, [b_q[c][qs], b_k[c][jt // 4]], [bB])
                self.mm(pB[:], self.negU16[:], sp[:], False, True, [bsp, self.b_c], [bB])
                if idx < njt - 1:
                    pC, bC = ps[5], b_ps[5]
                    self.mm(pC[:], self.ones16[:], sp[:], True, True, [bsp, self.b_c], [bC])

            def s4(i):
                qs, hd, idx, jt, njt, g, c, po, diag, od = geo(i)
                pB, bB = ps[3 + i % 2], b_ps[3 + i % 2]
                Rb, bR = R[g % 2]
                Tb, bT = Tt[i % 2]
                w, bw = W[i % 3]
                if idx == 0:
                    self.act(w[:], pB[:], AF.Exp, [bB], [bw])
                else:
                    self.tt("dve", Tb[:], pB[:], Rb[:], ALU.subtract, [bB, bR], [bT])
                    self.act(w[:], Tb[:], AF.Exp, [bT], [bw])
                if diag:
                    self.tt("pool", w[:], w[:], self.mask_lt[:, od, :], ALU.mult, [bw, self.b_c], [bw])
                if idx < njt - 1:
                    pC, bC = ps[5], b_ps[5]
                    if idx == 0:
                        self.cp("dve", Rb[:], pC[:], [bC], [bR])
                    else:
                        self.tt("dve", Rb[:], pC[:], Rb[:], ALU.add, [bC, bR], [bR])

            def s5(i):
                qs, hd, idx, jt, njt, g, c, po, diag, od = geo(i)
                w, bw = W[i % 3]
                pO, bO = ps[6 + g % 2], b_ps[6 + g % 2]
                self.mm(pO[:], v[:, jt, c * 128:(c + 1) * 128], w[:], idx == 0, idx == njt - 1, [b_v[jt], bw], [bO])
                if idx == njt - 1:
                    self.cp("act", ysb[po:po + 64, c, :], pO[po:po + 64, :], [bO], [b_ysb[c]])
                    if hd == 3:
                        self.group_rms_store(ysb, b_ysb, 2, 4, l, s * S + qs * 512, 512, sqy, b_sqy, ps[0], b_ps[0],
                                             rs2, b_rs2, yn, b_yn, 4)

            for n in range(NP + 3):
                if n < NP:
                    s1(n)
                    s2(n)
                if 0 <= n - 2 < NP:
                    s4(n - 2)
                if 0 <= n - 1 < NP:
                    s3(n - 1)
                if 0 <= n - 3 < NP:
                    s5(n - 3)
            P.flush()

    def fox_phase(self, l, s):
        nc, P, S = self.nc, self.P, self.S
        NT = 512
        NKT = S // 128
        with ExitStack() as es:
            sb = lambda n, sh, d: es.enter_context(nc.sbuf_tensor(self.nm(n), sh, d))
            win = sb("win", [128, NCH, 772], BF16)
            b_win = Buf("win")
            self.load_w_in(win, b_win, l, 1792, 2564)
            o = self.tile_bufs(sb, NT, h_only=True)
            ps = [es.enter_context(nc.psum_tensor(self.nm("ps"), [128, 512], F32)) for _ in range(8)]
            b_ps = [Buf(f"ps{i}") for i in range(8)]
            qh = sb("qh", [128, 4, S], BF16)
            kT = sb("kT", [128, 2, S], BF16)
            va = sb("va", [128, NKT, 4, 128], BF16)
            Fn = sb("Fn", [128, NKT, 4], F32)
            Fc = sb("Fc", [128, NKT, 4], F32)
            nsp = S // 512
            b_q = [[Buf(f"q{c}_{i}") for i in range(nsp)] for c in range(2)]
            b_k = [[Buf(f"k{c}_{i}") for i in range(nsp)] for c in range(2)]
            b_v = [Buf(f"v{i}") for i in range(NKT)]
            b_F = Buf("F")
            f32t = lambda n: (sb(n, [128, 512], F32), Buf(n))
            sq16 = [(sb(f"sq16{i}", [128, 512], BF16), Buf(f"sq16{i}")) for i in range(4)]
            rsq = [f32t(f"rsq{i}") for i in range(4)]
            tfs = [(sb(f"tf{i}", [128, 4], F32), Buf(f"tf{i}")) for i in range(4)]
            for i in range(NKT):
                self.memset("pool", va[:, i, :, :], 1.0, [b_v[i]])
            for hq in range(4):
                self.memset("pool", qh[:, hq, :], 0.0, [b for bb in b_q for b in bb])
            ntl = S // NT
            self.load_h(o, 0, s * S, NT)
            for ti in range(ntl):
                tok0 = s * S + ti * NT
                h, b_h = o["h"][ti % 2], o["b_h"][ti % 2]
                if ti + 1 < ntl:
                    self.load_h(o, ti + 1, tok0 + NT, NT)
                combos = [(0, 0), (0, 1), (1, 0), (1, 1)]
                for i2, (qk, c) in enumerate(combos):
                    pq, bq = ps[1 + i2], b_ps[1 + i2]
                    for k in range(NCH):
                        self.mm(pq[:], win[:, k, qk * 256 + c * 128:qk * 256 + (c + 1) * 128], h[:, k, :], k == 0,
                                k == NCH - 1, [b_win, b_h[k]], [bq])
                for i2 in range(4):
                    self.act(sq16[i2][0][:], ps[1 + i2][:], AF.Square, [b_ps[1 + i2]], [sq16[i2][1]])
                for i2 in range(4):
                    pss, bss = ps[5 + i2 % 2], b_ps[5 + i2 % 2]
                    self.mm(pss[:], self.bd16[:], sq16[i2][0][:], True, True, [sq16[i2][1], self.b_c], [bss])
                    self.act(rsq[i2][0][:], pss[:], AF.Ln, [bss], [rsq[i2][1]], scale=1.0 / 64, bias=EPS)
                for i2 in range(4):
                    self.act(rsq[i2][0][:], rsq[i2][0][:], AF.Exp, [rsq[i2][1]], [rsq[i2][1]], scale=-0.5)
                for i2, (qk, c) in enumerate(combos):
                    pq, bq = ps[1 + i2], b_ps[1 + i2]
                    rq_, brq = rsq[i2]
                    if qk == 1:
                        self.stt(kT[:, c, ti * NT:(ti + 1) * NT], pq[:], self.gq8[:, l, 1:2], rq_[:], ALU.mult, ALU.mult,
                                 [bq, brq, self.b_mod], [b_k[c][ti]])
                    else:
                        for hf in range(2):
                            pp_ = slice(hf * 64, hf * 64 + 64)
                            self.stt(qh[pp_, 2 * c + hf, ti * NT:(ti + 1) * NT], pq[pp_, :], self.gq8[pp_, l, 0:1], rq_[pp_, :],
                                     ALU.mult, ALU.mult, [bq, brq, self.b_mod], [b_q[c][ti]])
                for g in range(4):
                    tg = ti * 4 + g
                    tf, b_tf = tfs[g]
                    pv, bv = ps[5 + g % 2], b_ps[5 + g % 2]
                    for k in range(NCH):
                        self.mm(pv[:, 0:260], h[:, k, g * 128:(g + 1) * 128], win[:, k, 512:772], k == 0, k == NCH - 1,
                                [b_win, b_h[k]], [bv])
                    pv4 = pv[:, 0:256].rearrange("p (h d) -> p h d", h=4)
                    self.cp("act", va[:, tg, 0:4:2, 0:64], pv4[:, 0:4:2, :], [bv], [b_v[tg]])
                    self.cp("act", va[:, tg, 1:4:2, 64:128], pv4[:, 1:4:2, :], [bv], [b_v[tg]])
                    if CUT < 2:
                        continue
                    self.cp("act", tf[:], pv[:, 256:260], [bv], [b_tf])
                    self.tt("dve", tf[:], tf[:], self.ppc(l, "bf", 0, 4), ALU.add, [b_tf, self.b_pp], [b_tf])
                    self.act(tf[:], tf[:], AF.Exp, [b_tf], [b_tf], scale=-1.0)
                    self.act(tf[:], tf[:], AF.Ln, [b_tf], [b_tf], bias=1.0)
                    pF, bF_ = ps[7], b_ps[7]
                    self.mm(pF[:, 8 * g:8 * g + 4], self.Uf[:], tf[:], True, True, [b_tf, self.b_c], [bF_])
                    self.cp("act", Fn[:, tg, :], pF[:, 8 * g:8 * g + 4], [bF_], [b_F])
            pT, bT_ = ps[7], b_ps[7]
            self.mm(pT[:, 0:NKT * 4], self.selL[:], Fn[:].rearrange("p g h -> p (g h)"), True, True, [b_F, self.b_c], [bT_])
            Ft = sb("Ft", [128, NKT, 4], F32)
            b_Ft = Buf("Ft")
            self.cp("act", Ft[:].rearrange("p g h -> p (g h)"), pT[:, 0:NKT * 4], [bT_], [b_Ft])
            for hd in range(4):
                self.P.op("dve", lambda e, hd=hd: e.tensor_tensor_scan(out=Fc[:, :, hd], data0=self.onesf[:, 0:NKT],
                                                                      data1=Ft[:, :, hd], initial=0.0, op0=ALU.mult, op1=ALU.add),
                          [b_Ft, self.b_c], [b_F])
            if NKT > 1:
                self.tt("dve", Fn[:, 1:NKT, :], Fn[:, 1:NKT, :], Fc[:, 0:NKT - 1, :], ALU.add, [b_F], [b_F])
            Pw = [(sb(f"p{i}", [128, 512], BF16), Buf(f"p{i}")) for i in range(4)]
            bias_t = [(sb(f"bias{i}", [128, NKT], F32), Buf(f"bias{i}")) for i in range(2)]
            rd, b_rd = rsq[0]
            bsb, b_bsb = rsq[1]
            yfx = sb("yfx", [128, 2, 512], F32)
            b_yfx = [Buf("yfx0"), Buf("yfx1")]
            sqy = sb("sqy", [128, 2, 512], BF16)
            b_sqy = Buf("sqy")
            rs2, b_rs2 = f32t("rs2")
            yn = sb("yn", [128, 2, 512], BF16)
            b_yn = Buf("yn")
            pairs = []
            grp = 0
            for qs in range(nsp):
                njt = 4 * qs + 4
                for hd in range(4):
                    for jt in range(njt):
                        pairs.append((qs, hd, jt, njt, grp))
                    grp += 1
            NP = len(pairs)

            def geo(i):
                qs, hd, jt, njt, g = pairs[i]
                return qs, hd, jt, njt, g, hd // 2, (hd % 2) * 64, jt >= 4 * qs, jt - 4 * qs

            def f1(i):
                qs, hd, jt, njt, g, c, po, diag, od = geo(i)
                if jt == 0:
                    bt_, bbt = bias_t[g % 2]
                    self.ts("dve", bt_[:, 0:njt], Fn[:, 0:njt, hd], Fc[:, njt - 1, hd:hd + 1], None, ALU.subtract, ALU.bypass,
                            [b_F], [bbt])
                pA, bA = ps[1 + i % 4], b_ps[1 + i % 4]
                self.mm(pA[:], kT[:, c, jt * 128:(jt + 1) * 128], qh[:, hd, qs * 512:(qs + 1) * 512],
                        True, True, [b_q[c][qs], b_k[c][jt // 4]], [bA])

            def f2(i):
                qs, hd, jt, njt, g, c, po, diag, od = geo(i)
                bt_, bbt = bias_t[g % 2]
                pA, bA = ps[1 + i % 4], b_ps[1 + i % 4]
                p_, bp = Pw[i % 4]
                self.act(p_[:], pA[:], AF.Exp, [bA, bbt], [bp], bias=bt_[:, jt:jt + 1])
                if diag:
                    self.tt("dve", p_[:], p_[:], self.mask_le[:, od, :], ALU.mult, [bp, self.b_c], [bp])

            def f3(i):
                qs, hd, jt, njt, g, c, po, diag, od = geo(i)
                p_, bp = Pw[i % 4]
                pO, bO = ps[5 + g % 2], b_ps[5 + g % 2]
                self.mm(pO[:], va[:, jt, hd, :], p_[:], jt == 0, jt == njt - 1, [b_v[jt], bp], [bO])
                if jt == njt - 1:
                    dn = slice(64 - po, 128 - po)
                    self.recip(rd[dn, :], pO[dn, :], [bO], [b_rd])
                    self.cp("dve", bsb[po:po + 64, :], rd[dn, :], [b_rd], [b_bsb])
                    self.tt("dve", yfx[po:po + 64, c, :], pO[po:po + 64, :], bsb[po:po + 64, :], ALU.mult, [bO, b_bsb],
                            [b_yfx[c]])
                    if hd == 3:
                        self.group_rms_store(yfx, b_yfx, 2, 6, l, s * S + qs * 512, 512, sqy, b_sqy, ps[0], b_ps[0],
                                             rs2, b_rs2, yn, b_yn, 6)

            for n in range(NP + 3):
                if n < NP:
                    f1(n)
                if 0 <= n - 1 < NP:
                    f2(n - 1)
                if 0 <= n - 3 < NP:
                    f3(n - 3)
            P.flush()

    def outp_wo(self, es, l):
        wo = es.enter_context(self.nc.sbuf_tensor(self.nm("wo"), [128, NCH, D], BF16))
        b_wo = Buf("wo")
        for k in range(NCH):
            self.P.dma("pool", wo[:, k, :], self.w_out_d[l, k * 128:(k + 1) * 128, :], b_wo, writes=[b_wo], max_dma_last_dim=4096)
        self.wo_t = (wo, b_wo)

    def outp_phase(self, l):
        nc, P = self.nc, self.P
        NT = 256
        with ExitStack() as es:
            sb = lambda n, sh, d: es.enter_context(nc.sbuf_tensor(self.nm(n), sh, d))
            wo, b_wo = self.wo_t
            xt = [sb("xt", [128, NCH, NT], F32) for _ in range(2)]
            b_xt = [[Buf(f"xt{i}_{c}") for c in range(NCH)] for i in range(2)]
            yt = [sb("yt", [128, NCH, NT], BF16) for _ in range(2)]
            b_yt = [Buf("yt0"), Buf("yt1")]
            ps = [es.enter_context(nc.psum_tensor(self.nm("ps"), [128, 512], F32)) for _ in range(4)]
            b_ps = [Buf(f"ps{i}") for i in range(4)]
            for ti in range(self.T // NT):
                tok0 = ti * NT
                s = tok0 // self.S
                x, bx = xt[ti % 2], b_xt[ti % 2]
                y, by = yt[ti % 2], b_yt[ti % 2]
                self.load_x(x, bx, tok0, NT, False)
                P.dma("act", y[:], self.yT_d[:, :, tok0:tok0 + NT], by, writes=[by])
                for c in range(NCH):
                    pd, bd = ps[c % 4], b_ps[c % 4]
                    for k in range(NCH):
                        self.mm(pd[:, 0:NT], wo[:, k, c * 128:(c + 1) * 128], y[:, k, :], k == 0, k == NCH - 1, [b_wo, by], [bd])
                    self.stt(x[:, c, :], pd[:, 0:NT], self.Gtab[:, l, s, 8 + c:9 + c], x[:, c, :], ALU.mult, ALU.add,
                             [bd, bx[c], self.b_mod], [bx[c]])
                self.store_x(x, bx, tok0, NT, False)
            P.flush()


def make_in_maps(nseq_per_core, ncores, x, c, w_ada, b_ada, g_norm, w_ffn_up, w_ffn_down, w_in, b_fgate, conv_w, conv_b,
                 w_rgate, b_rgate, w_igate, b_igate, lru_lambda, g_qk, g_mix_out, w_out):
    f = lambda a: np.ascontiguousarray(np.asarray(a, dtype=np.float32))
    x, c = f(x), f(c)
    S = x.shape[1]
    pp = pack_pp(f(g_norm), f(b_ada), f(conv_w), f(conv_b), f(b_rgate), f(b_igate), f(lru_lambda), f(g_mix_out),
                 f(g_qk), f(b_fgate))
    shared = {"pp": pp, "w_ada": f(w_ada), "w_ffn_up": f(w_ffn_up), "w_ffn_down": f(w_ffn_down), "w_in": f(w_in),
              "w_rgate": f(w_rgate), "w_igate": f(w_igate), "w_out": f(w_out)}
    maps = []
    for i in range(ncores):
        xs = x[i * nseq_per_core:(i + 1) * nseq_per_core].reshape(nseq_per_core * S, D)
        cs = c[i * nseq_per_core:(i + 1) * nseq_per_core]
        cT = np.ascontiguousarray(cs.reshape(nseq_per_core, NCH, 128).transpose(2, 1, 0).reshape(128, NCH * nseq_per_core))
        m = dict(shared)
        m["x"] = np.ascontiguousarray(xs)
        m["cT"] = cT
        maps.append(m)
    return maps


_CACHE = {}


def kernel(**inputs):
    x = inputs["x"]
    B, S, _ = x.shape
    ns = B // NCORES
    key = (ns, S)
    if key not in _CACHE:
        _CACHE[key] = MK(ns, S)
    mk = _CACHE[key]
    maps = make_in_maps(ns, NCORES, **inputs)
    res = run_bass_kernel_spmd(mk.nc, maps, core_ids=list(range(NCORES)))
    out = np.stack([r["out"].reshape(ns, S, D) for r in res.results], axis=0).reshape(B, S, D)
    return out.astype(np.float32)
```

```python
from contextlib import ExitStack
import os
CUT = int(os.environ.get('CUT', '99'))
CUT2 = int(os.environ.get('CUT2', '99'))
import numpy as np
import concourse.bass as bass
import concourse.mybir as mybir
from concourse.bass_utils import run_bass_kernel_spmd

F32 = mybir.dt.float32
BF16 = mybir.dt.bfloat16
AF = mybir.ActivationFunctionType
ALU = mybir.AluOpType

ENG = ("pe", "act", "dve", "pool", "sp")

D = 1024
NCH = 8
DFF = 2816
NFF = 22
NIN = 2564
EPS = 1e-6
NCORES = 8


class Buf:
    __slots__ = ("name", "w", "r", "dk", "gd")

    def __init__(self, name):
        self.name = name
        self.w = None
        self.r = {}
        self.dk = None
        self.gd = ()


class Prog:
    def __init__(self, nc):
        self.nc = nc
        self.sems = {e: nc.alloc_semaphore("s_" + e) for e in ENG}
        self.val = {e: 0 for e in ENG}
        self.seen = {e: {} for e in ENG}
        self.q = {e: [] for e in ENG}
        self.ninst = 0

    def _need(self, e, reads, writes, skip_key=None):
        need = {}
        seen = self.seen[e]

        def add(k, v, raw):
            if k == e and e == "pe":
                return
            if seen.get(k, 0) >= v:
                return
            if need.get(k, 0) < v:
                need[k] = v

        for b in reads:
            if b.w is not None:
                add(b.w[0], b.w[1], True)
        for b in writes:
            if b.w is not None:
                if b.w[0] == skip_key:
                    for k, v in b.gd:
                        add(k, v, False)
                else:
                    add(b.w[0], b.w[1], False)
            for k, v in b.r.items():
                add(k, v, False)
        for k, v in need.items():
            self.q[e].append(("w", k, v))
            seen[k] = v

    def op(self, e, fn, reads=(), writes=()):
        self._need(e, reads, writes)
        self.val[e] += 1
        v = self.val[e]
        self.q[e].append(("i", fn))
        for b in reads:
            b.r[e] = v
        for b in writes:
            b.w = (e, v)
            b.r = {}
        self.ninst += 1

    def dma_key(self, buf):
        if buf.dk is None:
            buf.dk = "d_" + buf.name
            if buf.dk not in self.sems:
                self.sems[buf.dk] = self.nc.alloc_semaphore(buf.dk)
                self.val[buf.dk] = 0
        return buf.dk

    def dma(self, qe, out, in_, own, reads=(), writes=(), **kw):
        dk = self.dma_key(own)
        self._need(qe, reads, writes, skip_key=dk)
        self.val[dk] += 16
        v = self.val[dk]
        self.q[qe].append(("d", out, in_, dk, kw))
        for b in reads:
            b.r[dk] = v
        for b in writes:
            if b.w is None or b.w[0] != dk or b.r:
                b.gd = tuple(([b.w] if b.w is not None else []) + list(b.r.items()))
            b.w = (dk, v)
            b.r = {}
        self.ninst += 1

    def flush(self):
        for e in ENG:
            for k, v in self.val.items():
                if k != e and v > 0 and self.seen[e].get(k, 0) < v:
                    self.q[e].append(("w", k, v))
                    self.seen[e][k] = v
        nc = self.nc
        sems = self.sems
        with nc.Block() as blk:
            def mk(e):
                items = self.q[e]
                own = sems[e]

                def body(eng):
                    for it in items:
                        if it[0] == "w":
                            eng.wait_ge(sems[it[1]], it[2])
                        elif it[0] == "i":
                            it[1](eng).then_inc(own, 1)
                        else:
                            eng.dma_start(out=it[1], in_=it[2], **it[4]).then_inc(sems[it[3]], 16)
                return body
            blk.tensor(mk("pe"))
            blk.scalar(mk("act"))
            blk.vector(mk("dve"))
            blk.gpsimd(mk("pool"))
            blk.sync(mk("sp"))
        self.q = {e: [] for e in ENG}


PP_L = 142


def pp_off(l, name):
    offs = {"gn": 0, "bada": 24, "cw": 96, "cb": 112, "br": 116, "bi": 120, "lam": 124, "gmix": 128, "gqk": 136, "bf": 138}
    return l * PP_L + offs[name]


def pack_pp(g_norm, b_ada, conv_w, conv_b, b_rgate, b_igate, lru_lambda, g_mix_out, g_qk, b_fgate):
    pp = np.zeros((128, 2 * PP_L), np.float32)

    def fm(v):
        return np.ascontiguousarray(v.reshape(-1, 128).T)

    for l in range(2):
        o = l * PP_L
        for j in range(3):
            pp[:, o + j * 8:o + j * 8 + 8] = fm(g_norm[l, j])
        pp[:, o + 24:o + 96] = fm(b_ada[l])
        for k in range(4):
            pp[:, o + 96 + k * 4:o + 96 + k * 4 + 4] = fm(conv_w[l, k])
        pp[:, o + 112:o + 116] = fm(conv_b[l])
        pp[:, o + 116:o + 120] = fm(b_rgate[l])
        pp[:, o + 120:o + 124] = fm(b_igate[l])
        pp[:, o + 124:o + 128] = fm(lru_lambda[l])
        pp[:, o + 128:o + 136] = fm(g_mix_out[l])
        for i in range(2):
            pp[:, o + 136 + i] = np.concatenate([g_qk[l, i], g_qk[l, i]])
        pp[:, o + 138:o + 142] = b_fgate[l][None, :]
    return pp


class MK:
    def __init__(self, nseq, S, layers=(0, 1), dbg=None, phases=("ffn", "lru", "sb", "fox", "outp")):
        self.nseq, self.S = nseq, S
        self.T = T = nseq * S
        self.dbg = dbg
        nc = self.nc = bass.Bass("TRN2", target_bir_lowering=False)
        self.P = Prog(nc)
        di = lambda n, s: nc.dram_tensor(n, s, F32, kind="ExternalInput").ap()
        self.x_d = di("x", [T, D])
        self.cT_d = di("cT", [128, NCH * nseq])
        self.pp_d = di("pp", [128, 2 * PP_L])
        self.w_ada_d = di("w_ada", [2, D, 9 * D])
        self.w_up_d = di("w_ffn_up", [2, 2, D, 2 * DFF])
        self.w_dn_d = di("w_ffn_down", [2, 2, DFF, D])
        self.w_in_d = di("w_in", [2, D, NIN])
        self.w_rg_d = di("w_rgate", [2, 8, 64, 64])
        self.w_ig_d = di("w_igate", [2, 8, 64, 64])
        self.w_out_d = di("w_out", [2, D, D])
        self.out_d = nc.dram_tensor("out", [T, D], F32, kind="ExternalOutput").ap()
        self.xT_d = nc.dram_tensor("xT_scr", [NCH, 128, T], F32, kind="Internal").ap().rearrange("c p t -> p c t")
        self.yT_d = nc.dram_tensor("yT_scr", [NCH, 128, T], BF16, kind="Internal").ap().rearrange("c p t -> p c t")
        self.hT_d = nc.dram_tensor("hT_scr", [NCH, 128, T], BF16, kind="Internal").ap().rearrange("c p t -> p c t")
        self.uid = 0
        self.consts()
        for l in layers:
            if l == layers[0]:
                with ExitStack() as esw:
                    w = self.ffn_weights(esw, l, 0) if "ffn" in phases else None
                    self.ada_phase()
                    self.ffn_phase(l, 0, first=True, last=False, w=w)
            else:
                self.ffn_phase(l, 0, first=False, last=False)
            for s in range(nseq):
                if "lru" in phases:
                    self.lru_phase(l, s)
                if "sb" in phases:
                    self.sb_phase(l, s)
                if "fox" in phases:
                    self.fox_phase(l, s)
            with ExitStack() as esw:
                w = None
                if "outp" in phases:
                    w = self.ffn_weights(esw, l, 1, load=False)
                    with ExitStack() as eso:
                        self.outp_wo(eso, l)
                        self.ffn_weights_load(w, l, 1)
                        self.outp_phase(l)
                self.ffn_phase(l, 1, first=False, last=(l == layers[-1]), w=w)

    def nm(self, s):
        self.uid += 1
        return f"{s}_{self.uid}"

    def mm(self, out, lhsT, rhs, start, stop, reads, writes):
        self.P.op("pe", lambda e: e.matmul(out, lhsT, rhs, start=start, stop=stop), reads, writes)

    def tr(self, out, in_, ident, reads, writes):
        self.P.op("pe", lambda e: e.transpose(out, in_, ident), reads, writes)

    def act(self, out, in_, func, reads, writes, **kw):
        self.P.op("act", lambda e: e.activation(out=out, in_=in_, func=func, **kw), reads, writes)

    def tt(self, eng, out, in0, in1, op, reads, writes):
        self.P.op(eng, lambda e: e.tensor_tensor(out=out, in0=in0, in1=in1, op=op), reads, writes)

    def stt(self, out, in0, scalar, in1, op0, op1, reads, writes):
        self.P.op("dve", lambda e: e.scalar_tensor_tensor(out=out, in0=in0, scalar=scalar, in1=in1, op0=op0, op1=op1),
                  reads, writes)

    def ts(self, eng, out, in0, s1, s2, op0, op1, reads, writes):
        self.P.op(eng, lambda e: e.tensor_scalar(out=out, in0=in0, scalar1=s1, scalar2=s2, op0=op0, op1=op1), reads, writes)

    def cp(self, eng, out, in_, reads, writes):
        if eng == "act":
            self.act(out, in_, AF.Identity, reads, writes)
        else:
            self.P.op(eng, lambda e: e.tensor_copy(out=out, in_=in_), reads, writes)

    def memset(self, eng, ap, val, writes):
        self.P.op(eng, lambda e: e.memset(ap, val), (), writes)

    def recip(self, out, in_, reads, writes):
        self.P.op("dve", lambda e: e.reciprocal(out=out, in_=in_), reads, writes)

    def consts(self):
        nc, P = self.nc, self.P
        al = lambda n, s, d: nc.alloc_sbuf_tensor("c_" + n, s, d)
        self.pp = al("pp", [128, 2 * PP_L], F32)
        self.b_pp = Buf("pp")
        P.dma("sp", self.pp[:], self.pp_d, self.b_pp, writes=[self.b_pp])
        self.identf = al("identf", [128, 128], F32)
        self.ones16 = al("ones16", [128, 128], BF16)
        self.onesf = al("onesf", [128, 128], F32)
        self.bd16 = al("bd16", [128, 128], BF16)
        self.negU16 = al("negU16", [128, 128], BF16)
        self.Uf = al("Uf", [128, 128], F32)
        self.selL = al("selL", [128, 128], F32)
        self.mask_lt = al("mask_lt", [128, 4, 512], BF16)
        self.mask_le = al("mask_le", [128, 4, 512], BF16)
        self.b_c = Buf("consts")
        bc = [self.b_c]

        def sel(t, pattern, cmp, base, cm):
            P.op("pool", lambda g: g.affine_select(out=t, in_=t, pattern=pattern, compare_op=cmp, fill=0.0,
                                                   base=base, channel_multiplier=cm), bc, bc)
        self.memset("pool", self.identf[:], 1.0, bc)
        sel(self.identf[:], [[-1, 128]], ALU.is_equal, 0, 1)
        self.memset("pool", self.ones16[:], 1.0, bc)
        self.memset("pool", self.onesf[:], 1.0, bc)
        self.memset("pool", self.bd16[:], 0.0, bc)
        self.memset("pool", self.bd16[0:64, 0:64], 1.0, bc)
        self.memset("pool", self.bd16[64:128, 64:128], 1.0, bc)
        self.memset("pool", self.negU16[:], -1.0, bc)
        sel(self.negU16[:], [[-1, 128]], ALU.is_ge, 0, 1)
        self.memset("pool", self.Uf[:], 1.0, bc)
        sel(self.Uf[:], [[1, 128]], ALU.is_ge, 0, -1)
        self.memset("pool", self.selL[:], 1.0, bc)
        sel(self.selL[:], [[0, 128]], ALU.is_equal, -127, 1)
        self.memset("pool", self.mask_lt[:], 1.0, bc)
        self.memset("pool", self.mask_le[:], 1.0, bc)
        for o in range(4):
            sel(self.mask_lt[:, o, :], [[1, 512]], ALU.is_gt, -o * 128, -1)
            sel(self.mask_le[:, o, :], [[1, 512]], ALU.is_ge, -o * 128, -1)
        ns = self.nseq
        self.modT = al("modT", [128, 2, ns, 72], F32)
        self.Atab = al("Atab", [128, 2, ns, 24], F32)
        self.Gtab = al("Gtab", [128, 2, ns, 24], F32)
        self.cneg = al("cneg", [128, 2, 2, 4], F32)
        self.gq8 = al("gq8", [128, 2, 2], F32)
        self.b_mod = Buf("mod")

    def ppc(self, l, name, i=0, n=1):
        o = pp_off(l, name) + i
        return self.pp[:, o:o + n]

    def ada_phase(self):
        nc, P, ns = self.nc, self.P, self.nseq
        with ExitStack() as es:
            sb = lambda n, s, d: es.enter_context(nc.sbuf_tensor(self.nm(n), s, d))
            cT = sb("cT", [128, NCH * ns], F32)
            b_cT = Buf("cT")
            wb = [sb("wada", [128, NCH, 512], F32) for _ in range(2)]
            b_wb = [Buf("wada0"), Buf("wada1")]
            ps = es.enter_context(nc.psum_tensor(self.nm("adaps"), [128, 512], F32))
            b_ps = Buf("adaps")
            tmp = sb("adatmp", [128, 4], F32)
            P.dma("sp", cT[:], self.cT_d, b_cT, writes=[b_cT])
            self.act(cT[:], cT[:], AF.Silu, [b_cT], [b_cT])
            it = 0
            for l in range(2):
                for g in range(18):
                    w, bw = wb[it % 2], b_wb[it % 2]
                    it += 1
                    for k in range(NCH):
                        P.dma("sp" if k % 2 == 0 else "act", w[:, k, :],
                              self.w_ada_d[l, k * 128:(k + 1) * 128, g * 512:(g + 1) * 512], bw, writes=[bw])
                    for mm_ in range(4):
                        m = g * 4 + mm_
                        for k in range(NCH):
                            self.mm(ps[:, m * ns:(m + 1) * ns], w[:, k, mm_ * 128:(mm_ + 1) * 128],
                                    cT[:, k * ns:(k + 1) * ns], k == 0, k == NCH - 1, [bw, b_cT], [b_ps])
                bm = [self.b_mod]
                for s in range(ns):
                    self.tt("dve", self.modT[:, l, s, :], ps[:, s:72 * ns:ns], self.ppc(l, "bada", 0, 72), ALU.add,
                            [b_ps, self.b_pp], bm)
                    for j in range(3):
                        self.stt(self.Atab[:, l, s, j * 8:(j + 1) * 8], self.modT[:, l, s, j * 24 + 8:j * 24 + 16], 1.0,
                                 self.ppc(l, "gn", j * 8, 8), ALU.add, ALU.mult, bm + [self.b_pp], bm)
                        self.ts("dve", self.Gtab[:, l, s, j * 8:(j + 1) * 8], self.modT[:, l, s, j * 24 + 16:j * 24 + 24],
                                1.0, 1.0 if j == 1 else 0.5, ALU.add, ALU.mult, bm, bm)
                self.act(tmp[:], self.ppc(l, "lam", 0, 4), AF.Exp, [self.b_pp], bm, scale=-1.0)
                self.act(tmp[:], tmp[:], AF.Ln, bm, bm, bias=1.0)
                self.ts("dve", self.cneg[:, l, 0, :], tmp[:], -8.0, None, ALU.mult, ALU.bypass, bm, bm)
                self.ts("dve", self.cneg[:, l, 1, :], tmp[:], -16.0, None, ALU.mult, ALU.bypass, bm, bm)
                self.ts("dve", self.gq8[:, l, 0:1], self.ppc(l, "gqk", 0, 1), 0.125, None, ALU.mult, ALU.bypass,
                        [self.b_pp], bm)
                self.cp("dve", self.gq8[:, l, 1:2], self.ppc(l, "gqk", 1, 1), [self.b_pp], bm)
            if self.dbg == "ada":
                self.dbg_out("dbg_mod", self.modT[:].rearrange("p a b c -> p (a b c)"), [128, 2 * ns * 72], [self.b_mod])
            P.flush()

    def dbg_out(self, name, ap, shape, reads, dt=F32):
        d = self.nc.dram_tensor(name, shape, dt, kind="ExternalOutput").ap()
        b = Buf(self.nm("dbg"))
        self.P.dma("sp", d, ap, b, reads=reads)

    def norm_a(self, xt, b_xt, sq, b_sq):
        self.act(sq[:], xt[:], AF.Square, b_xt, [b_sq])

    def norm_b(self, xt, b_xt, h, b_h, sq, b_sq, ps, b_psb, rs, b_rs, tmp, b_tmp, l, j, s, NT):
        for c in range(NCH):
            self.mm(ps[:, 0:NT], self.ones16[:], sq[:, c, :], c == 0, c == NCH - 1, [b_sq, self.b_c], [b_psb])
        self.act(rs[:], ps[:, 0:NT], AF.Ln, [b_psb], [b_rs], scale=1.0 / D, bias=EPS)
        self.act(rs[:], rs[:], AF.Exp, [b_rs], [b_rs], scale=-0.5)
        for c in range(NCH):
            t, bt = tmp[c % 2], b_tmp[c % 2]
            self.stt(t[:], xt[:, c, :], self.Atab[:, l, s, j * 8 + c:j * 8 + c + 1], rs[:], ALU.mult, ALU.mult,
                     [b_xt[c], b_rs, self.b_mod], [bt])
            self.act(h[:, c, :], t[:], AF.Identity, [bt, self.b_mod], [b_h[c]],
                     bias=self.modT[:, l, s, j * 24 + c:j * 24 + c + 1])

    def norm_mod(self, xt, b_xt, h, b_h, sq, b_sq, ps, b_psb, rs, b_rs, tmp, b_tmp, l, j, s, NT):
        self.norm_a(xt, b_xt, sq, b_sq)
        self.norm_b(xt, b_xt, h, b_h, sq, b_sq, ps, b_psb, rs, b_rs, tmp, b_tmp, l, j, s, NT)

    def load_x(self, xt, b_xt, tok0, NT, first, xtok=None, b_xtok=None, pst=None, b_pst=None):
        P = self.P
        if not first:
            P.dma("sp", xt[:], self.xT_d[:, :, tok0:tok0 + NT], b_xt[0], writes=b_xt)
            return
        for g in range(NT // 128):
            xk, bk = xtok[g % 2], b_xtok[g % 2]
            P.dma("sp", xk[:], self.x_d[tok0 + g * 128:tok0 + (g + 1) * 128, :], bk, writes=[bk])
            for c in range(NCH):
                pt, bpt = pst[c % 2], b_pst[c % 2]
                self.tr(pt[:, 0:128], xk[:, c * 128:(c + 1) * 128], self.identf[:], [bk, self.b_c], [bpt])
                self.cp("act" if c % 2 == 0 else "dve", xt[:, c, g * 128:(g + 1) * 128], pt[:, 0:128], [bpt], [b_xt[c]])

    def store_x(self, xo, b_xo, tok0, NT, last, xtok=None, b_xtok=None, pst=None, b_pst=None):
        P = self.P
        if not last:
            P.dma("sp", self.xT_d[:, :, tok0:tok0 + NT], xo[:], b_xo[0], reads=b_xo)
            return
        for g in range(NT // 128):
            xk, bk = xtok[g % 2], b_xtok[g % 2]
            for hh in range(2):
                pt, bpt = pst[hh], b_pst[hh]
                for cc in range(4):
                    c = hh * 4 + cc
                    self.tr(pt[:, cc * 128:(cc + 1) * 128], xo[:, c, g * 128:(g + 1) * 128], self.identf[:],
                            [b_xo[c], self.b_c], [bpt])
                self.cp("act" if hh == 0 else "dve", xk[:, hh * 512:(hh + 1) * 512], pt[:], [bpt], [bk])
            P.dma("sp", self.out_d[tok0 + g * 128:tok0 + (g + 1) * 128, :], xk[:], bk, reads=[bk])

    def ffn_weights(self, es, l, f, load=True):
        nc = self.nc
        wup = es.enter_context(nc.sbuf_tensor(self.nm("wup"), [128, NCH, 2 * DFF], BF16))
        wdn = es.enter_context(nc.sbuf_tensor(self.nm("wdn"), [128, NFF, D], BF16))
        w = (wup, wdn, Buf("wup"), Buf("wdn"))
        if load:
            self.ffn_weights_load(w, l, f)
        return w

    def ffn_weights_load(self, w, l, f):
        wup, wdn, b_wup, b_wdn = w
        for k in range(NCH):
            self.P.dma("pool", wup[:, k, :], self.w_up_d[l, f, k * 128:(k + 1) * 128, :], b_wup, writes=[b_wup],
                       max_dma_last_dim=4096)
        for k in range(NFF):
            self.P.dma("pool", wdn[:, k, :], self.w_dn_d[l, f, k * 128:(k + 1) * 128, :], b_wdn, writes=[b_wdn],
                       max_dma_last_dim=4096)

    def ffn_phase(self, l, f, first, last, w=None):
        nc, P = self.nc, self.P
        NT = 256
        j = 0 if f == 0 else 2
        with ExitStack() as es:
            sb = lambda n, s, d: es.enter_context(nc.sbuf_tensor(self.nm(n), s, d))
            wup, wdn, b_wup, b_wdn = w if w is not None else self.ffn_weights(es, l, f)
            NS = 2
            xt = [sb("xt", [128, NCH, NT], F32) for _ in range(NS)]
            b_xt = [[Buf(f"xt{i}_{c}") for c in range(NCH)] for i in range(NS)]
            hh = [sb("h", [128, NCH, NT], BF16) for _ in range(2)]
            b_hh = [[Buf(f"h{i}_{c}") for c in range(NCH)] for i in range(2)]
            sqq = [sb("sq", [128, NCH, NT], BF16) for _ in range(2)]
            b_sqq = [Buf("sq0"), Buf("sq1")]
            rss = [sb("rs", [128, NT], F32) for _ in range(2)]
            b_rss = [Buf("rs0"), Buf("rs1")]
            tmp = [sb("tmp", [128, NT], F32) for _ in range(2)]
            b_tmp = [Buf("tmp0"), Buf("tmp1")]
            sg = [sb("sg", [128, NT], F32) for _ in range(2)]
            b_sg = [Buf("sg0"), Buf("sg1")]
            a = sb("a", [128, NFF, NT], BF16)
            b_a = [Buf(f"a{k}") for k in range(NFF)]
            ps = [es.enter_context(nc.psum_tensor(self.nm("ps"), [128, 512], F32)) for _ in range(8)]
            b_ps = [Buf(f"ps{i}") for i in range(8)]
            xtok = b_xtok = None
            if first or last:
                xtok = [sb("xtok", [128, D], F32) for _ in range(2)]
                b_xtok = [Buf("xtok0"), Buf("xtok1")]
            pst, b_pst = [ps[7], ps[0]], [b_ps[7], b_ps[0]]
            ntl = self.T // NT

            def prep_a(ti):
                self.load_x(xt[ti % NS], b_xt[ti % NS], ti * NT, NT, first, xtok, b_xtok, pst, b_pst)
                self.norm_a(xt[ti % NS], b_xt[ti % NS], sqq[ti % 2], b_sqq[ti % 2])

            def prep_b(ti):
                self.norm_b(xt[ti % NS], b_xt[ti % NS], hh[ti % 2], b_hh[ti % 2], sqq[ti % 2], b_sqq[ti % 2], ps[0], b_ps[0],
                            rss[ti % 2], b_rss[ti % 2], tmp, b_tmp, l, j, (ti * NT) // self.S, NT)

            prep_a(0)
            prep_b(0)
            for ti in range(ntl):
                tok0 = ti * NT
                s = tok0 // self.S
                x, bx = xt[ti % NS], b_xt[ti % NS]
                h, b_h = hh[ti % 2], b_hh[ti % 2]
                for jj in range(NFF):
                    pg, pu = ps[1 + 2 * (jj % 2)], ps[2 + 2 * (jj % 2)]
                    bg, bu = b_ps[1 + 2 * (jj % 2)], b_ps[2 + 2 * (jj % 2)]
                    for k in range(NCH):
                        self.mm(pg[:, 0:NT], wup[:, k, jj * 128:(jj + 1) * 128], h[:, k, :], k == 0, k == NCH - 1,
                                [b_wup, b_h[k]], [bg])
                    for k in range(NCH):
                        self.mm(pu[:, 0:NT], wup[:, k, DFF + jj * 128:DFF + (jj + 1) * 128], h[:, k, :], k == 0,
                                k == NCH - 1, [b_wup, b_h[k]], [bu])
                    self.act(sg[jj % 2][:], pg[:, 0:NT], AF.Silu, [bg], [b_sg[jj % 2]])
                    self.tt("dve", a[:, jj, :], sg[jj % 2][:], pu[:, 0:NT], ALU.mult, [b_sg[jj % 2], bu], [b_a[jj]])
                    if jj == 12 and ti + 1 < ntl:
                        prep_a(ti + 1)
                if ti + 1 < ntl:
                    prep_b(ti + 1)
                for c in range(NCH):
                    pd, bd = ps[5 + c % 2], b_ps[5 + c % 2]
                    for k in range(NFF):
                        self.mm(pd[:, 0:NT], wdn[:, k, c * 128:(c + 1) * 128], a[:, k, :], k == 0, k == NFF - 1,
                                [b_wdn, b_a[k]], [bd])
                    self.stt(x[:, c, :], pd[:, 0:NT], self.Gtab[:, l, s, j * 8 + c:j * 8 + c + 1], x[:, c, :],
                             ALU.mult, ALU.add, [bd, bx[c], self.b_mod], [bx[c]])
                self.store_x(x, bx, tok0, NT, last, xtok, b_xtok, pst, b_pst)
            P.flush()

    def tile_bufs(self, sb, NT, nslots=2, h_only=False):
        o = {}
        if h_only:
            o["h"] = [sb("h", [128, NCH, NT], BF16) for _ in range(2)]
            o["b_h"] = [[Buf(f"h{i}_{c}") for c in range(NCH)] for i in range(2)]
            return o
        o["xt"] = [sb("xt", [128, NCH, NT], F32) for _ in range(nslots)]
        o["b_xt"] = [[Buf(f"xt{i}_{c}") for c in range(NCH)] for i in range(nslots)]
        o["h"] = [sb("h", [128, NCH, NT], BF16) for _ in range(2)]
        o["b_h"] = [[Buf(f"h{i}_{c}") for c in range(NCH)] for i in range(2)]
        o["sq"] = [sb("sq", [128, NCH, NT], BF16) for _ in range(2)]
        o["b_sq"] = [Buf("sq0"), Buf("sq1")]
        o["rs"] = [sb("rs", [128, NT], F32) for _ in range(2)]
        o["b_rs"] = [Buf("rs0"), Buf("rs1")]
        o["tmp"] = [sb("tmp", [128, NT], F32) for _ in range(2)]
        o["b_tmp"] = [Buf("tmp0"), Buf("tmp1")]
        return o

    def tile_h(self, o, ti, tok0, NT, ps0, b_ps0, l, s):
        x, bx = o["xt"][ti % 2], o["b_xt"][ti % 2]
        self.load_x(x, bx, tok0, NT, False)
        i = ti % 2
        self.norm_mod(x, bx, o["h"][i], o["b_h"][i], o["sq"][i], o["b_sq"][i], ps0, b_ps0, o["rs"][i], o["b_rs"][i],
                      o["tmp"], o["b_tmp"], l, 1, s, NT)
        return o["h"][i], o["b_h"][i]

    def tile_a(self, o, ti, tok0, NT):
        x, bx = o["xt"][ti % 2], o["b_xt"][ti % 2]
        self.load_x(x, bx, tok0, NT, False)
        self.norm_a(x, bx, o["sq"][ti % 2], o["b_sq"][ti % 2])

    def tile_b(self, o, ti, NT, ps0, b_ps0, l, s):
        i = ti % 2
        self.norm_b(o["xt"][i], o["b_xt"][i], o["h"][i], o["b_h"][i], o["sq"][i], o["b_sq"][i], ps0, b_ps0, o["rs"][i],
                    o["b_rs"][i], o["tmp"], o["b_tmp"], l, 1, s, NT)

    def store_h(self, o, ti, tok0, NT):
        i = ti % 2
        self.P.dma("act", self.hT_d[:, :, tok0:tok0 + NT], o["h"][i][:], o["b_h"][i][0], reads=o["b_h"][i])

    def load_h(self, o, ti, tok0, NT):
        i = ti % 2
        self.P.dma("sp", o["h"][i][:], self.hT_d[:, :, tok0:tok0 + NT], o["b_h"][i][0], writes=o["b_h"][i])

    def load_w_in(self, w, b_w, l, c0, c1):
        for k in range(NCH):
            self.P.dma("pool", w[:, k, :], self.w_in_d[l, k * 128:(k + 1) * 128, c0:c1], b_w, writes=[b_w],
                       max_dma_last_dim=4096)

    def group_rms_store(self, y, b_y, nchk, gm0, l, tok0, NT, sqy, b_sqy, ps, b_ps, rs2, b_rs2, yn, b_yn, ych0):
        self.act(sqy[:], y[:], AF.Square, b_y, [b_sqy])
        for c in range(nchk):
            self.mm(ps[:, 0:NT], self.ones16[:], sqy[:, c, :], c == 0, c == nchk - 1, [b_sqy, self.b_c], [b_ps])
        self.act(rs2[:], ps[:, 0:NT], AF.Ln, [b_ps], [b_rs2], scale=1.0 / (nchk * 128), bias=EPS)
        self.act(rs2[:], rs2[:], AF.Exp, [b_rs2], [b_rs2], scale=-0.5)
        for c in range(nchk):
            self.stt(yn[:, c, :], y[:, c, :], self.ppc(l, "gmix", gm0 + c, 1), rs2[:], ALU.mult, ALU.mult,
                     [b_y[c], b_rs2, self.b_pp], [b_yn])
        self.P.dma("sp", self.yT_d[:, ych0:ych0 + nchk, tok0:tok0 + NT], yn[:], b_yn, reads=[b_yn])

    def lru_phase(self, l, s):
        nc, P = self.nc, self.P
        NT = 512
        with ExitStack() as es:
            sb = lambda n, sh, d: es.enter_context(nc.sbuf_tensor(self.nm(n), sh, d))
            win = sb("win", [128, NCH, 1024], BF16)
            b_win = Buf("win")
            self.load_w_in(win, b_win, l, 0, 1024)
            wg = [sb("wg", [128, 4, 128], F32) for _ in range(2)]
            b_wg = [Buf("wg0"), Buf("wg1")]
            for gi, wd in enumerate((self.w_rg_d, self.w_ig_d)):
                self.memset("pool", wg[gi][:], 0.0, [b_wg[gi]])
                src = wd[l].rearrange("(c two) i j -> two i c j", two=2)
                P.dma("sp", wg[gi][0:64, :, 0:64], src[0], b_wg[gi], writes=[b_wg[gi]])
                P.dma("sp", wg[gi][64:128, :, 64:128], src[1], b_wg[gi], writes=[b_wg[gi]])
            o = self.tile_bufs(sb, NT)
            ps = [es.enter_context(nc.psum_tensor(self.nm("ps"), [128, 512], F32)) for _ in range(8)]
            b_ps = [Buf(f"ps{i}") for i in range(8)]
            lx = [sb("lx", [128, NT + 3], F32) for _ in range(4)]
            b_lx = [Buf(f"lx{c}") for c in range(4)]
            carry = sb("carry", [128, 4], F32)
            b_carry = [Buf(f"carry{c}") for c in range(4)]
            y = sb("y", [128, 4, NT], F32)
            b_y = [Buf(f"y{c}") for c in range(4)]
            f32t = lambda n: (sb(n, [128, NT], F32), Buf(n))
            f32q = lambda n: [f32t(n + str(i)) for i in range(4)]
            gs_, u_, r_, ig_, a_, m_ = (f32q(n) for n in ("gs", "u", "r", "ig", "a", "m"))
            hs_ = r_
            t1_2 = [f32q("t1a"), f32q("t1b")]
            y2 = [y, sb("y2", [128, 4, NT], F32)]
            b_y2 = [b_y, [Buf(f"y2{c}") for c in range(4)]]
            sqy = sb("sqy", [128, 4, NT], BF16)
            b_sqy = Buf("sqy")
            rs2, b_rs2 = f32t("rs2")
            yn = sb("yn", [128, 4, NT], BF16)
            b_yn = Buf("yn")
            cw = lambda k, c: self.ppc(l, "cw", k * 4 + c, 1)
            C4 = range(4)
            ntl = self.S // NT
            S0 = s * self.S

            def head(ti):
                h, b_h = o["h"][ti % 2], o["b_h"][ti % 2]
                t1_ = t1_2[ti % 2]
                for c in C4:
                    pl, bl = ps[1 + c % 2], b_ps[1 + c % 2]
                    for k in range(NCH):
                        self.mm(pl[:], win[:, k, c * 128:(c + 1) * 128], h[:, k, :], k == 0, k == NCH - 1,
                                [b_win, b_h[k]], [bl])
                    if ti == 0:
                        self.memset("pool", lx[c][:, 0:3], 0.0, [b_lx[c]])
                    else:
                        self.cp("pool", lx[c][:, 0:3], lx[c][:, NT:NT + 3], [b_lx[c]], [b_lx[c]])
                    self.cp("act", lx[c][:, 3:NT + 3], pl[:], [bl], [b_lx[c]])
                if ti + 1 < ntl:
                    self.tile_a(o, ti + 1, S0 + (ti + 1) * NT, NT)
                for c in C4:
                    pg, bg = ps[3 + c % 2], b_ps[3 + c % 2]
                    for k in range(NCH):
                        self.mm(pg[:], win[:, k, 512 + c * 128:512 + (c + 1) * 128], h[:, k, :], k == 0, k == NCH - 1,
                                [b_win, b_h[k]], [bg])
                    self.cp("act", gs_[c][0][:], pg[:], [bg], [gs_[c][1]])
                    self.act(t1_[c][0][:], pg[:], AF.Square, [bg], [t1_[c][1]])
                if ti + 1 < ntl:
                    self.tile_b(o, ti + 1, NT, ps[0], b_ps[0], l, s)
                    self.store_h(o, ti + 1, S0 + (ti + 1) * NT, NT)
                for c in C4:
                    u, b_u = u_[c]
                    self.act(u[:], lx[c][:, 0:NT], AF.Identity, [b_lx[c], self.b_pp], [b_u], scale=cw(0, c),
                             bias=self.ppc(l, "cb", c, 1))
                    for k in range(1, 4):
                        self.stt(u[:], lx[c][:, k:k + NT], cw(k, c), u[:], ALU.mult, ALU.add, [b_lx[c], b_u, self.b_pp], [b_u])

            def st_a(ti):
                t1_ = t1_2[ti % 2]
                for c in C4:
                    u, b_u = u_[c]
                    pr, br_ = ps[1 + c], b_ps[1 + c]
                    pi, bi_ = ps[5 + c % 3], b_ps[5 + c % 3]
                    self.mm(pr[:], wg[0][:, c, :], u[:], True, True, [b_wg[0], b_u], [br_])
                    self.mm(pi[:], wg[1][:, c, :], u[:], True, True, [b_wg[1], b_u], [bi_])
                    self.act(r_[c][0][:], pr[:], AF.Sigmoid, [br_, self.b_pp], [r_[c][1]], bias=self.ppc(l, "br", c, 1))
                    self.act(ig_[c][0][:], pi[:], AF.Sigmoid, [bi_, self.b_pp], [ig_[c][1]], bias=self.ppc(l, "bi", c, 1))
                for c in C4:
                    t1, b_t1 = t1_[c]
                    self.ts("dve", t1[:], t1[:], 0.044715, 1.0, ALU.mult, ALU.add, [b_t1], [b_t1])
                    self.tt("pool", t1[:], t1[:], gs_[c][0][:], ALU.mult, [b_t1, gs_[c][1]], [b_t1])
                for c in C4:
                    t1, b_t1 = t1_[c]
                    self.act(t1[:], t1[:], AF.Sigmoid, [b_t1], [b_t1], scale=1.5957691216057308)
                for c in C4:
                    t1, b_t1 = t1_[c]
                    self.tt("pool", t1[:], t1[:], gs_[c][0][:], ALU.mult, [b_t1, gs_[c][1]], [b_t1])
                for c in C4:
                    r, b_r = r_[c]
                    self.act(a_[c][0][:], r[:], AF.Exp, [b_r, self.b_mod], [a_[c][1]], scale=self.cneg[:, l, 0, c:c + 1])
                    self.act(m_[c][0][:], r[:], AF.Exp, [b_r, self.b_mod], [m_[c][1]], scale=self.cneg[:, l, 1, c:c + 1])
                    ig, b_ig = ig_[c]
                    self.tt("pool", ig[:], ig[:], u_[c][0][:], ALU.mult, [b_ig, u_[c][1]], [b_ig])

            def st_d(ti):
                t1_ = t1_2[ti % 2]
                yy, b_yy = y2[ti % 2], b_y2[ti % 2]
                for c in C4:
                    self.act(m_[c][0][:], m_[c][0][:], AF.Sqrt, [m_[c][1]], [m_[c][1]], scale=-1.0, bias=1.0)
                    ig, b_ig = ig_[c]
                    self.tt("dve", ig[:], ig[:], m_[c][0][:], ALU.mult, [b_ig, m_[c][1]], [b_ig])
                for c in C4:
                    hs, b_hs = hs_[c]
                    init = 0.0 if ti == 0 else carry[:, c:c + 1]
                    self.P.op("dve", lambda e, init=init, hs=hs, a=a_[c][0], ig=ig_[c][0]: e.tensor_tensor_scan(
                        out=hs[:], data0=a[:], data1=ig[:], initial=init, op0=ALU.mult, op1=ALU.add),
                              [a_[c][1], ig_[c][1], b_carry[c]], [b_hs])
                    self.cp("pool", carry[:, c:c + 1], hs[:, NT - 1:NT], [b_hs], [b_carry[c]])
                    self.tt("pool", yy[:, c, :], hs[:], t1_[c][0][:], ALU.mult, [b_hs, t1_[c][1]], [b_yy[c]])

            def st_b(ti):
                self.group_rms_store(y2[ti % 2], b_y2[ti % 2], 4, 0, l, S0 + ti * NT, NT, sqy, b_sqy, ps[0], b_ps[0],
                                     rs2, b_rs2, yn, b_yn, 0)

            self.tile_a(o, 0, S0, NT)
            self.tile_b(o, 0, NT, ps[0], b_ps[0], l, s)
            self.store_h(o, 0, S0, NT)
            head(0)
            for ti in range(ntl):
                st_a(ti)
                if ti > 0:
                    st_b(ti - 1)
                if ti + 1 < ntl:
                    head(ti + 1)
                st_d(ti)
            st_b(ntl - 1)
            P.flush()

    def sb_phase(self, l, s):
        nc, P, S = self.nc, self.P, self.S
        NT = 512
        NKT = S // 128
        with ExitStack() as es:
            sb = lambda n, sh, d: es.enter_context(nc.sbuf_tensor(self.nm(n), sh, d))
            win = sb("win", [128, NCH, 768], BF16)
            b_win = Buf("win")
            self.load_w_in(win, b_win, l, 1024, 1792)
            o = self.tile_bufs(sb, NT, h_only=True)
            ps = [es.enter_context(nc.psum_tensor(self.nm("ps"), [128, 512], F32)) for _ in range(8)]
            b_ps = [Buf(f"ps{i}") for i in range(8)]
            qh = sb("qh", [128, 4, S], BF16)
            kT = sb("kT", [128, 2, S], BF16)
            v = sb("v", [128, NKT, 256], BF16)
            nsp = S // 512
            b_q = [[Buf(f"q{c}_{i}") for i in range(nsp)] for c in range(2)]
            b_k = [[Buf(f"k{c}_{i}") for i in range(nsp)] for c in range(2)]
            b_v = [Buf(f"v{i}") for i in range(NKT)]
            for hq in range(4):
                self.memset("dve", qh[:, hq, :], 0.0, [b for bb in b_q for b in bb])
            ntl = S // NT
            self.load_h(o, 0, s * S, NT)
            for ti in range(ntl):
                tok0 = s * S + ti * NT
                h, b_h = o["h"][ti % 2], o["b_h"][ti % 2]
                if ti + 1 < ntl:
                    self.load_h(o, ti + 1, tok0 + NT, NT)
                for c in range(2):
                    pq, bq = ps[1 + c], b_ps[1 + c]
                    for k in range(NCH):
                        self.mm(pq[:], win[:, k, c * 128:(c + 1) * 128], h[:, k, :], k == 0, k == NCH - 1, [b_win, b_h[k]], [bq])
                    self.act(qh[0:64, 2 * c, ti * NT:(ti + 1) * NT], pq[0:64, :], AF.Identity, [bq], [b_q[c][ti]], scale=0.125)
                    self.act(qh[64:128, 2 * c + 1, ti * NT:(ti + 1) * NT], pq[64:128, :], AF.Identity, [bq], [b_q[c][ti]],
                             scale=0.125)
                    pk, bk = ps[3 + c], b_ps[3 + c]
                    for k in range(NCH):
                        self.mm(pk[:], win[:, k, 256 + c * 128:256 + (c + 1) * 128], h[:, k, :], k == 0, k == NCH - 1,
                                [b_win, b_h[k]], [bk])
                    self.cp("dve", kT[:, c, ti * NT:(ti + 1) * NT], pk[:], [bk], [b_k[c][ti]])
                for g in range(4):
                    pv, bv = ps[5 + g % 2], b_ps[5 + g % 2]
                    for k in range(NCH):
                        self.mm(pv[:, 0:256], h[:, k, g * 128:(g + 1) * 128], win[:, k, 512:768], k == 0, k == NCH - 1,
                                [b_win, b_h[k]], [bv])
                    self.cp("act" if g % 2 == 0 else "dve", v[:, ti * 4 + g, :], pv[:, 0:256], [bv], [b_v[ti * 4 + g]])
            f32t = lambda n: (sb(n, [128, 512], F32), Buf(n))
            R = [f32t("R0"), f32t("R1")]
            E = [f32t("E0"), f32t("E1")]
            Tt = [f32t("T0"), f32t("T1")]
            SP = [(sb(f"sp{i}", [128, 512], BF16), Buf(f"sp{i}")) for i in range(3)]
            W = [(sb(f"w{i}", [128, 512], BF16), Buf(f"w{i}")) for i in range(3)]
            ysb = sb("ysb", [128, 2, 512], F32)
            b_ysb = [Buf("ysb0"), Buf("ysb1")]
            sqy = sb("sqy", [128, 2, 512], BF16)
            b_sqy = Buf("sqy")
            rs2, b_rs2 = f32t("rs2")
            yn = sb("yn", [128, 2, 512], BF16)
            b_yn = Buf("yn")
            pairs = []
            grp = 0
            for qs in range(nsp):
                njt = 4 * qs + 4
                for hd in range(4):
                    for idx, jt in enumerate(range(njt - 1, -1, -1)):
                        pairs.append((qs, hd, idx, jt, njt, grp))
                    grp += 1
            NP = len(pairs)

            def geo(i):
                qs, hd, idx, jt, njt, g = pairs[i]
                c, po = hd // 2, (hd % 2) * 64
                return qs, hd, idx, jt, njt, g, c, po, jt >= 4 * qs, jt - 4 * qs

            def s1(i):
                qs, hd, idx, jt, njt, g, c, po, diag, od = geo(i)
                pA, bA = ps[1 + i % 2], b_ps[1 + i % 2]
                self.mm(pA[:], kT[:, c, jt * 128:(jt + 1) * 128], qh[:, hd, qs * 512:(qs + 1) * 512],
                        True, True, [b_q[c][qs], b_k[c][jt // 4]], [bA])

            def s2(i):
                qs, hd, idx, jt, njt, g, c, po, diag, od = geo(i)
                pA, bA = ps[1 + i % 2], b_ps[1 + i % 2]
                Eb, bE = E[i % 2]
                sp, bsp = SP[i % 3]
                self.act(Eb[:], pA[:], AF.Exp, [bA], [bE])
                self.act(sp[:], Eb[:], AF.Ln, [bE], [bsp], bias=1.0)
                if diag:
                    self.tt("pool", sp[:], sp[:], self.mask_lt[:, od, :], ALU.mult, [bsp, self.b_c], [bsp])

            def s3(i):
                qs, hd, idx, jt, njt, g, c, po, diag, od = geo(i)
                pB, bB = ps[3 + i % 2], b_ps[3 + i % 2]
                sp, bsp = SP[i % 3]
                self.mm(pB[:], kT[:, c, jt * 128:(jt + 1) * 128], qh[:, hd, qs * 512:(qs + 1) * 512],
                        True, False, [b_q[c][qs], b_k[c][jt // 4]], [bB])
                self.mm(pB[:], self.negU16[:], sp[:], False, True, [bsp, self.b_c], [bB])
                if idx < njt - 1:
                    pC, bC = ps[5], b_ps[5]
                    self.mm(pC[:], self.ones16[:], sp[:], True, True, [bsp, self.b_c], [bC])

            def s4(i):
                qs, hd, idx, jt, njt, g, c, po, diag, od = geo(i)
                pB, bB = ps[3 + i % 2], b_ps[3 + i % 2]
                Rb, bR = R[g % 2]
                Tb, bT = Tt[i % 2]
                w, bw = W[i % 3]
                if idx == 0:
                    self.act(w[:], pB[:], AF.Exp, [bB], [bw])
                else:
                    self.tt("dve", Tb[:], pB[:], Rb[:], ALU.subtract, [bB, bR], [bT])
                    self.act(w[:], Tb[:], AF.Exp, [bT], [bw])
                if diag:
                    self.tt("pool", w[:], w[:], self.mask_lt[:, od, :], ALU.mult, [bw, self.b_c], [bw])
                if idx < njt - 1:
                    pC, bC = ps[5], b_ps[5]
                    if idx == 0:
                        self.cp("dve", Rb[:], pC[:], [bC], [bR])
                    else:
                        self.tt("dve", Rb[:], pC[:], Rb[:], ALU.add, [bC, bR], [bR])

            def s5(i):
                qs, hd, idx, jt, njt, g, c, po, diag, od = geo(i)
                w, bw = W[i % 3]
                pO, bO = ps[6 + g % 2], b_ps[6 + g % 2]
                self.mm(pO[:], v[:, jt, c * 128:(c + 1) * 128], w[:], idx == 0, idx == njt - 1, [b_v[jt], bw], [bO])
                if idx == njt - 1:
                    self.cp("act", ysb[po:po + 64, c, :], pO[po:po + 64, :], [bO], [b_ysb[c]])
                    if hd == 3:
                        self.group_rms_store(ysb, b_ysb, 2, 4, l, s * S + qs * 512, 512, sqy, b_sqy, ps[0], b_ps[0],
                                             rs2, b_rs2, yn, b_yn, 4)

            for n in range(NP + 3):
                if n < NP:
                    s1(n)
                    s2(n)
                if 0 <= n - 2 < NP:
                    s4(n - 2)
                if 0 <= n - 1 < NP:
                    s3(n - 1)
                if 0 <= n - 3 < NP:
                    s5(n - 3)
            P.flush()

    def fox_phase(self, l, s):
        nc, P, S = self.nc, self.P, self.S
        NT = 512
        NKT = S // 128
        with ExitStack() as es:
            sb = lambda n, sh, d: es.enter_context(nc.sbuf_tensor(self.nm(n), sh, d))
            win = sb("win", [128, NCH, 772], BF16)
            b_win = Buf("win")
            self.load_w_in(win, b_win, l, 1792, 2564)
            o = self.tile_bufs(sb, NT, h_only=True)
            ps = [es.enter_context(nc.psum_tensor(self.nm("ps"), [128, 512], F32)) for _ in range(8)]
            b_ps = [Buf(f"ps{i}") for i in range(8)]
            qh = sb("qh", [128, 4, S], BF16)
            kT = sb("kT", [128, 2, S], BF16)
            va = sb("va", [128, NKT, 4, 128], BF16)
            Fn = sb("Fn", [128, NKT, 4], F32)
            Fc = sb("Fc", [128, NKT, 4], F32)
            nsp = S // 512
            b_q = [[Buf(f"q{c}_{i}") for i in range(nsp)] for c in range(2)]
            b_k = [[Buf(f"k{c}_{i}") for i in range(nsp)] for c in range(2)]
            b_v = [Buf(f"v{i}") for i in range(NKT)]
            b_F = Buf("F")
            f32t = lambda n: (sb(n, [128, 512], F32), Buf(n))
            sq16 = [(sb(f"sq16{i}", [128, 512], BF16), Buf(f"sq16{i}")) for i in range(4)]
            rsq = [f32t(f"rsq{i}") for i in range(4)]
            tfs = [(sb(f"tf{i}", [128, 4], F32), Buf(f"tf{i}")) for i in range(4)]
            for i in range(NKT):
                self.memset("dve" if i % 2 == 0 else "pool", va[:, i, :, :], 1.0, [b_v[i]])
            for hq in range(4):
                self.memset("dve", qh[:, hq, :], 0.0, [b for bb in b_q for b in bb])
            ntl = S // NT
            self.load_h(o, 0, s * S, NT)
            for ti in range(ntl):
                tok0 = s * S + ti * NT
                h, b_h = o["h"][ti % 2], o["b_h"][ti % 2]
                if ti + 1 < ntl:
                    self.load_h(o, ti + 1, tok0 + NT, NT)
                combos = [(0, 0), (0, 1), (1, 0), (1, 1)]
                for i2, (qk, c) in enumerate(combos):
                    pq, bq = ps[1 + i2], b_ps[1 + i2]
                    for k in range(NCH):
                        self.mm(pq[:], win[:, k, qk * 256 + c * 128:qk * 256 + (c + 1) * 128], h[:, k, :], k == 0,
                                k == NCH - 1, [b_win, b_h[k]], [bq])
                for i2 in range(4):
                    self.act(sq16[i2][0][:], ps[1 + i2][:], AF.Square, [b_ps[1 + i2]], [sq16[i2][1]])
                for i2 in range(4):
                    pss, bss = ps[5 + i2 % 2], b_ps[5 + i2 % 2]
                    self.mm(pss[:], self.bd16[:], sq16[i2][0][:], True, True, [sq16[i2][1], self.b_c], [bss])
                    self.act(rsq[i2][0][:], pss[:], AF.Ln, [bss], [rsq[i2][1]], scale=1.0 / 64, bias=EPS)
                for i2 in range(4):
                    self.act(rsq[i2][0][:], rsq[i2][0][:], AF.Exp, [rsq[i2][1]], [rsq[i2][1]], scale=-0.5)
                for i2, (qk, c) in enumerate(combos):
                    pq, bq = ps[1 + i2], b_ps[1 + i2]
                    rq_, brq = rsq[i2]
                    if qk == 1:
                        self.stt(kT[:, c, ti * NT:(ti + 1) * NT], pq[:], self.gq8[:, l, 1:2], rq_[:], ALU.mult, ALU.mult,
                                 [bq, brq, self.b_mod], [b_k[c][ti]])
                    else:
                        for hf in range(2):
                            pp_ = slice(hf * 64, hf * 64 + 64)
                            self.stt(qh[pp_, 2 * c + hf, ti * NT:(ti + 1) * NT], pq[pp_, :], self.gq8[pp_, l, 0:1], rq_[pp_, :],
                                     ALU.mult, ALU.mult, [bq, brq, self.b_mod], [b_q[c][ti]])
                for g in range(4):
                    tg = ti * 4 + g
                    tf, b_tf = tfs[g]
                    pv, bv = ps[5 + g % 2], b_ps[5 + g % 2]
                    for k in range(NCH):
                        self.mm(pv[:, 0:260], h[:, k, g * 128:(g + 1) * 128], win[:, k, 512:772], k == 0, k == NCH - 1,
                                [b_win, b_h[k]], [bv])
                    pv4 = pv[:, 0:256].rearrange("p (h d) -> p h d", h=4)
                    self.cp("act", va[:, tg, 0:4:2, 0:64], pv4[:, 0:4:2, :], [bv], [b_v[tg]])
                    self.cp("act", va[:, tg, 1:4:2, 64:128], pv4[:, 1:4:2, :], [bv], [b_v[tg]])
                    if CUT < 2:
                        continue
                    self.cp("act", tf[:], pv[:, 256:260], [bv], [b_tf])
                    self.tt("dve", tf[:], tf[:], self.ppc(l, "bf", 0, 4), ALU.add, [b_tf, self.b_pp], [b_tf])
                    self.act(tf[:], tf[:], AF.Exp, [b_tf], [b_tf], scale=-1.0)
                    self.act(tf[:], tf[:], AF.Ln, [b_tf], [b_tf], bias=1.0)
                    pF, bF_ = ps[7], b_ps[7]
                    self.mm(pF[:, 8 * g:8 * g + 4], self.Uf[:], tf[:], True, True, [b_tf, self.b_c], [bF_])
                    self.cp("act", Fn[:, tg, :], pF[:, 8 * g:8 * g + 4], [bF_], [b_F])
            pT, bT_ = ps[7], b_ps[7]
            self.mm(pT[:, 0:NKT * 4], self.selL[:], Fn[:].rearrange("p g h -> p (g h)"), True, True, [b_F, self.b_c], [bT_])
            Ft = sb("Ft", [128, NKT, 4], F32)
            b_Ft = Buf("Ft")
            self.cp("act", Ft[:].rearrange("p g h -> p (g h)"), pT[:, 0:NKT * 4], [bT_], [b_Ft])
            for hd in range(4):
                self.P.op("dve", lambda e, hd=hd: e.tensor_tensor_scan(out=Fc[:, :, hd], data0=self.onesf[:, 0:NKT],
                                                                      data1=Ft[:, :, hd], initial=0.0, op0=ALU.mult, op1=ALU.add),
                          [b_Ft, self.b_c], [b_F])
            if NKT > 1:
                self.tt("dve", Fn[:, 1:NKT, :], Fn[:, 1:NKT, :], Fc[:, 0:NKT - 1, :], ALU.add, [b_F], [b_F])
            Pw = [(sb(f"p{i}", [128, 512], BF16), Buf(f"p{i}")) for i in range(4)]
            bias_t = [(sb(f"bias{i}", [128, NKT], F32), Buf(f"bias{i}")) for i in range(2)]
            rd, b_rd = rsq[0]
            bsb, b_bsb = rsq[1]
            yfx = sb("yfx", [128, 2, 512], F32)
            b_yfx = [Buf("yfx0"), Buf("yfx1")]
            sqy = sb("sqy", [128, 2, 512], BF16)
            b_sqy = Buf("sqy")
            rs2, b_rs2 = f32t("rs2")
            yn = sb("yn", [128, 2, 512], BF16)
            b_yn = Buf("yn")
            pairs = []
            grp = 0
            for qs in range(nsp):
                njt = 4 * qs + 4
                for hd in range(4):
                    for jt in range(njt):
                        pairs.append((qs, hd, jt, njt, grp))
                    grp += 1
            NP = len(pairs)

            def geo(i):
                qs, hd, jt, njt, g = pairs[i]
                return qs, hd, jt, njt, g, hd // 2, (hd % 2) * 64, jt >= 4 * qs, jt - 4 * qs

            def f1(i):
                qs, hd, jt, njt, g, c, po, diag, od = geo(i)
                if jt == 0:
                    bt_, bbt = bias_t[g % 2]
                    self.ts("dve", bt_[:, 0:njt], Fn[:, 0:njt, hd], Fc[:, njt - 1, hd:hd + 1], None, ALU.subtract, ALU.bypass,
                            [b_F], [bbt])
                pA, bA = ps[1 + i % 4], b_ps[1 + i % 4]
                self.mm(pA[:], kT[:, c, jt * 128:(jt + 1) * 128], qh[:, hd, qs * 512:(qs + 1) * 512],
                        True, True, [b_q[c][qs], b_k[c][jt // 4]], [bA])

            def f2(i):
                qs, hd, jt, njt, g, c, po, diag, od = geo(i)
                bt_, bbt = bias_t[g % 2]
                pA, bA = ps[1 + i % 4], b_ps[1 + i % 4]
                p_, bp = Pw[i % 4]
                self.act(p_[:], pA[:], AF.Exp, [bA, bbt], [bp], bias=bt_[:, jt:jt + 1])
                if diag:
                    self.tt("dve", p_[:], p_[:], self.mask_le[:, od, :], ALU.mult, [bp, self.b_c], [bp])

            def f3(i):
                qs, hd, jt, njt, g, c, po, diag, od = geo(i)
                p_, bp = Pw[i % 4]
                pO, bO = ps[5 + g % 2], b_ps[5 + g % 2]
                self.mm(pO[:], va[:, jt, hd, :], p_[:], jt == 0, jt == njt - 1, [b_v[jt], bp], [bO])
                if jt == njt - 1:
                    dn = slice(64 - po, 128 - po)
                    self.recip(rd[dn, :], pO[dn, :], [bO], [b_rd])
                    self.cp("dve", bsb[po:po + 64, :], rd[dn, :], [b_rd], [b_bsb])
                    self.tt("dve", yfx[po:po + 64, c, :], pO[po:po + 64, :], bsb[po:po + 64, :], ALU.mult, [bO, b_bsb],
                            [b_yfx[c]])
                    if hd == 3:
                        self.group_rms_store(yfx, b_yfx, 2, 6, l, s * S + qs * 512, 512, sqy, b_sqy, ps[0], b_ps[0],
                                             rs2, b_rs2, yn, b_yn, 6)

            for n in range(NP + 3):
                if n < NP:
                    f1(n)
                if 0 <= n - 1 < NP:
                    f2(n - 1)
                if 0 <= n - 3 < NP:
                    f3(n - 3)
            P.flush()

    def outp_wo(self, es, l):
        wo = es.enter_context(self.nc.sbuf_tensor(self.nm("wo"), [128, NCH, D], BF16))
        b_wo = Buf("wo")
        for k in range(NCH):
            self.P.dma("pool", wo[:, k, :], self.w_out_d[l, k * 128:(k + 1) * 128, :], b_wo, writes=[b_wo], max_dma_last_dim=4096)
        self.wo_t = (wo, b_wo)

    def outp_phase(self, l):
        nc, P = self.nc, self.P
        NT = 256
        with ExitStack() as es:
            sb = lambda n, sh, d: es.enter_context(nc.sbuf_tensor(self.nm(n), sh, d))
            wo, b_wo = self.wo_t
            xt = [sb("xt", [128, NCH, NT], F32) for _ in range(2)]
            b_xt = [[Buf(f"xt{i}_{c}") for c in range(NCH)] for i in range(2)]
            yt = [sb("yt", [128, NCH, NT], BF16) for _ in range(2)]
            b_yt = [Buf("yt0"), Buf("yt1")]
            ps = [es.enter_context(nc.psum_tensor(self.nm("ps"), [128, 512], F32)) for _ in range(4)]
            b_ps = [Buf(f"ps{i}") for i in range(4)]
            for ti in range(self.T // NT):
                tok0 = ti * NT
                s = tok0 // self.S
                x, bx = xt[ti % 2], b_xt[ti % 2]
                y, by = yt[ti % 2], b_yt[ti % 2]
                self.load_x(x, bx, tok0, NT, False)
                P.dma("act", y[:], self.yT_d[:, :, tok0:tok0 + NT], by, writes=[by])
                for c in range(NCH):
                    pd, bd = ps[c % 4], b_ps[c % 4]
                    for k in range(NCH):
                        self.mm(pd[:, 0:NT], wo[:, k, c * 128:(c + 1) * 128], y[:, k, :], k == 0, k == NCH - 1, [b_wo, by], [bd])
                    self.stt(x[:, c, :], pd[:, 0:NT], self.Gtab[:, l, s, 8 + c:9 + c], x[:, c, :], ALU.mult, ALU.add,
                             [bd, bx[c], self.b_mod], [bx[c]])
                self.store_x(x, bx, tok0, NT, False)
            P.flush()


def make_in_maps(nseq_per_core, ncores, x, c, w_ada, b_ada, g_norm, w_ffn_up, w_ffn_down, w_in, b_fgate, conv_w, conv_b,
                 w_rgate, b_rgate, w_igate, b_igate, lru_lambda, g_qk, g_mix_out, w_out):
    f = lambda a: np.ascontiguousarray(np.asarray(a, dtype=np.float32))
    x, c = f(x), f(c)
    S = x.shape[1]
    pp = pack_pp(f(g_norm), f(b_ada), f(conv_w), f(conv_b), f(b_rgate), f(b_igate), f(lru_lambda), f(g_mix_out),
                 f(g_qk), f(b_fgate))
    shared = {"pp": pp, "w_ada": f(w_ada), "w_ffn_up": f(w_ffn_up), "w_ffn_down": f(w_ffn_down), "w_in": f(w_in),
              "w_rgate": f(w_rgate), "w_igate": f(w_igate), "w_out": f(w_out)}
    maps = []
    for i in range(ncores):
        xs = x[i * nseq_per_core:(i + 1) * nseq_per_core].reshape(nseq_per_core * S, D)
        cs = c[i * nseq_per_core:(i + 1) * nseq_per_core]
        cT = np.ascontiguousarray(cs.reshape(nseq_per_core, NCH, 128).transpose(2, 1, 0).reshape(128, NCH * nseq_per_core))
        m = dict(shared)
        m["x"] = np.ascontiguousarray(xs)
        m["cT"] = cT
        maps.append(m)
    return maps


_CACHE = {}


def kernel(**inputs):
    x = inputs["x"]
    B, S, _ = x.shape
    ns = B // NCORES
    key = (ns, S)
    if key not in _CACHE:
        _CACHE[key] = MK(ns, S)
    mk = _CACHE[key]
    maps = make_in_maps(ns, NCORES, **inputs)
    res = run_bass_kernel_spmd(mk.nc, maps, core_ids=list(range(NCORES)))
    out = np.stack([r["out"].reshape(ns, S, D) for r in res.results], axis=0).reshape(B, S, D)
    return out.astype(np.float32)
```
